# Optimizing a Trainium2 kernel written in Bass

```python
import math
import jax, jax.numpy as jnp
from jax import lax
import numpy as np

D_MODEL = 1024
BATCH = 2
SEQ = 16384
DEPTH = 4

N_MIXERS = 3
N_A = (DEPTH + 2) // 3
N_B = (DEPTH + 1) // 3
N_C = DEPTH // 3

DEEPNORM_ALPHA = (2.0 * DEPTH) ** 0.25
DEEPNORM_BETA = (8.0 * DEPTH) ** -0.25
LN_EPS = 1e-5
RMS_EPS = 1e-6

D_FF = 2816

GMLP_CHUNK = 128
GMLP_WIDTH = D_MODEL
GMLP_GROUPS = 8
GMLP_GROUP_DIM = GMLP_WIDTH // GMLP_GROUPS

HGRN_EXPAND = 128
HGRN_HEADS = D_MODEL // HGRN_EXPAND
HGRN_DK = HGRN_EXPAND
HGRN_DV = D_MODEL // HGRN_HEADS
HGRN_WIDTH = HGRN_HEADS * HGRN_DK
HGRN_CHUNK = 64

ATTN_WINDOW = 128
ATTN_BLOCK = 128
ATTN_HEAD_DIM = 64
ATTN_Q_HEADS = D_MODEL // ATTN_HEAD_DIM
ATTN_KV_HEADS = 2
ATTN_GROUP = ATTN_Q_HEADS // ATTN_KV_HEADS
ATTN_QKV_WIDTH = (ATTN_Q_HEADS + 2 * ATTN_KV_HEADS) * ATTN_HEAD_DIM

kernel_name = 'hybrid_gmlp_hgrn2_swa_sink_deepnorm_macaron'


def layer_norm(x, g, b):
    xf = x.astype(jnp.float32)
    mu = jnp.mean(xf, axis=-1, keepdims=True)
    var = jnp.mean(jnp.square(xf - mu), axis=-1, keepdims=True)
    y = (xf - mu) * lax.rsqrt(var + LN_EPS)
    return (y * g.astype(jnp.float32) + b.astype(jnp.float32)).astype(x.dtype)


def rms_norm(x, g):
    xf = x.astype(jnp.float32)
    y = xf * lax.rsqrt(jnp.mean(jnp.square(xf), axis=-1, keepdims=True) + RMS_EPS)
    return y * g.astype(jnp.float32)


def swiglu(x, w_gate_up, w_down):
    gate, up = jnp.split(x @ w_gate_up, 2, axis=-1)
    return (jax.nn.silu(gate) * up) @ w_down


def chunked_gmlp(x, w_in, ln_g, ln_b, w_s, b_s, w_out):
    B, S, _ = x.shape
    nc = S // GMLP_CHUNK
    z = jax.nn.gelu(x @ w_in, approximate=False)
    u, v = jnp.split(z, 2, axis=-1)
    v = layer_norm(v, ln_g, ln_b)
    v = v.reshape(B, nc, GMLP_CHUNK, GMLP_GROUPS, GMLP_GROUP_DIM)
    causal = jnp.tril(jnp.ones((GMLP_CHUNK, GMLP_CHUNK), dtype=bool))
    w = jnp.where(causal[None], w_s, jnp.zeros_like(w_s))
    mixed = jnp.einsum('gts,bcsgd->bctgd', w, v) + jnp.transpose(b_s)[:, :, None]
    return (u * mixed.reshape(B, S, GMLP_WIDTH)) @ w_out


def hgrn2(x, w_in, lower_bound, norm_g, w_out):
    B, S, _ = x.shape
    H, K, V, C = HGRN_HEADS, HGRN_DK, HGRN_DV, HGRN_CHUNK
    nc = S // C
    q, f, i, g = jnp.split(x @ w_in, 4, axis=-1)
    q = jax.nn.silu(q.astype(jnp.float32))
    f = lower_bound + (1.0 - lower_bound) * jax.nn.sigmoid(f.astype(jnp.float32))
    k = 1.0 - f
    log_f = jnp.log(f)

    def to_chunks(t, d):
        return t.reshape(B, nc, C, H, d).transpose(1, 0, 3, 2, 4)

    qc = to_chunks(q, K)
    kc = to_chunks(k, K)
    vc = to_chunks(i.astype(jnp.float32), V)
    gcum = jnp.cumsum(to_chunks(log_f, K), axis=-2)
    mask = jnp.tril(jnp.ones((C, C), dtype=bool))[:, :, None]

    def step(state, inp):
        q_, k_, v_, gc_ = inp
        diff = gc_[..., :, None, :] - gc_[..., None, :, :]
        decay = jnp.exp(jnp.where(mask, diff, -jnp.inf))
        scores = jnp.einsum('bhtk,bhtsk,bhsk->bhts', q_, decay, k_)
        o = jnp.einsum('bhts,bhsv->bhtv', scores, v_) + jnp.einsum('bhtk,bhkv->bhtv', q_ * jnp.exp(gc_), state)
        g_last = gc_[..., -1:, :]
        k_dec = k_ * jnp.exp(g_last - gc_)
        state = jnp.exp(g_last[..., 0, :])[..., None] * state + jnp.einsum('bhsk,bhsv->bhkv', k_dec, v_)
        return state, o

    state0 = jnp.zeros((B, H, K, V), jnp.float32)
    _, o = lax.scan(step, state0, (qc, kc, vc, gcum))
    o = o.transpose(1, 0, 3, 2, 4).reshape(B, S, H, V)
    o = rms_norm(o, norm_g).reshape(B, S, H * V)
    o = (o * jax.nn.silu(g.astype(jnp.float32))).astype(x.dtype)
    return o @ w_out


def sliding_window_attention(x, w_qkv, b_qkv, sinks, w_o, b_o):
    B, S, _ = x.shape
    nb = S // ATTN_BLOCK
    L, HD, HKV, G = ATTN_BLOCK, ATTN_HEAD_DIM, ATTN_KV_HEADS, ATTN_GROUP
    qkv = x @ w_qkv + b_qkv
    q, k, v = jnp.split(qkv, [ATTN_Q_HEADS * HD, ATTN_Q_HEADS * HD + HKV * HD], axis=-1)
    q = q.reshape(B, nb, L, HKV, G, HD) * (HD ** -0.5)
    k = k.reshape(B, nb, L, HKV, HD)
    v = v.reshape(B, nb, L, HKV, HD)

    def with_prev(t):
        prev = jnp.concatenate([jnp.zeros_like(t[:, :1]), t[:, :-1]], axis=1)
        return jnp.concatenate([prev, t], axis=2)

    k2, v2 = with_prev(k), with_prev(v)
    scores = jnp.einsum('bnqhgd,bnkhd->bnhgqk', q, k2).astype(jnp.float32)
    q_pos = jnp.arange(L)[:, None] + L
    k_pos = jnp.arange(2 * L)[None, :]
    rel = q_pos - k_pos
    in_band = (rel >= 0) & (rel < ATTN_WINDOW)
    real_key = (jnp.arange(nb)[:, None, None] > 0) | (k_pos[None] >= L)
    mask = in_band[None] & real_key
    scores = jnp.where(mask[None, :, None, None], scores, -jnp.inf)
    sink = jnp.broadcast_to(sinks.astype(jnp.float32).reshape(1, 1, HKV, G, 1, 1), scores.shape[:-1] + (1,))
    probs = jax.nn.softmax(jnp.concatenate([scores, sink], axis=-1), axis=-1)[..., :-1]
    out = jnp.einsum('bnhgqk,bnkhd->bnqhgd', probs.astype(v2.dtype), v2)
    return out.reshape(B, S, ATTN_Q_HEADS * HD) @ w_o + b_o


def setup_inputs(seed: int = 0) -> dict:
    key = jax.random.key(seed)
    ks = jax.random.split(key, 24)

    def nrm(k, shape, scale):
        return jax.random.normal(k, shape, jnp.float32) * scale

    beta = DEEPNORM_BETA
    return {
        'x': nrm(ks[0], (BATCH, SEQ, D_MODEL), 1.0),
        'ffn_w_gate_up': nrm(ks[1], (DEPTH, 2, D_MODEL, 2 * D_FF), D_MODEL ** -0.5),
        'ffn_w_down': nrm(ks[2], (DEPTH, 2, D_FF, D_MODEL), beta * D_FF ** -0.5),
        'ln_gain': 1.0 + nrm(ks[3], (DEPTH, 3, D_MODEL), 0.01),
        'ln_bias': nrm(ks[4], (DEPTH, 3, D_MODEL), 0.01),
        'gmlp_w_in': nrm(ks[5], (N_A, D_MODEL, 2 * GMLP_WIDTH), D_MODEL ** -0.5),
        'gmlp_ln_gain': 1.0 + nrm(ks[6], (N_A, GMLP_WIDTH), 0.01),
        'gmlp_ln_bias': nrm(ks[7], (N_A, GMLP_WIDTH), 0.01),
        'gmlp_w_spatial': nrm(ks[8], (N_A, GMLP_GROUPS, GMLP_CHUNK, GMLP_CHUNK), GMLP_CHUNK ** -0.5),
        'gmlp_b_spatial': 1.0 + nrm(ks[9], (N_A, GMLP_GROUPS, GMLP_CHUNK), 0.01),
        'gmlp_w_out': nrm(ks[10], (N_A, GMLP_WIDTH, D_MODEL), beta * GMLP_WIDTH ** -0.5),
        'hgrn_w_in': nrm(ks[11], (N_B, D_MODEL, 4 * HGRN_WIDTH), D_MODEL ** -0.5),
        'hgrn_lb_logits': nrm(ks[12], (DEPTH, HGRN_WIDTH), 0.1),
        'hgrn_norm_gain': 1.0 + nrm(ks[13], (N_B, HGRN_DV), 0.01),
        'hgrn_w_out': nrm(ks[14], (N_B, HGRN_HEADS * HGRN_DV, D_MODEL), beta * (HGRN_HEADS * HGRN_DV) ** -0.5),
        'attn_w_qkv': nrm(ks[15], (N_C, D_MODEL, ATTN_QKV_WIDTH), D_MODEL ** -0.5),
        'attn_b_qkv': nrm(ks[16], (N_C, ATTN_QKV_WIDTH), 0.01),
        'attn_sinks': nrm(ks[17], (N_C, ATTN_Q_HEADS), 1.0),
        'attn_w_o': nrm(ks[18], (N_C, ATTN_Q_HEADS * ATTN_HEAD_DIM, D_MODEL), beta * (ATTN_Q_HEADS * ATTN_HEAD_DIM) ** -0.5),
        'attn_b_o': nrm(ks[19], (N_C, D_MODEL), 0.01),
    }


def reference(x, ffn_w_gate_up, ffn_w_down, ln_gain, ln_bias,
              gmlp_w_in, gmlp_ln_gain, gmlp_ln_bias, gmlp_w_spatial, gmlp_b_spatial, gmlp_w_out,
              hgrn_w_in, hgrn_lb_logits, hgrn_norm_gain, hgrn_w_out,
              attn_w_qkv, attn_b_qkv, attn_sinks, attn_w_o, attn_b_o):
    lb_cum = jnp.cumsum(jax.nn.softmax(hgrn_lb_logits.astype(jnp.float32), axis=0), axis=0)
    lower_bounds = lb_cum - lb_cum[:1]
    alpha = DEEPNORM_ALPHA
    h = x
    for layer in range(DEPTH):
        kind = layer % N_MIXERS
        slot = layer // N_MIXERS
        h = layer_norm(alpha * h + 0.5 * swiglu(h, ffn_w_gate_up[layer, 0], ffn_w_down[layer, 0]),
                       ln_gain[layer, 0], ln_bias[layer, 0])
        if kind == 0:
            mix = chunked_gmlp(h, gmlp_w_in[slot], gmlp_ln_gain[slot], gmlp_ln_bias[slot],
                               gmlp_w_spatial[slot], gmlp_b_spatial[slot], gmlp_w_out[slot])
        elif kind == 1:
            mix = hgrn2(h, hgrn_w_in[slot], lower_bounds[layer], hgrn_norm_gain[slot], hgrn_w_out[slot])
        else:
            mix = sliding_window_attention(h, attn_w_qkv[slot], attn_b_qkv[slot], attn_sinks[slot],
                                           attn_w_o[slot], attn_b_o[slot])
        h = layer_norm(alpha * h + mix, ln_gain[layer, 1], ln_bias[layer, 1])
        h = layer_norm(alpha * h + 0.5 * swiglu(h, ffn_w_gate_up[layer, 1], ffn_w_down[layer, 1]),
                       ln_gain[layer, 2], ln_bias[layer, 2])
    return h
```

```python
import contextlib
import numpy as np
import concourse.bass as bass
import concourse.mybir as mybir
from concourse.bass_utils import run_bass_kernel_spmd

F32 = mybir.dt.float32
BF16 = mybir.dt.bfloat16
AF = mybir.ActivationFunctionType
ALU = mybir.AluOpType
AX = mybir.AxisListType

P = 128
D = 1024
KC = 8
DFF = 2816
FC = 22
DEPTH = 4
ALPHA = (2.0 * DEPTH) ** 0.25
LN_EPS = 1e-5
RMS_EPS = 1e-6
SEQ = 16384
BATCH = 2
NCORES = 8
CHUNK = SEQ // 4
HALO = 256
SLABW = 256


class Buf:
    __slots__ = ("name", "w", "r")

    def __init__(self, name):
        self.name = name
        self.w = None
        self.r = []


class Sched:
    ENGS = ("pe", "act", "dve", "pool", "sp")

    def __init__(self, nc, stack):
        self.nc = nc
        self.stack = stack
        self.streams = {e: [] for e in self.ENGS}
        self.esem = {}
        self.ecnt = {}
        self.eepoch = {e: 0 for e in self.ENGS}
        self.seen = {e: {} for e in self.ENGS}
        self.semobj = {}
        self.nsem = 0
        for e in self.ENGS:
            self._new_epoch(e)
        self.dsem = {}
        self.dcnt = {}

    def _mksem(self, name):
        s = self.stack.enter_context(self.nc.semaphore(name))
        self.nsem += 1
        self.semobj[name] = s
        return name

    def _new_epoch(self, e):
        self.eepoch[e] += 1
        self.esem[e] = self._mksem(f"e_{e}_{self.eepoch[e]}")
        self.ecnt[e] = 0

    def _waits(self, eng, deps):
        best = {}
        for d in deps:
            if d is None:
                continue
            s, v = d
            if v > best.get(s, 0):
                best[s] = v
        out = []
        seen = self.seen[eng]
        for s, v in best.items():
            if seen.get(s, 0) >= v:
                continue
            seen[s] = v
            out.append((s, v))
        return out

    def _deps(self, reads, writes):
        deps = []
        for b in reads:
            deps.append(b.w)
        for b in writes:
            deps.append(b.w)
            deps.extend(b.r)
        return deps

    def op(self, eng, fns, reads=(), writes=()):
        if callable(fns):
            fns = [fns]
        if self.ecnt[eng] > 30000:
            self._new_epoch(eng)
        st = self.streams[eng]
        for s, v in self._waits(eng, self._deps(reads, writes)):
            st.append(("w", s, v))
        for f in fns[:-1]:
            st.append(("i", f, None))
        self.ecnt[eng] += 1
        ev = (self.esem[eng], self.ecnt[eng])
        st.append(("i", fns[-1], ev))
        for b in reads:
            b.r.append(ev)
        for b in writes:
            b.w = ev
            b.r = []
        return ev

    def dma(self, eng, pairs, semkey, reads=(), writes=()):
        if semkey not in self.dsem:
            self.dsem[semkey] = self._mksem(f"d_{semkey}")
            self.dcnt[semkey] = 0
        st = self.streams[eng]
        for s, v in self._waits(eng, self._deps(reads, writes)):
            st.append(("w", s, v))
        s = self.dsem[semkey]
        for (o, i) in pairs:
            self.dcnt[semkey] += 16
            st.append(("d", (o, i), s))
        ev = (s, self.dcnt[semkey])
        for b in reads:
            b.r.append(ev)
        for b in writes:
            b.w = ev
            b.r = []
        return ev

    def wait_all(self, eng, evs):
        st = self.streams[eng]
        for s, v in self._waits(eng, evs):
            st.append(("w", s, v))

    def replay(self, eng, h):
        so = self.semobj
        for kind, a, b in self.streams[eng]:
            if kind == "w":
                h.wait_ge(so[a], b)
            elif kind == "i":
                ins = a(h)
                if b is not None:
                    ins.then_inc(so[b[0]], 1)
            else:
                o, i = a
                h.dma_start(out=o, in_=i).then_inc(so[b], 16)


def I_mm(out, lhsT, rhs, start, stop):
    return lambda h: h.matmul(out, lhsT, rhs, start=start, stop=stop)


def I_tr(out, in_, ident):
    return lambda h: h.transpose(out, in_, ident)


def I_act(out, in_, func, **kw):
    return lambda h: h.activation(out=out, in_=in_, func=func, **kw)


def I_acopy(out, in_):
    return lambda h: h.copy(out, in_)


def I_tt(out, in0, in1, op):
    return lambda h: h.tensor_tensor(out=out, in0=in0, in1=in1, op=op)


def I_ts(out, in0, s1, s2, op0, op1=None):
    if op1 is None:
        return lambda h: h.tensor_scalar(out=out, in0=in0, scalar1=s1, scalar2=None, op0=op0)
    return lambda h: h.tensor_scalar(out=out, in0=in0, scalar1=s1, scalar2=s2, op0=op0, op1=op1)


def I_stt(out, in0, scalar, in1, op0, op1):
    return lambda h: h.scalar_tensor_tensor(out=out, in0=in0, scalar=scalar, in1=in1, op0=op0, op1=op1)


def I_cp(out, in_):
    return lambda h: h.tensor_copy(out, in_)


def I_bnstats(out, in_):
    return lambda h: h.bn_stats(out, in_)


def I_bnaggr(out, in_):
    return lambda h: h.bn_aggr(out, in_)


def I_recip(out, in_):
    return lambda h: h.reciprocal(out, in_)


def I_memset(ap, v):
    return lambda h: h.memset(ap, v)

def slab_layout(w):
    k, n = w.shape
    assert k == D and n % SLABW == 0
    return np.ascontiguousarray(
        w.reshape(KC, P, n // SLABW, SLABW).transpose(2, 1, 0, 3).reshape(n // SLABW, P, KC * SLABW))


def gate_up_interleave(w):
    g = w[:, :DFF].reshape(D, FC, P)
    u = w[:, DFF:].reshape(D, FC, P)
    return np.concatenate([g, u], axis=2).reshape(D, 2 * DFF)


class Cfg:
    def __init__(self, pass_blocks, halo_blocks, n_layers=DEPTH, sub_limit=None):
        self.pass_blocks = list(pass_blocks)
        self.halo_blocks = halo_blocks
        self.n_layers = n_layers
        self.sub_limit = sub_limit
        self.nblk = sum(pass_blocks)
        self.ntok = self.nblk * P
        self.nout = (self.nblk - halo_blocks) * P


def token_tiles(nb):
    nt = (nb + 3) // 4
    base, rem = divmod(nb, nt)
    out, s = [], 0
    for i in range(nt):
        n = base + (1 if i < rem else 0)
        out.append((s, n))
        s += n
    return out


def bcast(ap, axis, n):
    dims = [list(d) for d in ap.ap]
    dims.insert(axis, [0, n])
    return bass.AP(ap.tensor, ap.offset, dims)


def n_gmlp(L):
    return (L + 2) // 3


def n_hgrn(L):
    return (L + 1) // 3


def n_attn(L):
    return L // 3


def build_program(cfg):
    nc = bass.Bass("TRN2", target_bir_lowering=False)
    NBM = max(cfg.pass_blocks)
    TM = NBM * P
    NCHM = TM // 64
    L = cfg.n_layers
    NG_, NH_, NA_ = n_gmlp(L), n_hgrn(L), n_attn(L)

    def din(name, shape, dt=F32):
        return nc.dram_tensor(name, list(shape), dt, kind="ExternalInput").ap()

    x_d = din("x", [cfg.ntok, D])
    wgu_d = din("wgu", [L * 2, FC, P, KC * SLABW])
    wd_d = din("wd", [L * 2, DFF, D])
    lng_d = din("lng", [L * 3 + 2, D])
    lnb_d = din("lnb", [L * 3 + 2, D])
    ident_d = din("ident", [P, P])
    gwin_d = din("gwin", [max(NG_, 1), 8, P, KC * SLABW])
    gwout_d = din("gwout", [max(NG_, 1), D, D])
    gwsp_d = din("gwsp", [max(NG_, 1), 8, P, P])
    gbs_d = din("gbs", [max(NG_, 1), 8 * P])
    tril_d = din("tril", [P, P])
    hwin_d = din("hwin", [max(NH_, 1), 16, P, KC * SLABW])
    hwout_d = din("hwout", [max(NH_, 1), D, D])
    hlb_d = din("hlb", [P, DEPTH, 8])
    hng_d = din("hng", [max(NH_, 1), P])
    cm64_d = din("cm64", [64, 64])
    scanm_d = din("scanm", [P, TM])
    hflag_d = din("hflag", [P, 1])
    awqkv_d = din("awqkv", [max(NA_, 1), 6, P, KC * SLABW])
    abq_d = din("abq", [max(NA_, 1), P, 8])
    abk_d = din("abk", [max(NA_, 1), P, 2])
    abv_d = din("abv", [max(NA_, 1), P])
    asink_d = din("asink", [max(NA_, 1), 16])
    awo_d = din("awo", [max(NA_, 1), D, D])
    abo_d = din("abo", [max(NA_, 1), D])
    m2_d = din("m2", [P, 256])
    m2f_d = din("m2f", [P, 256])
    out_d = nc.dram_tensor("out", [cfg.nout, D], F32, kind="ExternalOutput").ap()

    stack = contextlib.ExitStack()
    with stack:
        def sb(name, shape, dt):
            return stack.enter_context(nc.sbuf_tensor(name, list(shape), dt))

        def ps(name, shape, dt):
            return stack.enter_context(nc.psum_tensor(name, list(shape), dt))

        h_tok = sb("h_tok", [P, NBM, D], F32)
        hT = sb("hT", [P, KC, TM], BF16)
        mixT = sb("mixT", [P, KC, TM], BF16)
        act = sb("act", [P, FC, TM], BF16)
        SCR = FC * TM
        wd_sb = sb("wd_sb", [P, FC, D], BF16)
        NSLAB = 5
        slab_sb = [sb(f"slab{i}", [P, KC, SLABW], BF16) for i in range(NSLAB)]
        lng_sb = [sb(f"lng{i}", [P, D], F32) for i in range(2)]
        lnb_sb = [sb(f"lnb{i}", [P, D], F32) for i in range(2)]
        hbf = [sb(f"hbf{i}", [P, D], BF16) for i in range(2)]
        silu_t = [sb(f"silu{i}", [P, 512], F32) for i in range(2)]
        ident_f = sb("ident_f32", [P, P], F32)
        ident = sb("ident_bf", [P, P], BF16)
        stats = [sb(f"stats{i}", [P, 2, 6], F32) for i in range(2)]
        mv = [sb(f"mv{i}", [P, 2], F32) for i in range(2)]
        sd = [sb(f"sd{i}", [P, 1], F32) for i in range(2)]
        rstd = [sb(f"rstd{i}", [P, 1], F32) for i in range(2)]
        epsb = sb("epsb", [P, 2], F32)
        wmT = sb("wmT", [P, max(NG_, 1), 8, P], BF16)
        bs_sb = sb("bs_sb", [P, 8, P], F32)
        S_st = sb("S_st", [P, 8, P], F32)
        S_bf = sb("S_bf", [P, 4, P], BF16)
        scanm = sb("scanm_sb", [P, TM], F32)
        lbt = sb("lbt", [P, 8], F32)
        omlbt = sb("omlbt", [P, 8], F32)
        ngt = sb("ngt", [P, P], F32)
        cm64 = sb("cm64_sb", [64, 64], F32)
        hflag = sb("hflag_sb", [P, 1], F32)
        ss_t = sb("ss_t", [P, 16], F32)
        kT2 = sb("kT2", [P, 2, (NBM + 1) * P], BF16)
        vtk = sb("vtk", [P, NBM + 1, P], BF16)
        m2b = sb("m2b", [P, 256], BF16)
        m2fb = sb("m2fb", [P, 256], BF16)
        bq8 = sb("bq8", [P, 8], F32)
        bk2 = sb("bk2", [P, 2], F32)
        bvt = sb("bvt", [P, P], F32)
        sinkt = sb("sinkt", [P, 16], F32)
        bot = sb("bot", [P, D], F32)
        att_small = [sb(f"atts{i}", [P, 8], F32) for i in range(2)]

        bankA = [ps(f"bankA{i}", [P, 512], F32) for i in range(4)]
        bankD = [ps(f"bankD{i}", [P, 512], F32) for i in range(3)]
        bankT = ps("bankT", [P, KC, P], BF16)

        S = Sched(nc, stack)

        def carve(off, shape, dt):
            flat = act[:].rearrange("p c t -> p (c t)")
            n = 1
            for s_ in shape[1:]:
                n *= s_
            if dt == F32:
                assert off % 4 == 0
                v = flat[:, off // 2: off // 2 + 2 * n].bitcast(F32)
                nbytes = 4 * n
            else:
                v = flat[:, off // 2: off // 2 + n]
                nbytes = 2 * n
            assert off + nbytes <= SCR * 2, (off, nbytes, SCR * 2)
            if len(shape) == 3:
                v = v.rearrange("p (a b) -> p a b", b=shape[2])
            elif len(shape) == 4:
                v = v.rearrange("p (a b c) -> p a b c", b=shape[2], c=shape[3])
            return v

        B_h = [Buf(f"h{b}") for b in range(NBM)]
        B_hT = [Buf(f"hT{b}") for b in range(NBM)]
        B_mixT = [Buf(f"mixT{b}") for b in range(NBM)]
        B_scr = Buf("scratch")
        B_act = {}
        B_wd = Buf("wd")
        B_slab = [Buf(f"slab{i}") for i in range(NSLAB)]
        B_lngb = [Buf(f"lngb{i}") for i in range(2)]
        B_hbf = [Buf(f"hbf{i}") for i in range(2)]
        B_silu = [Buf(f"silu{i}") for i in range(2)]
        B_A = [Buf(f"A{i}") for i in range(4)]
        B_D = [Buf(f"D{i}") for i in range(3)]
        B_T = Buf("T")
        B_ident = Buf("ident")
        B_small = [Buf(f"small{i}") for i in range(2)]
        B_const = Buf("const")
        B_bs = Buf("bs")
        B_S = Buf("S")
        B_Sbf = [Buf(f"Sbf{i}") for i in range(4)]
        B_kv = Buf("kv")
        B_el, B_ig, B_kd, B_at, B_ss = Buf("el"), Buf("ig"), Buf("kd"), Buf("at"), Buf("ss")
        B_e = [Buf("e0"), Buf("e1")]
        B_eT = [Buf("eT0"), Buf("eT1")]

        def bact(j, t):
            k = (j, t)
            if k not in B_act:
                B_act[k] = Buf(f"act{k}")
            return B_act[k]

        def all_act():
            return list(B_act.values())

        state = {"slab_issue": 0, "slab_use": 0, "dctr": 0, "actr": 0, "ln_issue": 0, "ln_use": 0,
                 "dbank": 0, "sctr": 0, "scr_dirty": True}

        S.dma("sp", [(ident_f[:], ident_d[:, :])], "ident", writes=[B_ident])
        S.op("dve", I_cp(ident[:], ident_f[:]), reads=[B_ident], writes=[B_ident])
        S.op("dve", I_memset(epsb[:, 0:1], float(LN_EPS / ALPHA ** 2)), writes=[B_small[0], B_small[1]])
        S.op("dve", I_memset(epsb[:, 1:2], float(LN_EPS)), writes=[B_small[0], B_small[1]])
        cpairs = [(scanm[:], scanm_d[:, :]), (cm64[:], cm64_d[:, :]), (hflag[:], hflag_d[:, :])]
        tmp_f = carve(0, [P, 8, P], F32)
        tmp_f2 = carve(4096, [P, 8, P], F32)
        if NH_ > 0:
            cpairs.append((ngt[:], hng_d[0:1, :].partition_broadcast(P)))
            cpairs.append((tmp_f[:, 0:DEPTH, 0:8], hlb_d[:, :, :]))
        if NA_ > 0:
            cpairs += [(bq8[:], abq_d[0]), (bk2[:], abk_d[0]),
                       (bvt[:], abv_d[0:1, :].partition_broadcast(P)),
                       (sinkt[:], asink_d[0:1, :].partition_broadcast(P)),
                       (bot[:], abo_d[0:1, :].partition_broadcast(P)),
                       (tmp_f2[:, 0, :], m2_d[:, 0:128]), (tmp_f2[:, 1, :], m2_d[:, 128:256]),
                       (tmp_f2[:, 2, :], m2f_d[:, 0:128]), (tmp_f2[:, 3, :], m2f_d[:, 128:256])]
        S.dma("sp", cpairs, "const", writes=[B_const, B_scr])
        if NH_ > 0:
            hl = 1
            e4 = tmp_f[:, 0:DEPTH, 0:8]
            mx = tmp_f[:, 4, 0:8]
            S.op("dve", I_tt(mx, tmp_f[:, 0, 0:8], tmp_f[:, 1, 0:8], ALU.max), reads=[B_const, B_scr],
                 writes=[B_scr])
            for l in range(2, DEPTH):
                S.op("dve", I_tt(mx, mx, tmp_f[:, l, 0:8], ALU.max), reads=[B_scr], writes=[B_scr])
            for l in range(DEPTH):
                S.op("dve", I_tt(tmp_f[:, l, 0:8], tmp_f[:, l, 0:8], mx, ALU.subtract), reads=[B_scr],
                     writes=[B_scr])
            for l in range(DEPTH):
                S.op("act", I_act(tmp_f[:, l, 0:8], tmp_f[:, l, 0:8], AF.Exp), reads=[B_scr], writes=[B_scr])
            den = tmp_f[:, 5, 0:8]
            S.op("dve", I_tt(den, tmp_f[:, 0, 0:8], tmp_f[:, 1, 0:8], ALU.add), reads=[B_scr], writes=[B_scr])
            for l in range(2, DEPTH):
                S.op("dve", I_tt(den, den, tmp_f[:, l, 0:8], ALU.add), reads=[B_scr], writes=[B_scr])
            S.op("dve", I_recip(den, den), reads=[B_scr], writes=[B_scr])
            num = tmp_f[:, 6, 0:8]
            S.op("dve", I_cp(num, tmp_f[:, 1, 0:8]), reads=[B_scr], writes=[B_scr])
            for l in range(2, hl + 1):
                S.op("dve", I_tt(num, num, tmp_f[:, l, 0:8], ALU.add), reads=[B_scr], writes=[B_scr])
            S.op("dve", I_tt(lbt[:], num, den, ALU.mult), reads=[B_scr], writes=[B_const])
            S.op("dve", I_ts(omlbt[:], lbt[:], -1.0, 1.0, ALU.mult, ALU.add), reads=[B_const], writes=[B_const])
            S.op("dve", I_memset(S_st[:], 0.0), writes=[B_S])
        if NA_ > 0:
            S.op("dve", I_cp(m2b[:], tmp_f2[:, 0:2, :].rearrange("p a b -> p (a b)")), reads=[B_const, B_scr],
                 writes=[B_const])
            S.op("dve", I_cp(m2fb[:], tmp_f2[:, 2:4, :].rearrange("p a b -> p (a b)")), reads=[B_const, B_scr],
                 writes=[B_const])
            S.op("dve", I_ts(bq8[:], bq8[:], 0.125, None, ALU.mult), reads=[B_const], writes=[B_const])
            S.op("dve", I_ts(bot[:], bot[:], float(1.0 / ALPHA), None, ALU.mult), reads=[B_const],
                 writes=[B_const])
            S.op("dve", I_memset(kT2[:], 0.0), writes=[B_kv])
            S.op("dve", I_memset(vtk[:], 0.0), writes=[B_kv])
        for gs in range(NG_):
            wsp_f = carve(8192, [P, 8, P], F32)
            wsp_b = carve(8192 + 4096, [P, 8, P], BF16)
            trl = carve(8192 + 4096 + 2048, [P, P], F32)
            S.dma("sp", [(wsp_f, gwsp_d[gs].rearrange("g t s -> t g s")), (trl, tril_d[:, :])], "const",
                  writes=[B_scr])
            S.op("dve", I_tt(wsp_b, wsp_f, bcast(trl, 1, 8), ALU.mult), reads=[B_scr], writes=[B_scr])
            fns = [I_tr(bankT[:, g, :], wsp_b[:, g, :], ident[:]) for g in range(8)]
            S.op("pe", fns, reads=[B_scr, B_ident], writes=[B_T])
            S.op("act", I_acopy(wmT[:, gs], bankT[:]), reads=[B_T], writes=[B_const])

        slab_plan = []
        ln_plan = []

        def issue_slab():
            n = state["slab_issue"]
            if n >= len(slab_plan):
                return
            slot = n % NSLAB
            S.dma("pool", [(slab_sb[slot][:].rearrange("p k c -> p (k c)"), slab_plan[n])], f"slab{slot}",
                  writes=[B_slab[slot]])
            state["slab_issue"] = n + 1

        def next_slab():
            n = state["slab_use"]
            state["slab_use"] = n + 1
            assert n < state["slab_issue"], "slab used before issued"
            return n % NSLAB

        def issue_ln():
            n = state["ln_issue"]
            if n >= len(ln_plan):
                return
            slot = n % 2
            r = ln_plan[n]
            S.dma("sp", [(lng_sb[slot][:], lng_d[r:r + 1, :].partition_broadcast(P)),
                         (lnb_sb[slot][:], lnb_d[r:r + 1, :].partition_broadcast(P))], f"lngb{slot}",
                  writes=[B_lngb[slot]])
            state["ln_issue"] = n + 1

        def next_ln():
            n = state["ln_use"]
            state["ln_use"] = n + 1
            return n % 2

        def issue_wd(src, nch):
            v = src.rearrange("(j p) n -> p j n", p=P)
            if nch > 8:
                hh = nch // 2
                pairs = [(wd_sb[:, 0:hh, :], v[:, 0:hh, :]), (wd_sb[:, hh:nch, :], v[:, hh:nch, :])]
            else:
                pairs = [(wd_sb[:, 0:nch, :], v)]
            S.dma("pool", pairs, "wd", writes=[B_wd])

        def emit_transpose_block(b, src_slot, dstT, dstB):
            fns = [I_tr(bankT[:, c, :], hbf[src_slot][:, c * P:(c + 1) * P], ident[:]) for c in range(KC)]
            S.op("pe", fns, reads=[B_hbf[src_slot], B_ident], writes=[B_T])
            S.op("act", I_acopy(dstT[:, :, b * P:(b + 1) * P], bankT[:]), reads=[B_T], writes=[dstB[b]])

        def ln_core(src, srcB, eps_col, ln_slot, out_f, out_fB, out_bf, out_bfB):
            k = state["sctr"] % 2
            state["sctr"] += 1
            for hf in range(2):
                S.op("dve", I_bnstats(stats[k][:, hf, :], src[:, hf * 512:(hf + 1) * 512]),
                     reads=[srcB], writes=[B_small[k]])
            S.op("dve", I_bnaggr(mv[k][:], stats[k][:].rearrange("p a s -> p (a s)")),
                 reads=[B_small[k]], writes=[B_small[k]])
            S.op("act", I_act(sd[k][:], mv[k][:, 1:2], AF.Sqrt, bias=epsb[:, eps_col:eps_col + 1], scale=1.0),
                 reads=[B_small[k]], writes=[B_small[k]])
            S.op("dve", I_recip(rstd[k][:], sd[k][:]), reads=[B_small[k]], writes=[B_small[k]])
            S.op("dve", I_ts(out_f, src, mv[k][:, 0:1], rstd[k][:], ALU.subtract, ALU.mult),
                 reads=[B_small[k], srcB], writes=[out_fB])
            S.op("dve", I_tt(out_f, out_f, lng_sb[ln_slot][:], ALU.mult), reads=[B_lngb[ln_slot], out_fB],
                 writes=[out_fB])
            S.op("dve", I_tt(out_f, out_f, lnb_sb[ln_slot][:], ALU.add), reads=[B_lngb[ln_slot], out_fB],
                 writes=[out_fB])
            S.op("act", I_acopy(out_bf, out_f), reads=[out_fB], writes=[out_bfB])

        def emit_ln_part1(b, banks, bbufs, coef, bias_tile=None):
            for hf in range(2):
                hs = h_tok[:, b, hf * 512:(hf + 1) * 512]
                S.op("dve", I_stt(hs, banks[hf][:], float(coef), hs, ALU.mult, ALU.add),
                     reads=[bbufs[hf]], writes=[B_h[b]])
            if bias_tile is not None:
                hb = h_tok[:, b, :]
                S.op("dve", I_tt(hb, hb, bias_tile, ALU.add), reads=[B_h[b], B_const], writes=[B_h[b]])

        def emit_ln_part2(b, ln_slot):
            k = state["dctr"] % 2
            state["dctr"] += 1
            hb = h_tok[:, b, :]
            ln_core(hb, B_h[b], 0, ln_slot, hb, B_h[b], hbf[k][:], B_hbf[k])
            emit_transpose_block(b, k, hT, B_hT)

        def emit_outproj(nb, nch, srcT, srcB, coef, ln_slot, bias_tile=None):
            pending = None
            for b in range(nb):
                banks, bbufs = [], []
                for hf in range(2):
                    d = state["dbank"] % 3
                    state["dbank"] += 1
                    fns = [I_mm(bankD[d][:], srcT[:, j, b * P:(b + 1) * P],
                                wd_sb[:, j, hf * 512:(hf + 1) * 512], j == 0, j == nch - 1)
                           for j in range(nch)]
                    S.op("pe", fns, reads=[B_wd] + srcB(b), writes=[B_D[d]])
                    banks.append(bankD[d])
                    bbufs.append(B_D[d])
                emit_ln_part1(b, banks, bbufs, coef, bias_tile)
                if pending is not None:
                    emit_ln_part2(pending, ln_slot)
                pending = b
            emit_ln_part2(pending, ln_slot)

        def emit_ffn(nb):
            tiles = token_tiles(nb)
            ln_slot = next_ln()
            for j in range(FC):
                slot = next_slab()
                for ti, (b0, nbt) in enumerate(tiles):
                    n = nbt * P
                    t0 = b0 * P
                    pa = state["actr"] % 2
                    state["actr"] += 1
                    bg, bu = bankA[2 * pa], bankA[2 * pa + 1]
                    fns = []
                    for kc in range(KC):
                        fns.append(I_mm(bg[:, :n], slab_sb[slot][:, kc, 0:P], hT[:, kc, t0:t0 + n],
                                        kc == 0, kc == KC - 1))
                    for kc in range(KC):
                        fns.append(I_mm(bu[:, :n], slab_sb[slot][:, kc, P:2 * P], hT[:, kc, t0:t0 + n],
                                        kc == 0, kc == KC - 1))
                    S.op("pe", fns, reads=[B_slab[slot]] + [B_hT[b] for b in range(b0, b0 + nbt)],
                         writes=[B_A[2 * pa], B_A[2 * pa + 1]])
                    S.op("act", I_act(silu_t[pa][:, :n], bg[:, :n], AF.Silu),
                         reads=[B_A[2 * pa]], writes=[B_silu[pa]])
                    wr = [bact(j, ti), B_A[2 * pa]]
                    if state["scr_dirty"]:
                        wr = wr + [B_scr] + all_act()
                        state["scr_dirty"] = False
                    S.op("dve", I_tt(act[:, j, t0:t0 + n], silu_t[pa][:, :n], bu[:, :n], ALU.mult),
                         reads=[B_silu[pa], B_A[2 * pa + 1]], writes=wr)
                issue_slab()

            def srcB(b):
                ti = [i for i, (b0, nbt) in enumerate(tiles) if b0 <= b < b0 + nbt][0]
                return [bact(j, ti) for j in range(FC)]
            emit_outproj(nb, FC, act, srcB, 0.5 / ALPHA, ln_slot)
            issue_ln()

        def emit_gmlp(gs, nb):
            state["scr_dirty"] = True
            T = nb * P
            tiles = token_tiles(nb)
            uT = carve(0, [P, KC, T], BF16)
            vtok = carve(2 * KC * TM, [P, NBM, D], BF16)
            vblk = [carve(4 * KC * TM + i * 4096, [P, D], F32) for i in range(2)]
            assert 4 * KC * TM + 8192 <= SCR * 2
            scr_deps = [B_scr] + all_act()
            ln_v = next_ln()
            S.dma("sp", [(bs_sb[:].rearrange("p g t -> p (g t)"), gbs_d[gs:gs + 1, :].partition_broadcast(P))],
                  "bs", writes=[B_bs])
            first = True
            for us in range(4):
                slot = next_slab()
                for ti, (b0, nbt) in enumerate(tiles):
                    n = nbt * P
                    t0 = b0 * P
                    for cc in range(2):
                        ch = us * 2 + cc
                        a = state["actr"] % 4
                        state["actr"] += 1
                        fns = [I_mm(bankA[a][:, :n], slab_sb[slot][:, kc, cc * P:(cc + 1) * P],
                                    hT[:, kc, t0:t0 + n], kc == 0, kc == KC - 1) for kc in range(KC)]
                        S.op("pe", fns, reads=[B_slab[slot]] + [B_hT[b] for b in range(b0, b0 + nbt)],
                             writes=[B_A[a]])
                        S.op("act", I_act(uT[:, ch, t0:t0 + n], bankA[a][:, :n], AF.Gelu), reads=[B_A[a]],
                             writes=(scr_deps if first else [B_scr]))
                        first = False
                issue_slab()
            vslots = [next_slab() for _ in range(4)]
            for b in range(nb):
                k = state["dctr"] % 2
                state["dctr"] += 1
                for vs in range(4):
                    a = state["actr"] % 4
                    state["actr"] += 1
                    fns = [I_mm(bankA[a][:, 0:SLABW], hT[:, kc, b * P:(b + 1) * P], slab_sb[vslots[vs]][:, kc, :],
                                kc == 0, kc == KC - 1) for kc in range(KC)]
                    S.op("pe", fns, reads=[B_slab[vslots[vs]], B_hT[b]], writes=[B_A[a]])
                    S.op("act", I_act(vblk[k][:, vs * SLABW:(vs + 1) * SLABW], bankA[a][:, 0:SLABW], AF.Gelu),
                         reads=[B_A[a]], writes=[B_scr])
                ln_core(vblk[k], B_scr, 1, ln_v, vblk[k], B_scr, vtok[:, b, :], B_scr)
            for _ in range(4):
                issue_slab()
            issue_ln()
            for b in range(nb):
                for hg in range(2):
                    a = state["actr"] % 4
                    state["actr"] += 1
                    fns = []
                    for g4 in range(4):
                        g = hg * 4 + g4
                        fns.append(I_mm(bankA[a][:, g4 * P:(g4 + 1) * P], vtok[:, b, g * P:(g + 1) * P],
                                        wmT[:, gs, g, :], True, True))
                    S.op("pe", fns, reads=[B_scr, B_const], writes=[B_A[a]])
                    tmpm = silu_t[a % 2]
                    S.op("dve", I_tt(tmpm[:], bankA[a][:], bs_sb[:, hg * 4:(hg + 1) * 4, :].rearrange("p g t -> p (g t)"),
                                     ALU.add), reads=[B_A[a], B_bs], writes=[B_silu[a % 2]])
                    uv = uT[:, hg * 4:(hg + 1) * 4, b * P:(b + 1) * P]
                    S.op("dve", I_tt(uv, tmpm[:].rearrange("p (g t) -> p g t", t=P), uv, ALU.mult),
                         reads=[B_silu[a % 2], B_scr], writes=[B_scr])
            ln_slot = next_ln()
            emit_outproj(nb, KC, uT, lambda b: [B_scr], 1.0 / ALPHA, ln_slot)
            issue_ln()

        def emit_hgrn(hs, nb, is_first_pass):
            state["scr_dirty"] = True
            T = nb * P
            NCH = T // 64
            tiles = token_tiles(nb)
            FB = 4 * TM
            qs = carve(0 * FB, [P, TM], F32)
            fv = carve(1 * FB, [P, TM], F32)
            lf = carve(2 * FB, [P, TM], F32)
            gc = carve(3 * FB, [P, TM], F32)
            eg = carve(4 * FB, [P, TM], F32)
            o_raw = carve(0, [P, NCHM, P], F32)
            sq_t = carve(4 * FB, [P, NCHM // 2, P], F32)
            o0 = 5 * FB
            qdT = carve(o0, [P, TM], BF16)
            kdT = carve(o0 + 2 * TM, [P, TM], BF16)
            kdecT = carve(o0 + 4 * TM, [P, TM], BF16)
            o1 = o0 + 6 * TM
            CB = NCHM * P * 2
            kdec64 = carve(o1, [P, NCHM, P], BF16)
            on64 = kdec64
            i64 = carve(o1 + CB, [P, NCHM, P], BF16)
            gs64 = carve(o1 + 2 * CB, [P, NCHM, P], F32)
            at_bf = carve(o1 + 4 * CB, [P, NCHM, 64], BF16)
            assert o1 + 4 * CB + NCHM * 64 * 2 <= SCR * 2
            ss = ss_t
            hh2 = NCH // 2
            for hd in range(8):
                slot = next_slab()
                for ti, (b0, nbt) in enumerate(tiles):
                    n = nbt * P
                    t0 = b0 * P
                    pa = state["actr"] % 2
                    state["actr"] += 1
                    bq_, bf_ = bankA[2 * pa], bankA[2 * pa + 1]
                    fns = []
                    for kc in range(KC):
                        fns.append(I_mm(bq_[:, :n], slab_sb[slot][:, kc, 0:P], hT[:, kc, t0:t0 + n],
                                        kc == 0, kc == KC - 1))
                    for kc in range(KC):
                        fns.append(I_mm(bf_[:, :n], slab_sb[slot][:, kc, P:2 * P], hT[:, kc, t0:t0 + n],
                                        kc == 0, kc == KC - 1))
                    S.op("pe", fns, reads=[B_slab[slot]] + [B_hT[b] for b in range(b0, b0 + nbt)],
                         writes=[B_A[2 * pa], B_A[2 * pa + 1]])
                    S.op("act", I_act(qs[:, t0:t0 + n], bq_[:, :n], AF.Silu), reads=[B_A[2 * pa]],
                         writes=[B_el])
                    S.op("act", I_act(fv[:, t0:t0 + n], bf_[:, :n], AF.Sigmoid), reads=[B_A[2 * pa + 1]],
                         writes=[B_el])
                issue_slab()
                slot2 = next_slab()
                groups = list(range(0, NCH, 2))

                def tm_group(c):
                    a = state["actr"] % 4
                    state["actr"] += 1
                    fns = []
                    for cc in range(2):
                        for kc in range(KC):
                            fns.append(I_mm(bankA[a][0:64, cc * SLABW:(cc + 1) * SLABW],
                                            hT[:, kc, (c + cc) * 64:(c + cc + 1) * 64], slab_sb[slot2][:, kc, :],
                                            kc == 0, kc == KC - 1))
                    S.op("pe", fns, reads=[B_slab[slot2], B_hT[c // 2]], writes=[B_A[a]])
                    pv = bankA[a][0:64, :].rearrange("p (c x) -> p c x", x=SLABW)
                    S.op("act", I_acopy(i64[0:64, c:c + 2, :], pv[:, :, 0:P]), reads=[B_A[a]], writes=[B_ig])
                    S.op("act", I_act(gs64[0:64, c:c + 2, :], pv[:, :, P:2 * P], AF.Silu), reads=[B_A[a]],
                         writes=[B_ig])

                gi = iter(groups)

                def tm_some(k_):
                    for _ in range(k_):
                        c = next(gi, None)
                        if c is not None:
                            tm_group(c)

                S.op("dve", I_ts(fv[:, :T], fv[:, :T], omlbt[:, hd:hd + 1], lbt[:, hd:hd + 1], ALU.mult, ALU.add),
                     reads=[B_el, B_const], writes=[B_el])
                tm_some(1)
                S.op("act", I_act(lf[:, :T], fv[:, :T], AF.Ln), reads=[B_el], writes=[B_el])
                S.op("dve", I_ts(fv[:, :T], fv[:, :T], -1.0, 1.0, ALU.mult, ALU.add), reads=[B_el], writes=[B_el])
                S.op("dve", lambda h, o=gc[:, :T], d0=scanm[:, :T], d1=lf[:, :T]: h.tensor_tensor_scan(
                    o, d0, d1, 0.0, ALU.mult, ALU.add), reads=[B_el, B_const], writes=[B_el])
                tm_some(1)
                S.op("act", I_act(eg[:, :T], gc[:, :T], AF.Exp), reads=[B_el], writes=[B_el])
                S.op("act", I_act(lf[:, :T], gc[:, :T], AF.Exp, scale=-1.0), reads=[B_el], writes=[B_el])
                tm_some(1)
                S.op("dve", I_tt(qdT[:, :T], qs[:, :T], eg[:, :T], ALU.mult), reads=[B_el], writes=[B_el])
                S.op("dve", I_tt(fv[:, :T], fv[:, :T], lf[:, :T], ALU.mult), reads=[B_el], writes=[B_el])
                S.op("dve", I_cp(kdT[:, :T], fv[:, :T]), reads=[B_el], writes=[B_el])
                egl = eg[:, :T].rearrange("p (c s) -> p c s", s=64)[:, :, 63:64]
                egl_b = bass.AP(egl.tensor, egl.offset, [list(egl.ap[0]), list(egl.ap[1]), [0, 64]])
                S.op("dve", I_tt(kdecT[:, :T].rearrange("p (c s) -> p c s", s=64),
                                 fv[:, :T].rearrange("p (c s) -> p c s", s=64), egl_b, ALU.mult),
                     reads=[B_el], writes=[B_el])
                S.op("dve", I_cp(ss[:, 0:NCH], egl.rearrange("p c o -> p (c o)")), reads=[B_el], writes=[B_ss])
                tm_some(len(groups))
                issue_slab()
                for half in range(2):
                    c0, c1 = (0, hh2) if half == 0 else (hh2, NCH)
                    fns = [I_tr(bankT[0:64, c - c0, :], kdecT[:, c * 64:(c + 1) * 64], ident[:]) for c in range(c0, c1)]
                    S.op("pe", fns, reads=[B_el, B_ident], writes=[B_T])
                    S.op("act", I_acopy(kdec64[0:64, c0:c1, :], bankT[0:64, 0:c1 - c0, :]), reads=[B_T],
                         writes=[B_kd])
                for half in range(2):
                    c0, c1 = (0, min(8, NCH)) if half == 0 else (8, NCH)
                    if c1 <= c0:
                        continue
                    a = state["actr"] % 4
                    state["actr"] += 1
                    fns = [I_mm(bankA[a][0:64, (c - c0) * 64:(c - c0 + 1) * 64], kdT[:, c * 64:(c + 1) * 64],
                                qdT[:, c * 64:(c + 1) * 64], True, True) for c in range(c0, c1)]
                    S.op("pe", fns, reads=[B_el], writes=[B_A[a]])
                    S.op("dve", I_tt(at_bf[0:64, c0:c1, :],
                                     bankA[a][0:64, 0:(c1 - c0) * 64].rearrange("p (c t) -> p c t", t=64),
                                     bcast(cm64[:], 1, c1 - c0), ALU.mult),
                         reads=[B_A[a], B_const], writes=[B_at])
                dS = {}
                for c in range(NCH):
                    d = c // 4
                    dS[c] = bankD[d][:, (c % 4) * P:(c % 4 + 1) * P]
                for d in range((NCH + 3) // 4):
                    cs = [c for c in range(NCH) if c // 4 == d]
                    fns = [I_mm(dS[c], kdec64[0:64, c, :], i64[0:64, c, :], True, True) for c in cs]
                    S.op("pe", fns, reads=[B_kd, B_ig], writes=[B_D[d]])
                cur_a = None
                for c in range(NCH):
                    sl = state["sctr"] % 4
                    state["sctr"] += 1
                    if is_first_pass and c == 2 * cfg.halo_blocks and cfg.halo_blocks > 0:
                        S.op("dve", I_ts(S_st[:, hd, :], S_st[:, hd, :], hflag[:, 0:1], None, ALU.mult),
                             reads=[B_S, B_const], writes=[B_S])
                    S.op("dve", I_cp(S_bf[:, sl, :], S_st[:, hd, :]), reads=[B_S], writes=[B_Sbf[sl]])
                    if c % 4 == 0:
                        cur_a = state["actr"] % 4
                        state["actr"] += 1
                    oc = bankA[cur_a][0:64, (c % 4) * P:(c % 4 + 1) * P]
                    fns = [I_mm(oc, at_bf[0:64, c, :], i64[0:64, c, :], True, False),
                           I_mm(oc, qdT[:, c * 64:(c + 1) * 64], S_bf[:, sl, :], False, True)]
                    S.op("pe", fns, reads=[B_at, B_ig, B_el, B_Sbf[sl]], writes=[B_A[cur_a]])
                    S.op("dve", I_stt(S_st[:, hd, :], S_st[:, hd, :], ss[:, c:c + 1], dS[c], ALU.mult, ALU.add),
                         reads=[B_S, B_ss, B_D[c // 4]], writes=[B_S])
                    if c % 4 == 3 or c == NCH - 1:
                        cb = c - (c % 4)
                        S.op("act", I_acopy(o_raw[0:64, cb:c + 1, :],
                                            bankA[cur_a][0:64, 0:(c - cb + 1) * P].rearrange("p (c v) -> p c v", v=P)),
                             reads=[B_A[cur_a]], writes=[B_el])
                for half in range(2):
                    c0, c1 = (0, hh2) if half == 0 else (hh2, NCH)
                    S.op("dve", I_tt(sq_t[0:64, 0:c1 - c0, :], o_raw[0:64, c0:c1, :], o_raw[0:64, c0:c1, :], ALU.mult),
                         reads=[B_el], writes=[B_el])
                    S.op("dve", lambda h, o=ss[0:64, c0:c1], i=sq_t[0:64, 0:c1 - c0, :]: h.tensor_reduce(
                        out=o, in_=i, axis=AX.X, op=ALU.add), reads=[B_el], writes=[B_ss])
                S.op("dve", I_ts(ss[0:64, 0:NCH], ss[0:64, 0:NCH], 1.0 / P, float(RMS_EPS), ALU.mult, ALU.add),
                     reads=[B_ss], writes=[B_ss])
                S.op("act", I_act(ss[0:64, 0:NCH], ss[0:64, 0:NCH], AF.Sqrt), reads=[B_ss], writes=[B_ss])
                S.op("dve", I_recip(ss[0:64, 0:NCH], ss[0:64, 0:NCH]), reads=[B_ss], writes=[B_ss])
                ssb = ss[0:64, 0:NCH]
                ss_b = bass.AP(ssb.tensor, ssb.offset, [list(ssb.ap[0]), list(ssb.ap[1]), [0, P]])
                orw = o_raw[0:64, 0:NCH, :]
                S.op("dve", I_tt(orw, orw, ss_b, ALU.mult), reads=[B_el, B_ss], writes=[B_el])
                S.op("dve", I_tt(orw, orw, bcast(ngt[0:64, :], 1, NCH), ALU.mult), reads=[B_el, B_const],
                     writes=[B_el])
                S.op("dve", I_tt(on64[0:64, 0:NCH, :], orw, gs64[0:64, 0:NCH, :], ALU.mult), reads=[B_el, B_ig],
                     writes=[B_kd])
                fns = [I_tr(bankT[:, c // 2, (c % 2) * 64:(c % 2 + 1) * 64], on64[0:64, c, :], ident[0:64, 0:64])
                       for c in range(NCH)]
                S.op("pe", fns, reads=[B_kd, B_ident], writes=[B_T])
                S.op("act", I_acopy(mixT[:, hd, 0:T], bankT[:, 0:nb, :].rearrange("p b t -> p (b t)")),
                     reads=[B_T], writes=B_mixT[:nb])
            ln_slot = next_ln()
            emit_outproj(nb, KC, mixT, lambda b: [B_mixT[b]], 1.0 / ALPHA, ln_slot)
            issue_ln()

        def emit_attn(as_, nb, is_first_pass):
            state["scr_dirty"] = True
            T = nb * P
            tiles = token_tiles(nb)
            qT = carve(0, [P, KC, TM], BF16)
            o0 = 2 * KC * TM
            e_bf = [carve(o0 + i * 512, [P, 256], BF16) for i in range(2)]
            eT = [carve(o0 + 1024 + i * 512, [P, 2, P], BF16) for i in range(2)]
            scr_deps = [B_scr] + all_act()
            first = True
            for qs_ in range(4):
                slot = next_slab()
                for ti, (b0, nbt) in enumerate(tiles):
                    n = nbt * P
                    t0 = b0 * P
                    for cc in range(2):
                        ch = qs_ * 2 + cc
                        a = state["actr"] % 4
                        state["actr"] += 1
                        fns = [I_mm(bankA[a][:, :n], slab_sb[slot][:, kc, cc * P:(cc + 1) * P],
                                    hT[:, kc, t0:t0 + n], kc == 0, kc == KC - 1) for kc in range(KC)]
                        S.op("pe", fns, reads=[B_slab[slot]] + [B_hT[b] for b in range(b0, b0 + nbt)],
                             writes=[B_A[a]])
                        S.op("act", I_act(qT[:, ch, t0:t0 + n], bankA[a][:, :n], AF.Identity,
                                          bias=bq8[:, ch:ch + 1], scale=0.125), reads=[B_A[a], B_const],
                             writes=(scr_deps if first else [B_scr]))
                        first = False
                issue_slab()
            slot = next_slab()
            for ti, (b0, nbt) in enumerate(tiles):
                n = nbt * P
                t0 = b0 * P
                for kvh in range(2):
                    a = state["actr"] % 4
                    state["actr"] += 1
                    fns = [I_mm(bankA[a][:, :n], slab_sb[slot][:, kc, kvh * P:(kvh + 1) * P],
                                hT[:, kc, t0:t0 + n], kc == 0, kc == KC - 1) for kc in range(KC)]
                    S.op("pe", fns, reads=[B_slab[slot]] + [B_hT[b] for b in range(b0, b0 + nbt)],
                         writes=[B_A[a]])
                    S.op("act", I_act(kT2[:, kvh, P + t0:P + t0 + n], bankA[a][:, :n], AF.Identity,
                                      bias=bk2[:, kvh:kvh + 1], scale=1.0), reads=[B_A[a], B_const],
                         writes=[B_kv])
            issue_slab()
            slot = next_slab()
            for b in range(nb):
                a = state["actr"] % 4
                state["actr"] += 1
                fns = [I_mm(bankA[a][:, 0:P], hT[:, kc, b * P:(b + 1) * P], slab_sb[slot][:, kc, 0:P],
                            kc == 0, kc == KC - 1) for kc in range(KC)]
                S.op("pe", fns, reads=[B_slab[slot], B_hT[b]], writes=[B_A[a]])
                S.op("dve", I_tt(vtk[:, b + 1, :], bankA[a][:, 0:P], bvt[:], ALU.add), reads=[B_A[a], B_const],
                     writes=[B_kv])
            issue_slab()
            def stage1(b, hq, mask):
                kvh = hq // 8
                pb = (hq % 2) * 64
                a = state["actr"] % 4
                state["actr"] += 1
                sm = att_small[hq % 2]
                fns = [I_mm(bankA[a][:, 0:256], qT[pb:pb + 64, hq // 2, b * P:(b + 1) * P],
                            kT2[pb:pb + 64, kvh, b * P:(b + 2) * P], True, False),
                       I_mm(bankA[a][:, 0:256], ident[:], mask[:], False, True)]
                S.op("pe", fns, reads=[B_scr, B_kv, B_const, B_ident], writes=[B_A[a]])
                bsm = B_small[hq % 2]
                S.op("dve", lambda h, o=sm[:, 0:1], i=bankA[a][:, 0:256]: h.tensor_reduce(
                    out=o, in_=i, axis=AX.X, op=ALU.max), reads=[B_A[a]], writes=[bsm])
                S.op("dve", I_tt(sm[:, 0:1], sm[:, 0:1], sinkt[:, hq:hq + 1], ALU.max), reads=[bsm, B_const],
                     writes=[bsm])
                S.op("dve", I_ts(sm[:, 1:2], sm[:, 0:1], -1.0, None, ALU.mult), reads=[bsm], writes=[bsm])
                ei = hq % 2
                S.op("act", I_act(e_bf[ei][:], bankA[a][:, 0:256], AF.Exp, bias=sm[:, 1:2], scale=1.0,
                                  accum_out=sm[:, 2:3]), reads=[B_A[a], bsm], writes=[B_e[ei], bsm, B_A[a]])
                S.op("act", I_act(sm[:, 3:4], sinkt[:, hq:hq + 1], AF.Exp, bias=sm[:, 1:2], scale=1.0),
                     reads=[bsm, B_const], writes=[bsm])
                S.op("dve", I_tt(sm[:, 4:5], sm[:, 2:3], sm[:, 3:4], ALU.add), reads=[bsm], writes=[bsm])
                S.op("dve", I_recip(sm[:, 5:6], sm[:, 4:5]), reads=[bsm], writes=[bsm])

            def stage2(b, hq, k, ob):
                kvh = hq // 8
                ei = hq % 2
                sm = att_small[hq % 2]
                bsm = B_small[hq % 2]
                fns = [I_tr(bankT[:, j, :], e_bf[ei][:, j * P:(j + 1) * P], ident[:]) for j in range(2)]
                S.op("pe", fns, reads=[B_e[ei], B_ident], writes=[B_T])
                S.op("act", I_acopy(eT[ei][:], bankT[:, 0:2, :]), reads=[B_T], writes=[B_eT[ei]])
                if hq % 8 == 0:
                    d = state["dbank"] % 3
                    state["dbank"] += 1
                    ob[hq // 8] = d
                d = ob[hq // 8]
                oc = bankD[d][:, (hq % 8) * 64:(hq % 8 + 1) * 64]
                fns = [I_mm(oc, eT[ei][:, 0, :], vtk[:, b, kvh * 64:(kvh + 1) * 64], True, False),
                       I_mm(oc, eT[ei][:, 1, :], vtk[:, b + 1, kvh * 64:(kvh + 1) * 64], False, True)]
                S.op("pe", fns, reads=[B_eT[ei], B_kv], writes=[B_D[d]])
                S.op("dve", I_ts(hbf[k][:, hq * 64:(hq + 1) * 64], oc, sm[:, 5:6], None, ALU.mult),
                     reads=[B_D[d], bsm], writes=[B_hbf[k], B_D[d]])

            for b in range(nb):
                k = state["dctr"] % 2
                state["dctr"] += 1
                use_first = is_first_pass and b == cfg.halo_blocks
                mask = m2fb if use_first else m2b
                ob = [None, None]
                stage1(b, 0, mask)
                for hq in range(16):
                    if hq + 1 < 16:
                        stage1(b, hq + 1, mask)
                    stage2(b, hq, k, ob)
                emit_transpose_block(b, k, mixT, B_mixT)
            S.op("dve", I_cp(kT2[:, :, 0:P], kT2[:, :, nb * P:(nb + 1) * P]), reads=[B_kv], writes=[B_kv])
            S.op("dve", I_cp(vtk[:, 0, :], vtk[:, nb, :]), reads=[B_kv], writes=[B_kv])
            ln_slot = next_ln()
            emit_outproj(nb, KC, mixT, lambda b: [B_mixT[b]], 1.0 / ALPHA, ln_slot, bias_tile=bot[:])
            issue_ln()

        subs = []
        for li in range(L):
            subs.append(("ffn", li, 0))
            subs.append(("mix", li, li % 3))
            subs.append(("ffn", li, 1))
        if cfg.sub_limit is not None:
            subs = subs[:cfg.sub_limit]
        npass = len(cfg.pass_blocks)
        wd_seq = []
        for _ in range(npass):
            for (kind, li, x_) in subs:
                if kind == "ffn":
                    for j in range(FC):
                        slab_plan.append(wgu_d[li * 2 + x_, j])
                    ln_plan.append(li * 3 + (0 if x_ == 0 else 2))
                    wd_seq.append((wd_d[li * 2 + x_], FC))
                else:
                    slot_ = li // 3
                    if x_ == 0:
                        for s_ in range(8):
                            slab_plan.append(gwin_d[slot_, s_])
                        ln_plan.append(L * 3 + slot_)
                        wd_seq.append((gwout_d[slot_], KC))
                    elif x_ == 1:
                        for s_ in range(16):
                            slab_plan.append(hwin_d[slot_, s_])
                        wd_seq.append((hwout_d[slot_], KC))
                    else:
                        for s_ in range(6):
                            slab_plan.append(awqkv_d[slot_, s_])
                        wd_seq.append((awo_d[slot_], KC))
                    ln_plan.append(li * 3 + 1)
        wd_ctr = [0]

        def issue_next_wd():
            if wd_ctr[0] < len(wd_seq):
                issue_wd(*wd_seq[wd_ctr[0]])
                wd_ctr[0] += 1

        for _ in range(NSLAB):
            issue_slab()
        issue_ln()
        issue_ln()
        issue_next_wd()

        out_evs = []
        blk0 = 0
        for pi, nb in enumerate(cfg.pass_blocks):
            src = x_d[blk0 * P:(blk0 + nb) * P, :].rearrange("(b p) d -> p b d", p=P)
            S.dma("sp", [(h_tok[:, 0:nb, :], src)], "xload", writes=B_h[:nb])
            for b in range(nb):
                k = state["dctr"] % 2
                state["dctr"] += 1
                S.op("act", I_acopy(hbf[k][:], h_tok[:, b, :]), reads=[B_h[b]], writes=[B_hbf[k]])
                emit_transpose_block(b, k, hT, B_hT)
            for (kind, li, x_) in subs:
                if kind == "ffn":
                    emit_ffn(nb)
                elif x_ == 0:
                    emit_gmlp(li // 3, nb)
                elif x_ == 1:
                    emit_hgrn(li // 3, nb, pi == 0)
                else:
                    emit_attn(li // 3, nb, pi == 0)
                issue_next_wd()
            pairs = []
            for b in range(nb):
                gb = blk0 + b
                if gb < cfg.halo_blocks:
                    continue
                ob_ = gb - cfg.halo_blocks
                pairs.append((out_d[ob_ * P:(ob_ + 1) * P, :], h_tok[:, b, :]))
            if pairs:
                ev = S.dma("sp", pairs, "store", reads=B_h[:nb])
                out_evs.append(ev)
            blk0 += nb
        S.wait_all("sp", out_evs)

        with nc.Block() as block:
            @block.tensor
            def _(h):
                S.replay("pe", h)

            @block.scalar
            def _(h):
                S.replay("act", h)

            @block.vector
            def _(h):
                S.replay("dve", h)

            @block.gpsimd
            def _(h):
                S.replay("pool", h)

            @block.sync
            def _(h):
                S.replay("sp", h)
    return nc


def make_in_maps(cfg, x_cores, inputs, L, flags=None):
    f32 = np.float32
    ncore = len(x_cores)
    if flags is None:
        flags = [1.0] * ncore
    NBM = max(cfg.pass_blocks)
    TM = NBM * P
    NG_, NH_, NA_ = n_gmlp(L), n_hgrn(L), n_attn(L)
    wgu = inputs["ffn_w_gate_up"]
    wd = inputs["ffn_w_down"]
    common = {
        "wgu": np.stack([slab_layout(gate_up_interleave(wgu[li, fi])) for li in range(L) for fi in range(2)]),
        "wd": np.ascontiguousarray(wd[:L].reshape(L * 2, DFF, D)),
        "lng": np.ascontiguousarray(np.concatenate(
            [inputs["ln_gain"][:L].reshape(L * 3, D), inputs["gmlp_ln_gain"][:2].reshape(-1, D)], 0)[:L * 3 + 2]),
        "lnb": np.ascontiguousarray(np.concatenate(
            [inputs["ln_bias"][:L].reshape(L * 3, D), inputs["gmlp_ln_bias"][:2].reshape(-1, D)], 0)[:L * 3 + 2]),
        "ident": np.eye(P, dtype=f32),
        "tril": np.tril(np.ones((P, P), f32)),
        "cm64": np.triu(np.ones((64, 64), f32)),
        "scanm": np.tile((np.arange(TM) % 64 != 0).astype(f32)[None, :], (P, 1)),
        "hlb": np.ascontiguousarray(inputs["hgrn_lb_logits"].reshape(DEPTH, 8, P).transpose(2, 0, 1)),
    }
    ng = max(NG_, 1)
    common["gwin"] = np.stack([slab_layout(inputs["gmlp_w_in"][s]) for s in range(ng)])
    common["gwout"] = np.ascontiguousarray(inputs["gmlp_w_out"][:ng])
    common["gwsp"] = np.ascontiguousarray(inputs["gmlp_w_spatial"][:ng])
    common["gbs"] = np.ascontiguousarray(inputs["gmlp_b_spatial"][:ng].reshape(ng, 8 * P))
    hw = inputs["hgrn_w_in"][0]
    q_, f_, i_, g_ = [hw[:, j * D:(j + 1) * D].reshape(D, 8, P) for j in range(4)]
    hperm = np.concatenate([np.concatenate([q_[:, h], f_[:, h], i_[:, h], g_[:, h]], axis=1) for h in range(8)], axis=1)
    common["hwin"] = slab_layout(hperm)[None]
    common["hwout"] = np.ascontiguousarray(inputs["hgrn_w_out"][:1])
    common["hng"] = np.ascontiguousarray(inputs["hgrn_norm_gain"][:1])
    aw = inputs["attn_w_qkv"][0]
    ab = inputs["attn_b_qkv"][0]
    k0, k1 = aw[:, 1024:1088], aw[:, 1088:1152]
    awp = np.concatenate([aw[:, :1024], k0, k0, k1, k1, aw[:, 1152:1280], np.zeros((D, 128), f32)], axis=1)
    common["awqkv"] = slab_layout(awp)[None]
    common["abq"] = np.ascontiguousarray(ab[:1024].reshape(8, P).T)[None]
    bk0, bk1 = ab[1024:1088], ab[1088:1152]
    common["abk"] = np.ascontiguousarray(np.stack([np.concatenate([bk0, bk0]), np.concatenate([bk1, bk1])], 1))[None]
    common["abv"] = np.ascontiguousarray(ab[1152:1280])[None]
    common["asink"] = np.ascontiguousarray(inputs["attn_sinks"][:1])
    common["awo"] = np.ascontiguousarray(inputs["attn_w_o"][:1])
    common["abo"] = np.ascontiguousarray(inputs["attn_b_o"][:1])
    NEG = -30000.0
    qi = np.arange(P)[:, None]
    kj = np.arange(P)[None, :]
    m_prev = np.where(kj > qi, 0.0, NEG).astype(f32)
    m_cur = np.where(kj <= qi, 0.0, NEG).astype(f32)
    common["m2"] = np.concatenate([m_prev, m_cur], 1)
    maps = []
    for c in range(ncore):
        m = dict(common)
        m["x"] = np.ascontiguousarray(x_cores[c], dtype=f32)
        m["hflag"] = np.full((P, 1), flags[c], f32)
        if flags[c] > 0:
            m["m2f"] = common["m2"]
        else:
            m["m2f"] = np.concatenate([np.full((P, P), NEG, f32), m_cur], 1)
        maps.append(m)
    return maps


PASS_BLOCKS = [6, 6, 6, 6, 6, 4]


def kernel(**inputs):
    inputs = {k: np.asarray(v) for k, v in inputs.items()}
    x = inputs["x"]
    cfg = Cfg(PASS_BLOCKS, HALO // P)
    x_cores, flags = [], []
    for c in range(NCORES):
        b, p = divmod(c, 4)
        xc = np.zeros((cfg.ntok, D), np.float32)
        s = p * CHUNK - HALO
        if s < 0:
            xc[HALO:] = x[b, 0:CHUNK]
            flags.append(0.0)
        else:
            xc[:] = x[b, s:s + cfg.ntok]
            flags.append(1.0)
        x_cores.append(xc)
    nc = build_program(cfg)
    in_maps = make_in_maps(cfg, x_cores, inputs, DEPTH, flags)
    res = run_bass_kernel_spmd(nc, in_maps, core_ids=list(range(NCORES)))
    out = np.empty((BATCH, SEQ, D), np.float32)
    for c in range(NCORES):
        b, p = divmod(c, 4)
        out[b, p * CHUNK:(p + 1) * CHUNK] = res.results[c]["out"]
    return out
```

```python
import contextlib
import numpy as np
import concourse.bass as bass
import concourse.mybir as mybir
from concourse.bass_utils import run_bass_kernel_spmd

F32 = mybir.dt.float32
BF16 = mybir.dt.bfloat16
AF = mybir.ActivationFunctionType
ALU = mybir.AluOpType
AX = mybir.AxisListType

P = 128
D = 1024
KC = 8
DFF = 2816
FC = 22
DEPTH = 4
ALPHA = (2.0 * DEPTH) ** 0.25
LN_EPS = 1e-5
RMS_EPS = 1e-6
SEQ = 16384
BATCH = 2
NCORES = 8
CHUNK = SEQ // 4
HALO = 256
SLABW = 256
LN_DELAY = 2


class Buf:
    __slots__ = ("name", "w", "r")

    def __init__(self, name):
        self.name = name
        self.w = None
        self.r = []


class Sched:
    ENGS = ("pe", "act", "dve", "pool", "sp")

    def __init__(self, nc, stack):
        self.nc = nc
        self.stack = stack
        self.streams = {e: [] for e in self.ENGS}
        self.esem = {}
        self.ecnt = {}
        self.eepoch = {e: 0 for e in self.ENGS}
        self.seen = {e: {} for e in self.ENGS}
        self.semobj = {}
        self.nsem = 0
        for e in self.ENGS:
            self._new_epoch(e)
        self.dsem = {}
        self.dcnt = {}

    def _mksem(self, name):
        s = self.stack.enter_context(self.nc.semaphore(name))
        self.nsem += 1
        self.semobj[name] = s
        return name

    def _new_epoch(self, e):
        self.eepoch[e] += 1
        self.esem[e] = self._mksem(f"e_{e}_{self.eepoch[e]}")
        self.ecnt[e] = 0

    def _waits(self, eng, deps):
        best = {}
        for d in deps:
            if d is None:
                continue
            s, v = d
            if v > best.get(s, 0):
                best[s] = v
        out = []
        seen = self.seen[eng]
        for s, v in best.items():
            if seen.get(s, 0) >= v:
                continue
            seen[s] = v
            out.append((s, v))
        return out

    def _deps(self, reads, writes):
        deps = []
        for b in reads:
            deps.append(b.w)
        for b in writes:
            deps.append(b.w)
            deps.extend(b.r)
        return deps

    def op(self, eng, fns, reads=(), writes=()):
        if callable(fns):
            fns = [fns]
        if self.ecnt[eng] > 30000:
            self._new_epoch(eng)
        st = self.streams[eng]
        for s, v in self._waits(eng, self._deps(reads, writes)):
            st.append(("w", s, v))
        for f in fns[:-1]:
            st.append(("i", f, None))
        self.ecnt[eng] += 1
        ev = (self.esem[eng], self.ecnt[eng])
        st.append(("i", fns[-1], ev))
        for b in reads:
            b.r.append(ev)
        for b in writes:
            b.w = ev
            b.r = []
        return ev

    def dma(self, eng, pairs, semkey, reads=(), writes=()):
        if semkey not in self.dsem:
            self.dsem[semkey] = self._mksem(f"d_{semkey}")
            self.dcnt[semkey] = 0
        st = self.streams[eng]
        for s, v in self._waits(eng, self._deps(reads, writes)):
            st.append(("w", s, v))
        s = self.dsem[semkey]
        for (o, i) in pairs:
            self.dcnt[semkey] += 16
            st.append(("d", (o, i), s))
        ev = (s, self.dcnt[semkey])
        for b in reads:
            b.r.append(ev)
        for b in writes:
            b.w = ev
            b.r = []
        return ev

    def wait_all(self, eng, evs):
        st = self.streams[eng]
        for s, v in self._waits(eng, evs):
            st.append(("w", s, v))

    def replay(self, eng, h):
        so = self.semobj
        for kind, a, b in self.streams[eng]:
            if kind == "w":
                h.wait_ge(so[a], b)
            elif kind == "i":
                ins = a(h)
                if b is not None:
                    ins.then_inc(so[b[0]], 1)
            else:
                o, i = a
                h.dma_start(out=o, in_=i).then_inc(so[b], 16)


def I_mm(out, lhsT, rhs, start, stop):
    return lambda h: h.matmul(out, lhsT, rhs, start=start, stop=stop)


def I_tr(out, in_, ident):
    return lambda h: h.transpose(out, in_, ident)


def I_act(out, in_, func, **kw):
    return lambda h: h.activation(out=out, in_=in_, func=func, **kw)


def I_acopy(out, in_):
    return lambda h: h.copy(out, in_)


def I_tt(out, in0, in1, op):
    return lambda h: h.tensor_tensor(out=out, in0=in0, in1=in1, op=op)


def I_ts(out, in0, s1, s2, op0, op1=None):
    if op1 is None:
        return lambda h: h.tensor_scalar(out=out, in0=in0, scalar1=s1, scalar2=None, op0=op0)
    return lambda h: h.tensor_scalar(out=out, in0=in0, scalar1=s1, scalar2=s2, op0=op0, op1=op1)


def I_stt(out, in0, scalar, in1, op0, op1):
    return lambda h: h.scalar_tensor_tensor(out=out, in0=in0, scalar=scalar, in1=in1, op0=op0, op1=op1)


def I_cp(out, in_):
    return lambda h: h.tensor_copy(out, in_)


def I_bnstats(out, in_):
    return lambda h: h.bn_stats(out, in_)


def I_bnaggr(out, in_):
    return lambda h: h.bn_aggr(out, in_)


def I_recip(out, in_):
    return lambda h: h.reciprocal(out, in_)


def I_memset(ap, v):
    return lambda h: h.memset(ap, v)

def slab_layout(w):
    k, n = w.shape
    assert k == D and n % SLABW == 0
    return np.ascontiguousarray(
        w.reshape(KC, P, n // SLABW, SLABW).transpose(2, 1, 0, 3).reshape(n // SLABW, P, KC * SLABW))


def gate_up_interleave(w):
    g = w[:, :DFF].reshape(D, FC, P)
    u = w[:, DFF:].reshape(D, FC, P)
    return np.concatenate([g, u], axis=2).reshape(D, 2 * DFF)


class Cfg:
    def __init__(self, pass_blocks, halo_blocks, n_layers=DEPTH, sub_limit=None):
        self.pass_blocks = list(pass_blocks)
        self.halo_blocks = halo_blocks
        self.n_layers = n_layers
        self.sub_limit = sub_limit
        self.nblk = sum(pass_blocks)
        self.ntok = self.nblk * P
        self.nout = (self.nblk - halo_blocks) * P


def token_tiles(nb):
    nt = (nb + 3) // 4
    base, rem = divmod(nb, nt)
    out, s = [], 0
    for i in range(nt):
        n = base + (1 if i < rem else 0)
        out.append((s, n))
        s += n
    return out


def bcast(ap, axis, n):
    dims = [list(d) for d in ap.ap]
    dims.insert(axis, [0, n])
    return bass.AP(ap.tensor, ap.offset, dims)


def n_gmlp(L):
    return (L + 2) // 3


def n_hgrn(L):
    return (L + 1) // 3


def n_attn(L):
    return L // 3


def build_program(cfg):
    nc = bass.Bass("TRN2", target_bir_lowering=False)
    NBM = max(cfg.pass_blocks)
    TM = NBM * P
    NCHM = TM // 64
    L = cfg.n_layers
    NG_, NH_, NA_ = n_gmlp(L), n_hgrn(L), n_attn(L)

    def din(name, shape, dt=F32):
        return nc.dram_tensor(name, list(shape), dt, kind="ExternalInput").ap()

    x_d = din("x", [cfg.ntok, D])
    wgu_d = din("wgu", [L * 2, FC, P, KC * SLABW])
    wd_d = din("wd", [L * 2, DFF, D])
    lng_d = din("lng", [L * 3 + 2, D])
    lnb_d = din("lnb", [L * 3 + 2, D])
    ident_d = din("ident", [P, P])
    gwin_d = din("gwin", [max(NG_, 1), 8, P, KC * SLABW])
    gwout_d = din("gwout", [max(NG_, 1), D, D])
    gwsp_d = din("gwsp", [max(NG_, 1), 8, P, P])
    gbs_d = din("gbs", [max(NG_, 1), 8 * P])
    tril_d = din("tril", [P, P])
    hwin_d = din("hwin", [max(NH_, 1), 16, P, KC * SLABW])
    hwout_d = din("hwout", [max(NH_, 1), D, D])
    hlb_d = din("hlb", [P, DEPTH, 8])
    hng_d = din("hng", [max(NH_, 1), P])
    cm64_d = din("cm64", [64, 64])
    scanm_d = din("scanm", [P, TM])
    hflag_d = din("hflag", [P, 1])
    awqkv_d = din("awqkv", [max(NA_, 1), 6, P, KC * SLABW])
    abq_d = din("abq", [max(NA_, 1), P, 8])
    abk_d = din("abk", [max(NA_, 1), P, 2])
    abv_d = din("abv", [max(NA_, 1), P])
    asink_d = din("asink", [max(NA_, 1), 16])
    awo_d = din("awo", [max(NA_, 1), D, D])
    abo_d = din("abo", [max(NA_, 1), D])
    m2_d = din("m2", [P, 256])
    m2f_d = din("m2f", [P, 256])
    out_d = nc.dram_tensor("out", [cfg.nout, D], F32, kind="ExternalOutput").ap()

    stack = contextlib.ExitStack()
    with stack:
        def sb(name, shape, dt):
            return stack.enter_context(nc.sbuf_tensor(name, list(shape), dt))

        def ps(name, shape, dt):
            return stack.enter_context(nc.psum_tensor(name, list(shape), dt))

        h_tok = sb("h_tok", [P, NBM, D], F32)
        hT = sb("hT", [P, KC, TM], BF16)
        mixT = sb("mixT", [P, KC, TM], BF16)
        act = sb("act", [P, FC, TM], BF16)
        SCR = FC * TM
        wd_sb = sb("wd_sb", [P, FC, D], BF16)
        NSLAB = 5
        slab_sb = [sb(f"slab{i}", [P, KC, SLABW], BF16) for i in range(NSLAB)]
        lng_sb = [sb(f"lng{i}", [P, D], F32) for i in range(2)]
        lnb_sb = [sb(f"lnb{i}", [P, D], F32) for i in range(2)]
        hbf = [sb(f"hbf{i}", [P, D], BF16) for i in range(2)]
        silu_t = [sb(f"silu{i}", [P, 512], F32) for i in range(2)]
        ident_f = sb("ident_f32", [P, P], F32)
        ident = sb("ident_bf", [P, P], BF16)
        stats = [sb(f"stats{i}", [P, 2, 6], F32) for i in range(2)]
        mv = [sb(f"mv{i}", [P, 2], F32) for i in range(2)]
        sd = [sb(f"sd{i}", [P, 1], F32) for i in range(2)]
        rstd = [sb(f"rstd{i}", [P, 1], F32) for i in range(2)]
        epsb = sb("epsb", [P, 2], F32)
        wmT = sb("wmT", [P, max(NG_, 1), 8, P], BF16)
        bs_sb = sb("bs_sb", [P, 8, P], F32)
        S_st = sb("S_st", [P, 8, P], F32)
        S_bf = sb("S_bf", [P, 4, P], BF16)
        scanm = sb("scanm_sb", [P, TM], F32)
        lbt = sb("lbt", [P, 8], F32)
        omlbt = sb("omlbt", [P, 8], F32)
        ngt = sb("ngt", [P, P], F32)
        cm64 = sb("cm64_sb", [64, 64], F32)
        hflag = sb("hflag_sb", [P, 1], F32)
        ss_t = sb("ss_t", [P, 16], F32)
        kT2 = sb("kT2", [P, 2, (NBM + 1) * P], BF16)
        vtk = sb("vtk", [P, NBM + 1, P], BF16)
        m2b = sb("m2b", [P, 256], BF16)
        m2fb = sb("m2fb", [P, 256], BF16)
        bq8 = sb("bq8", [P, 8], F32)
        bk2 = sb("bk2", [P, 2], F32)
        bvt = sb("bvt", [P, P], F32)
        sinkt = sb("sinkt", [P, 16], F32)
        bot = sb("bot", [P, D], F32)
        att_small = [sb(f"atts{i}", [P, 8], F32) for i in range(2)]

        bankA = [ps(f"bankA{i}", [P, 512], F32) for i in range(4)]
        bankD = [ps(f"bankD{i}", [P, 512], F32) for i in range(3)]
        bankT = ps("bankT", [P, KC, P], BF16)

        S = Sched(nc, stack)

        def carve(off, shape, dt):
            flat = act[:].rearrange("p c t -> p (c t)")
            n = 1
            for s_ in shape[1:]:
                n *= s_
            if dt == F32:
                assert off % 4 == 0
                v = flat[:, off // 2: off // 2 + 2 * n].bitcast(F32)
                nbytes = 4 * n
            else:
                v = flat[:, off // 2: off // 2 + n]
                nbytes = 2 * n
            assert off + nbytes <= SCR * 2, (off, nbytes, SCR * 2)
            if len(shape) == 3:
                v = v.rearrange("p (a b) -> p a b", b=shape[2])
            elif len(shape) == 4:
                v = v.rearrange("p (a b c) -> p a b c", b=shape[2], c=shape[3])
            return v

        B_h = [Buf(f"h{b}") for b in range(NBM)]
        B_hT = [Buf(f"hT{b}") for b in range(NBM)]
        B_mixT = [Buf(f"mixT{b}") for b in range(NBM)]
        B_scr = Buf("scratch")
        B_act = {}
        B_wd = Buf("wd")
        B_slab = [Buf(f"slab{i}") for i in range(NSLAB)]
        B_lngb = [Buf(f"lngb{i}") for i in range(2)]
        B_hbf = [Buf(f"hbf{i}") for i in range(2)]
        B_silu = [Buf(f"silu{i}") for i in range(2)]
        B_A = [Buf(f"A{i}") for i in range(4)]
        B_D = [Buf(f"D{i}") for i in range(3)]
        B_T = Buf("T")
        B_ident = Buf("ident")
        B_small = [Buf(f"small{i}") for i in range(2)]
        B_rs = [Buf(f"rs{i}") for i in range(2)]
        B_const = Buf("const")
        B_bs = Buf("bs")
        B_S = Buf("S")
        B_Sbf = [Buf(f"Sbf{i}") for i in range(4)]
        B_kv = Buf("kv")
        B_vb = [Buf("vb0"), Buf("vb1")]
        B_el, B_ig, B_kd, B_at, B_ss = Buf("el"), Buf("ig"), Buf("kd"), Buf("at"), Buf("ss")
        B_e = [Buf("e0"), Buf("e1")]
        B_eT = [Buf("eT0"), Buf("eT1")]

        def bact(j, t):
            k = (j, t)
            if k not in B_act:
                B_act[k] = Buf(f"act{k}")
            return B_act[k]

        def all_act():
            return list(B_act.values())

        state = {"slab_issue": 0, "slab_use": 0, "dctr": 0, "actr": 0, "ln_issue": 0, "ln_use": 0,
                 "dbank": 0, "sctr": 0, "scr_dirty": True}

        S.dma("sp", [(ident_f[:], ident_d[:, :])], "ident", writes=[B_ident])
        S.op("dve", I_cp(ident[:], ident_f[:]), reads=[B_ident], writes=[B_ident])
        S.op("dve", I_memset(epsb[:, 0:1], float(LN_EPS / ALPHA ** 2)), writes=[B_small[0], B_small[1]])
        S.op("dve", I_memset(epsb[:, 1:2], float(LN_EPS)), writes=[B_small[0], B_small[1]])
        cpairs = [(scanm[:], scanm_d[:, :]), (cm64[:], cm64_d[:, :]), (hflag[:], hflag_d[:, :])]
        tmp_f = carve(0, [P, 8, P], F32)
        tmp_f2 = carve(4096, [P, 8, P], F32)
        if NH_ > 0:
            cpairs.append((ngt[:], hng_d[0:1, :].partition_broadcast(P)))
            cpairs.append((tmp_f[:, 0:DEPTH, 0:8], hlb_d[:, :, :]))
        if NA_ > 0:
            cpairs += [(bq8[:], abq_d[0]), (bk2[:], abk_d[0]),
                       (bvt[:], abv_d[0:1, :].partition_broadcast(P)),
                       (sinkt[:], asink_d[0:1, :].partition_broadcast(P)),
                       (bot[:], abo_d[0:1, :].partition_broadcast(P)),
                       (tmp_f2[:, 0, :], m2_d[:, 0:128]), (tmp_f2[:, 1, :], m2_d[:, 128:256]),
                       (tmp_f2[:, 2, :], m2f_d[:, 0:128]), (tmp_f2[:, 3, :], m2f_d[:, 128:256])]
        S.dma("sp", cpairs, "const", writes=[B_const, B_scr])
        if NH_ > 0:
            hl = 1
            e4 = tmp_f[:, 0:DEPTH, 0:8]
            mx = tmp_f[:, 4, 0:8]
            S.op("dve", I_tt(mx, tmp_f[:, 0, 0:8], tmp_f[:, 1, 0:8], ALU.max), reads=[B_const, B_scr],
                 writes=[B_scr])
            for l in range(2, DEPTH):
                S.op("dve", I_tt(mx, mx, tmp_f[:, l, 0:8], ALU.max), reads=[B_scr], writes=[B_scr])
            for l in range(DEPTH):
                S.op("dve", I_tt(tmp_f[:, l, 0:8], tmp_f[:, l, 0:8], mx, ALU.subtract), reads=[B_scr],
                     writes=[B_scr])
            for l in range(DEPTH):
                S.op("act", I_act(tmp_f[:, l, 0:8], tmp_f[:, l, 0:8], AF.Exp), reads=[B_scr], writes=[B_scr])
            den = tmp_f[:, 5, 0:8]
            S.op("dve", I_tt(den, tmp_f[:, 0, 0:8], tmp_f[:, 1, 0:8], ALU.add), reads=[B_scr], writes=[B_scr])
            for l in range(2, DEPTH):
                S.op("dve", I_tt(den, den, tmp_f[:, l, 0:8], ALU.add), reads=[B_scr], writes=[B_scr])
            S.op("dve", I_recip(den, den), reads=[B_scr], writes=[B_scr])
            num = tmp_f[:, 6, 0:8]
            S.op("dve", I_cp(num, tmp_f[:, 1, 0:8]), reads=[B_scr], writes=[B_scr])
            for l in range(2, hl + 1):
                S.op("dve", I_tt(num, num, tmp_f[:, l, 0:8], ALU.add), reads=[B_scr], writes=[B_scr])
            S.op("dve", I_tt(lbt[:], num, den, ALU.mult), reads=[B_scr], writes=[B_const])
            S.op("dve", I_ts(omlbt[:], lbt[:], -1.0, 1.0, ALU.mult, ALU.add), reads=[B_const], writes=[B_const])
            S.op("dve", I_memset(S_st[:], 0.0), writes=[B_S])
        if NA_ > 0:
            S.op("dve", I_cp(m2b[:], tmp_f2[:, 0:2, :].rearrange("p a b -> p (a b)")), reads=[B_const, B_scr],
                 writes=[B_const])
            S.op("dve", I_cp(m2fb[:], tmp_f2[:, 2:4, :].rearrange("p a b -> p (a b)")), reads=[B_const, B_scr],
                 writes=[B_const])
            S.op("dve", I_ts(bq8[:], bq8[:], 0.125, None, ALU.mult), reads=[B_const], writes=[B_const])
            S.op("dve", I_ts(bot[:], bot[:], float(1.0 / ALPHA), None, ALU.mult), reads=[B_const],
                 writes=[B_const])
            S.op("dve", I_memset(kT2[:], 0.0), writes=[B_kv])
            S.op("dve", I_memset(vtk[:], 0.0), writes=[B_kv])
        for gs in range(NG_):
            wsp_f = carve(8192, [P, 8, P], F32)
            wsp_b = carve(8192 + 4096, [P, 8, P], BF16)
            trl = carve(8192 + 4096 + 2048, [P, P], F32)
            S.dma("sp", [(wsp_f, gwsp_d[gs].rearrange("g t s -> t g s")), (trl, tril_d[:, :])], "const",
                  writes=[B_scr])
            S.op("dve", I_tt(wsp_b, wsp_f, bcast(trl, 1, 8), ALU.mult), reads=[B_scr], writes=[B_scr])
            fns = [I_tr(bankT[:, g, :], wsp_b[:, g, :], ident[:]) for g in range(8)]
            S.op("pe", fns, reads=[B_scr, B_ident], writes=[B_T])
            S.op("act", I_acopy(wmT[:, gs], bankT[:]), reads=[B_T], writes=[B_const])

        slab_plan = []
        ln_plan = []

        def issue_slab():
            n = state["slab_issue"]
            if n >= len(slab_plan):
                return
            slot = n % NSLAB
            S.dma("pool", [(slab_sb[slot][:].rearrange("p k c -> p (k c)"), slab_plan[n])], f"slab{slot}",
                  writes=[B_slab[slot]])
            state["slab_issue"] = n + 1

        def next_slab():
            n = state["slab_use"]
            state["slab_use"] = n + 1
            assert n < state["slab_issue"], "slab used before issued"
            return n % NSLAB

        def issue_ln():
            n = state["ln_issue"]
            if n >= len(ln_plan):
                return
            slot = n % 2
            r = ln_plan[n]
            S.dma("sp", [(lng_sb[slot][:], lng_d[r:r + 1, :].partition_broadcast(P)),
                         (lnb_sb[slot][:], lnb_d[r:r + 1, :].partition_broadcast(P))], f"lngb{slot}",
                  writes=[B_lngb[slot]])
            state["ln_issue"] = n + 1

        def next_ln():
            n = state["ln_use"]
            state["ln_use"] = n + 1
            return n % 2

        def issue_wd(src, nch):
            v = src.rearrange("(j p) n -> p j n", p=P)
            if nch > 8:
                hh = nch // 2
                pairs = [(wd_sb[:, 0:hh, :], v[:, 0:hh, :]), (wd_sb[:, hh:nch, :], v[:, hh:nch, :])]
            else:
                pairs = [(wd_sb[:, 0:nch, :], v)]
            S.dma("pool", pairs, "wd", writes=[B_wd])

        def emit_transpose_block(b, src_slot, dstT, dstB):
            fns = [I_tr(bankT[:, c, :], hbf[src_slot][:, c * P:(c + 1) * P], ident[:]) for c in range(KC)]
            S.op("pe", fns, reads=[B_hbf[src_slot], B_ident], writes=[B_T])
            S.op("act", I_acopy(dstT[:, :, b * P:(b + 1) * P], bankT[:]), reads=[B_T], writes=[dstB[b]])

        def ln_core(src, srcB, eps_col, ln_slot, out_f, out_fB, out_bf, out_bfB):
            k = state["sctr"] % 2
            state["sctr"] += 1
            for hf in range(2):
                S.op("dve", I_bnstats(stats[k][:, hf, :], src[:, hf * 512:(hf + 1) * 512]),
                     reads=[srcB], writes=[B_small[k]])
            S.op("dve", I_bnaggr(mv[k][:], stats[k][:].rearrange("p a s -> p (a s)")),
                 reads=[B_small[k]], writes=[B_small[k]])
            S.op("act", I_act(sd[k][:], mv[k][:, 1:2], AF.Sqrt, bias=epsb[:, eps_col:eps_col + 1], scale=1.0),
                 reads=[B_small[k]], writes=[B_rs[k]])
            S.op("dve", I_stt(src, src, mv[k][:, 0:1], lng_sb[ln_slot][:], ALU.subtract, ALU.mult),
                 reads=[B_small[k], B_lngb[ln_slot], srcB], writes=[srcB])
            S.op("dve", I_recip(rstd[k][:], sd[k][:]), reads=[B_rs[k]], writes=[B_rs[k]])
            S.op("dve", I_stt(out_bf, src, rstd[k][:], lnb_sb[ln_slot][:], ALU.mult, ALU.add),
                 reads=[B_rs[k], B_lngb[ln_slot], srcB], writes=[out_bfB])
            if out_f is not None:
                S.op("dve", I_stt(out_f, src, rstd[k][:], lnb_sb[ln_slot][:], ALU.mult, ALU.add),
                     reads=[B_rs[k], B_lngb[ln_slot], srcB], writes=[out_fB])

        def emit_ln_part1(b, banks, bbufs, coef, bias_tile=None):
            for hf in range(2):
                hs = h_tok[:, b, hf * 512:(hf + 1) * 512]
                S.op("dve", I_stt(hs, banks[hf][:], float(coef), hs, ALU.mult, ALU.add),
                     reads=[bbufs[hf]], writes=[B_h[b]])
            if bias_tile is not None:
                hb = h_tok[:, b, :]
                S.op("dve", I_tt(hb, hb, bias_tile, ALU.add), reads=[B_h[b], B_const], writes=[B_h[b]])

        def emit_ln_part2(b, ln_slot):
            k = state["dctr"] % 2
            state["dctr"] += 1
            hb = h_tok[:, b, :]
            ln_core(hb, B_h[b], 0, ln_slot, hb, B_h[b], hbf[k][:], B_hbf[k])
            emit_transpose_block(b, k, hT, B_hT)

        def emit_outproj(nb, nch, srcT, srcB, coef, ln_slot, bias_tile=None):
            pending = []
            for b in range(nb):
                banks, bbufs = [], []
                for hf in range(2):
                    d = state["dbank"] % 3
                    state["dbank"] += 1
                    fns = [I_mm(bankD[d][:], srcT[:, j, b * P:(b + 1) * P],
                                wd_sb[:, j, hf * 512:(hf + 1) * 512], j == 0, j == nch - 1)
                           for j in range(nch)]
                    S.op("pe", fns, reads=[B_wd] + srcB(b), writes=[B_D[d]])
                    banks.append(bankD[d])
                    bbufs.append(B_D[d])
                emit_ln_part1(b, banks, bbufs, coef, bias_tile)
                pending.append(b)
                if len(pending) > LN_DELAY:
                    emit_ln_part2(pending.pop(0), ln_slot)
            while pending:
                emit_ln_part2(pending.pop(0), ln_slot)

        def emit_ffn(nb):
            tiles = token_tiles(nb)
            ln_slot = next_ln()
            for j in range(FC):
                slot = next_slab()
                for ti, (b0, nbt) in enumerate(tiles):
                    n = nbt * P
                    t0 = b0 * P
                    pa = state["actr"] % 2
                    state["actr"] += 1
                    bg, bu = bankA[2 * pa], bankA[2 * pa + 1]
                    fns = []
                    for kc in range(KC):
                        fns.append(I_mm(bg[:, :n], slab_sb[slot][:, kc, 0:P], hT[:, kc, t0:t0 + n],
                                        kc == 0, kc == KC - 1))
                    for kc in range(KC):
                        fns.append(I_mm(bu[:, :n], slab_sb[slot][:, kc, P:2 * P], hT[:, kc, t0:t0 + n],
                                        kc == 0, kc == KC - 1))
                    S.op("pe", fns, reads=[B_slab[slot]] + [B_hT[b] for b in range(b0, b0 + nbt)],
                         writes=[B_A[2 * pa], B_A[2 * pa + 1]])
                    S.op("act", I_act(silu_t[pa][:, :n], bg[:, :n], AF.Silu),
                         reads=[B_A[2 * pa]], writes=[B_silu[pa]])
                    wr = [bact(j, ti), B_A[2 * pa]]
                    if state["scr_dirty"]:
                        wr = wr + [B_scr] + all_act()
                        state["scr_dirty"] = False
                    S.op("dve", I_tt(act[:, j, t0:t0 + n], silu_t[pa][:, :n], bu[:, :n], ALU.mult),
                         reads=[B_silu[pa], B_A[2 * pa + 1]], writes=wr)
                issue_slab()

            def srcB(b):
                ti = [i for i, (b0, nbt) in enumerate(tiles) if b0 <= b < b0 + nbt][0]
                return [bact(j, ti) for j in range(FC)]
            emit_outproj(nb, FC, act, srcB, 0.5 / ALPHA, ln_slot)
            issue_ln()

        def emit_gmlp(gs, nb):
            state["scr_dirty"] = True
            T = nb * P
            tiles = token_tiles(nb)
            uT = carve(0, [P, KC, T], BF16)
            vtok = carve(2 * KC * TM, [P, NBM, D], BF16)
            vblk = [carve(4 * KC * TM + i * 4096, [P, D], F32) for i in range(2)]
            assert 4 * KC * TM + 8192 <= SCR * 2
            scr_deps = [B_scr] + all_act()
            ln_v = next_ln()
            S.dma("sp", [(bs_sb[:].rearrange("p g t -> p (g t)"), gbs_d[gs:gs + 1, :].partition_broadcast(P))],
                  "bs", writes=[B_bs])
            first = True
            for us in range(4):
                slot = next_slab()
                for ti, (b0, nbt) in enumerate(tiles):
                    n = nbt * P
                    t0 = b0 * P
                    for cc in range(2):
                        ch = us * 2 + cc
                        a = state["actr"] % 4
                        state["actr"] += 1
                        fns = [I_mm(bankA[a][:, :n], slab_sb[slot][:, kc, cc * P:(cc + 1) * P],
                                    hT[:, kc, t0:t0 + n], kc == 0, kc == KC - 1) for kc in range(KC)]
                        S.op("pe", fns, reads=[B_slab[slot]] + [B_hT[b] for b in range(b0, b0 + nbt)],
                             writes=[B_A[a]])
                        S.op("act", I_act(uT[:, ch, t0:t0 + n], bankA[a][:, :n], AF.Gelu), reads=[B_A[a]],
                             writes=(scr_deps if first else [B_scr]))
                        first = False
                issue_slab()
            vslots = [next_slab() for _ in range(4)]
            for b in range(nb):
                k = state["dctr"] % 2
                state["dctr"] += 1
                for vs in range(4):
                    a = state["actr"] % 4
                    state["actr"] += 1
                    fns = [I_mm(bankA[a][:, 0:SLABW], hT[:, kc, b * P:(b + 1) * P], slab_sb[vslots[vs]][:, kc, :],
                                kc == 0, kc == KC - 1) for kc in range(KC)]
                    S.op("pe", fns, reads=[B_slab[vslots[vs]], B_hT[b]], writes=[B_A[a]])
                    S.op("act", I_act(vblk[k][:, vs * SLABW:(vs + 1) * SLABW], bankA[a][:, 0:SLABW], AF.Gelu),
                         reads=[B_A[a]], writes=[B_vb[k]])
                ln_core(vblk[k], B_vb[k], 1, ln_v, None, None, vtok[:, b, :], B_scr)
            for _ in range(4):
                issue_slab()
            issue_ln()
            for b in range(nb):
                for hg in range(2):
                    a = state["actr"] % 4
                    state["actr"] += 1
                    fns = []
                    for g4 in range(4):
                        g = hg * 4 + g4
                        fns.append(I_mm(bankA[a][:, g4 * P:(g4 + 1) * P], vtok[:, b, g * P:(g + 1) * P],
                                        wmT[:, gs, g, :], True, True))
                    S.op("pe", fns, reads=[B_scr, B_const], writes=[B_A[a]])
                    tmpm = silu_t[a % 2]
                    S.op("dve", I_tt(tmpm[:], bankA[a][:], bs_sb[:, hg * 4:(hg + 1) * 4, :].rearrange("p g t -> p (g t)"),
                                     ALU.add), reads=[B_A[a], B_bs], writes=[B_silu[a % 2]])
                    uv = uT[:, hg * 4:(hg + 1) * 4, b * P:(b + 1) * P]
                    S.op("dve", I_tt(uv, tmpm[:].rearrange("p (g t) -> p g t", t=P), uv, ALU.mult),
                         reads=[B_silu[a % 2], B_scr], writes=[B_scr])
            ln_slot = next_ln()
            emit_outproj(nb, KC, uT, lambda b: [B_scr], 1.0 / ALPHA, ln_slot)
            issue_ln()

        def emit_hgrn(hs, nb, is_first_pass):
            state["scr_dirty"] = True
            T = nb * P
            NCH = T // 64
            tiles = token_tiles(nb)
            FB = 4 * TM
            qs = carve(0 * FB, [P, TM], F32)
            fv = carve(1 * FB, [P, TM], F32)
            lf = carve(2 * FB, [P, TM], F32)
            gc = carve(3 * FB, [P, TM], F32)
            eg = carve(4 * FB, [P, TM], F32)
            o_raw = carve(0, [P, NCHM, P], F32)
            sq_t = carve(4 * FB, [P, NCHM // 2, P], F32)
            o0 = 5 * FB
            qdT = carve(o0, [P, TM], BF16)
            kdT = carve(o0 + 2 * TM, [P, TM], BF16)
            kdecT = carve(o0 + 4 * TM, [P, TM], BF16)
            o1 = o0 + 6 * TM
            CB = NCHM * P * 2
            kdec64 = carve(o1, [P, NCHM, P], BF16)
            on64 = kdec64
            i64 = carve(o1 + CB, [P, NCHM, P], BF16)
            gs64 = carve(o1 + 2 * CB, [P, NCHM, P], F32)
            at_bf = carve(o1 + 4 * CB, [P, NCHM, 64], BF16)
            assert o1 + 4 * CB + NCHM * 64 * 2 <= SCR * 2
            ss = ss_t
            hh2 = NCH // 2
            for hd in range(8):
                slot = next_slab()
                for ti, (b0, nbt) in enumerate(tiles):
                    n = nbt * P
                    t0 = b0 * P
                    pa = state["actr"] % 2
                    state["actr"] += 1
                    bq_, bf_ = bankA[2 * pa], bankA[2 * pa + 1]
                    fns = []
                    for kc in range(KC):
                        fns.append(I_mm(bq_[:, :n], slab_sb[slot][:, kc, 0:P], hT[:, kc, t0:t0 + n],
                                        kc == 0, kc == KC - 1))
                    for kc in range(KC):
                        fns.append(I_mm(bf_[:, :n], slab_sb[slot][:, kc, P:2 * P], hT[:, kc, t0:t0 + n],
                                        kc == 0, kc == KC - 1))
                    S.op("pe", fns, reads=[B_slab[slot]] + [B_hT[b] for b in range(b0, b0 + nbt)],
                         writes=[B_A[2 * pa], B_A[2 * pa + 1]])
                    S.op("act", I_act(qs[:, t0:t0 + n], bq_[:, :n], AF.Silu), reads=[B_A[2 * pa]],
                         writes=[B_el])
                    S.op("act", I_act(fv[:, t0:t0 + n], bf_[:, :n], AF.Sigmoid), reads=[B_A[2 * pa + 1]],
                         writes=[B_el])
                issue_slab()
                slot2 = next_slab()
                groups = list(range(0, NCH, 2))

                def tm_group(c):
                    a = state["actr"] % 4
                    state["actr"] += 1
                    fns = []
                    for cc in range(2):
                        for kc in range(KC):
                            fns.append(I_mm(bankA[a][0:64, cc * SLABW:(cc + 1) * SLABW],
                                            hT[:, kc, (c + cc) * 64:(c + cc + 1) * 64], slab_sb[slot2][:, kc, :],
                                            kc == 0, kc == KC - 1))
                    S.op("pe", fns, reads=[B_slab[slot2], B_hT[c // 2]], writes=[B_A[a]])
                    pv = bankA[a][0:64, :].rearrange("p (c x) -> p c x", x=SLABW)
                    S.op("act", I_acopy(i64[0:64, c:c + 2, :], pv[:, :, 0:P]), reads=[B_A[a]], writes=[B_ig])
                    S.op("act", I_act(gs64[0:64, c:c + 2, :], pv[:, :, P:2 * P], AF.Silu), reads=[B_A[a]],
                         writes=[B_ig])

                gi = iter(groups)

                def tm_some(k_):
                    for _ in range(k_):
                        c = next(gi, None)
                        if c is not None:
                            tm_group(c)

                S.op("dve", I_ts(fv[:, :T], fv[:, :T], omlbt[:, hd:hd + 1], lbt[:, hd:hd + 1], ALU.mult, ALU.add),
                     reads=[B_el, B_const], writes=[B_el])
                tm_some(1)
                S.op("act", I_act(lf[:, :T], fv[:, :T], AF.Ln), reads=[B_el], writes=[B_el])
                S.op("dve", I_ts(fv[:, :T], fv[:, :T], -1.0, 1.0, ALU.mult, ALU.add), reads=[B_el], writes=[B_el])
                S.op("dve", lambda h, o=gc[:, :T], d0=scanm[:, :T], d1=lf[:, :T]: h.tensor_tensor_scan(
                    o, d0, d1, 0.0, ALU.mult, ALU.add), reads=[B_el, B_const], writes=[B_el])
                tm_some(1)
                S.op("act", I_act(eg[:, :T], gc[:, :T], AF.Exp), reads=[B_el], writes=[B_el])
                S.op("act", I_act(lf[:, :T], gc[:, :T], AF.Exp, scale=-1.0), reads=[B_el], writes=[B_el])
                tm_some(1)
                S.op("dve", I_tt(qdT[:, :T], qs[:, :T], eg[:, :T], ALU.mult), reads=[B_el], writes=[B_el])
                S.op("dve", I_tt(fv[:, :T], fv[:, :T], lf[:, :T], ALU.mult), reads=[B_el], writes=[B_el])
                S.op("dve", I_cp(kdT[:, :T], fv[:, :T]), reads=[B_el], writes=[B_el])
                egl = eg[:, :T].rearrange("p (c s) -> p c s", s=64)[:, :, 63:64]
                egl_b = bass.AP(egl.tensor, egl.offset, [list(egl.ap[0]), list(egl.ap[1]), [0, 64]])
                S.op("dve", I_tt(kdecT[:, :T].rearrange("p (c s) -> p c s", s=64),
                                 fv[:, :T].rearrange("p (c s) -> p c s", s=64), egl_b, ALU.mult),
                     reads=[B_el], writes=[B_el])
                S.op("dve", I_cp(ss[:, 0:NCH], egl.rearrange("p c o -> p (c o)")), reads=[B_el], writes=[B_ss])
                tm_some(len(groups))
                issue_slab()
                for half in range(2):
                    c0, c1 = (0, hh2) if half == 0 else (hh2, NCH)
                    fns = [I_tr(bankT[0:64, c - c0, :], kdecT[:, c * 64:(c + 1) * 64], ident[:]) for c in range(c0, c1)]
                    S.op("pe", fns, reads=[B_el, B_ident], writes=[B_T])
                    S.op("act", I_acopy(kdec64[0:64, c0:c1, :], bankT[0:64, 0:c1 - c0, :]), reads=[B_T],
                         writes=[B_kd])
                for half in range(2):
                    c0, c1 = (0, min(8, NCH)) if half == 0 else (8, NCH)
                    if c1 <= c0:
                        continue
                    a = state["actr"] % 4
                    state["actr"] += 1
                    fns = [I_mm(bankA[a][0:64, (c - c0) * 64:(c - c0 + 1) * 64], kdT[:, c * 64:(c + 1) * 64],
                                qdT[:, c * 64:(c + 1) * 64], True, True) for c in range(c0, c1)]
                    S.op("pe", fns, reads=[B_el], writes=[B_A[a]])
                    S.op("dve", I_tt(at_bf[0:64, c0:c1, :],
                                     bankA[a][0:64, 0:(c1 - c0) * 64].rearrange("p (c t) -> p c t", t=64),
                                     bcast(cm64[:], 1, c1 - c0), ALU.mult),
                         reads=[B_A[a], B_const], writes=[B_at])
                dS = {}
                for c in range(NCH):
                    d = c // 4
                    dS[c] = bankD[d][:, (c % 4) * P:(c % 4 + 1) * P]
                for d in range((NCH + 3) // 4):
                    cs = [c for c in range(NCH) if c // 4 == d]
                    fns = [I_mm(dS[c], kdec64[0:64, c, :], i64[0:64, c, :], True, True) for c in cs]
                    S.op("pe", fns, reads=[B_kd, B_ig], writes=[B_D[d]])
                cur_a = None
                for c in range(NCH):
                    sl = state["sctr"] % 4
                    state["sctr"] += 1
                    if is_first_pass and c == 2 * cfg.halo_blocks and cfg.halo_blocks > 0:
                        S.op("dve", I_ts(S_st[:, hd, :], S_st[:, hd, :], hflag[:, 0:1], None, ALU.mult),
                             reads=[B_S, B_const], writes=[B_S])
                    S.op("dve", I_cp(S_bf[:, sl, :], S_st[:, hd, :]), reads=[B_S], writes=[B_Sbf[sl]])
                    if c % 4 == 0:
                        cur_a = state["actr"] % 4
                        state["actr"] += 1
                    oc = bankA[cur_a][0:64, (c % 4) * P:(c % 4 + 1) * P]
                    fns = [I_mm(oc, at_bf[0:64, c, :], i64[0:64, c, :], True, False),
                           I_mm(oc, qdT[:, c * 64:(c + 1) * 64], S_bf[:, sl, :], False, True)]
                    S.op("pe", fns, reads=[B_at, B_ig, B_el, B_Sbf[sl]], writes=[B_A[cur_a]])
                    S.op("dve", I_stt(S_st[:, hd, :], S_st[:, hd, :], ss[:, c:c + 1], dS[c], ALU.mult, ALU.add),
                         reads=[B_S, B_ss, B_D[c // 4]], writes=[B_S])
                    if c % 4 == 3 or c == NCH - 1:
                        cb = c - (c % 4)
                        S.op("act", I_acopy(o_raw[0:64, cb:c + 1, :],
                                            bankA[cur_a][0:64, 0:(c - cb + 1) * P].rearrange("p (c v) -> p c v", v=P)),
                             reads=[B_A[cur_a]], writes=[B_el])
                for half in range(2):
                    c0, c1 = (0, hh2) if half == 0 else (hh2, NCH)
                    S.op("dve", I_tt(sq_t[0:64, 0:c1 - c0, :], o_raw[0:64, c0:c1, :], o_raw[0:64, c0:c1, :], ALU.mult),
                         reads=[B_el], writes=[B_el])
                    S.op("dve", lambda h, o=ss[0:64, c0:c1], i=sq_t[0:64, 0:c1 - c0, :]: h.tensor_reduce(
                        out=o, in_=i, axis=AX.X, op=ALU.add), reads=[B_el], writes=[B_ss])
                S.op("dve", I_ts(ss[0:64, 0:NCH], ss[0:64, 0:NCH], 1.0 / P, float(RMS_EPS), ALU.mult, ALU.add),
                     reads=[B_ss], writes=[B_ss])
                S.op("act", I_act(ss[0:64, 0:NCH], ss[0:64, 0:NCH], AF.Sqrt), reads=[B_ss], writes=[B_ss])
                S.op("dve", I_recip(ss[0:64, 0:NCH], ss[0:64, 0:NCH]), reads=[B_ss], writes=[B_ss])
                ssb = ss[0:64, 0:NCH]
                ss_b = bass.AP(ssb.tensor, ssb.offset, [list(ssb.ap[0]), list(ssb.ap[1]), [0, P]])
                orw = o_raw[0:64, 0:NCH, :]
                S.op("dve", I_tt(orw, orw, ss_b, ALU.mult), reads=[B_el, B_ss], writes=[B_el])
                S.op("dve", I_tt(orw, orw, bcast(ngt[0:64, :], 1, NCH), ALU.mult), reads=[B_el, B_const],
                     writes=[B_el])
                S.op("dve", I_tt(on64[0:64, 0:NCH, :], orw, gs64[0:64, 0:NCH, :], ALU.mult), reads=[B_el, B_ig],
                     writes=[B_kd])
                fns = [I_tr(bankT[:, c // 2, (c % 2) * 64:(c % 2 + 1) * 64], on64[0:64, c, :], ident[0:64, 0:64])
                       for c in range(NCH)]
                S.op("pe", fns, reads=[B_kd, B_ident], writes=[B_T])
                S.op("act", I_acopy(mixT[:, hd, 0:T], bankT[:, 0:nb, :].rearrange("p b t -> p (b t)")),
                     reads=[B_T], writes=B_mixT[:nb])
            ln_slot = next_ln()
            emit_outproj(nb, KC, mixT, lambda b: [B_mixT[b]], 1.0 / ALPHA, ln_slot)
            issue_ln()

        def emit_attn(as_, nb, is_first_pass):
            state["scr_dirty"] = True
            T = nb * P
            tiles = token_tiles(nb)
            qT = carve(0, [P, KC, TM], BF16)
            o0 = 2 * KC * TM
            e_bf = [carve(o0 + i * 512, [P, 256], BF16) for i in range(2)]
            eT = [carve(o0 + 1024 + i * 512, [P, 2, P], BF16) for i in range(2)]
            scr_deps = [B_scr] + all_act()
            first = True
            for qs_ in range(4):
                slot = next_slab()
                for ti, (b0, nbt) in enumerate(tiles):
                    n = nbt * P
                    t0 = b0 * P
                    for cc in range(2):
                        ch = qs_ * 2 + cc
                        a = state["actr"] % 4
                        state["actr"] += 1
                        fns = [I_mm(bankA[a][:, :n], slab_sb[slot][:, kc, cc * P:(cc + 1) * P],
                                    hT[:, kc, t0:t0 + n], kc == 0, kc == KC - 1) for kc in range(KC)]
                        S.op("pe", fns, reads=[B_slab[slot]] + [B_hT[b] for b in range(b0, b0 + nbt)],
                             writes=[B_A[a]])
                        S.op("act", I_act(qT[:, ch, t0:t0 + n], bankA[a][:, :n], AF.Identity,
                                          bias=bq8[:, ch:ch + 1], scale=0.125), reads=[B_A[a], B_const],
                             writes=(scr_deps if first else [B_scr]))
                        first = False
                issue_slab()
            slot = next_slab()
            for ti, (b0, nbt) in enumerate(tiles):
                n = nbt * P
                t0 = b0 * P
                for kvh in range(2):
                    a = state["actr"] % 4
                    state["actr"] += 1
                    fns = [I_mm(bankA[a][:, :n], slab_sb[slot][:, kc, kvh * P:(kvh + 1) * P],
                                hT[:, kc, t0:t0 + n], kc == 0, kc == KC - 1) for kc in range(KC)]
                    S.op("pe", fns, reads=[B_slab[slot]] + [B_hT[b] for b in range(b0, b0 + nbt)],
                         writes=[B_A[a]])
                    S.op("act", I_act(kT2[:, kvh, P + t0:P + t0 + n], bankA[a][:, :n], AF.Identity,
                                      bias=bk2[:, kvh:kvh + 1], scale=1.0), reads=[B_A[a], B_const],
                         writes=[B_kv])
            issue_slab()
            slot = next_slab()
            for b in range(nb):
                a = state["actr"] % 4
                state["actr"] += 1
                fns = [I_mm(bankA[a][:, 0:P], hT[:, kc, b * P:(b + 1) * P], slab_sb[slot][:, kc, 0:P],
                            kc == 0, kc == KC - 1) for kc in range(KC)]
                S.op("pe", fns, reads=[B_slab[slot], B_hT[b]], writes=[B_A[a]])
                S.op("dve", I_tt(vtk[:, b + 1, :], bankA[a][:, 0:P], bvt[:], ALU.add), reads=[B_A[a], B_const],
                     writes=[B_kv])
            issue_slab()
            def stage1(b, hq, mask):
                kvh = hq // 8
                pb = (hq % 2) * 64
                a = state["actr"] % 4
                state["actr"] += 1
                sm = att_small[hq % 2]
                fns = [I_mm(bankA[a][:, 0:256], qT[pb:pb + 64, hq // 2, b * P:(b + 1) * P],
                            kT2[pb:pb + 64, kvh, b * P:(b + 2) * P], True, False),
                       I_mm(bankA[a][:, 0:256], ident[:], mask[:], False, True)]
                S.op("pe", fns, reads=[B_scr, B_kv, B_const, B_ident], writes=[B_A[a]])
                bsm = B_small[hq % 2]
                S.op("dve", lambda h, o=sm[:, 0:1], i=bankA[a][:, 0:256]: h.tensor_reduce(
                    out=o, in_=i, axis=AX.X, op=ALU.max), reads=[B_A[a]], writes=[bsm])
                S.op("dve", I_tt(sm[:, 0:1], sm[:, 0:1], sinkt[:, hq:hq + 1], ALU.max), reads=[bsm, B_const],
                     writes=[bsm])
                S.op("dve", I_ts(sm[:, 1:2], sm[:, 0:1], -1.0, None, ALU.mult), reads=[bsm], writes=[bsm])
                ei = hq % 2
                S.op("act", I_act(e_bf[ei][:], bankA[a][:, 0:256], AF.Exp, bias=sm[:, 1:2], scale=1.0,
                                  accum_out=sm[:, 2:3]), reads=[B_A[a], bsm], writes=[B_e[ei], bsm, B_A[a]])
                S.op("act", I_act(sm[:, 3:4], sinkt[:, hq:hq + 1], AF.Exp, bias=sm[:, 1:2], scale=1.0),
                     reads=[bsm, B_const], writes=[bsm])
                S.op("dve", I_tt(sm[:, 4:5], sm[:, 2:3], sm[:, 3:4], ALU.add), reads=[bsm], writes=[bsm])
                S.op("dve", I_recip(sm[:, 5:6], sm[:, 4:5]), reads=[bsm], writes=[bsm])

            def stage2(b, hq, k, ob):
                kvh = hq // 8
                ei = hq % 2
                sm = att_small[hq % 2]
                bsm = B_small[hq % 2]
                fns = [I_tr(bankT[:, j, :], e_bf[ei][:, j * P:(j + 1) * P], ident[:]) for j in range(2)]
                S.op("pe", fns, reads=[B_e[ei], B_ident], writes=[B_T])
                S.op("act", I_acopy(eT[ei][:], bankT[:, 0:2, :]), reads=[B_T], writes=[B_eT[ei]])
                if hq % 8 == 0:
                    d = state["dbank"] % 3
                    state["dbank"] += 1
                    ob[hq // 8] = d
                d = ob[hq // 8]
                oc = bankD[d][:, (hq % 8) * 64:(hq % 8 + 1) * 64]
                fns = [I_mm(oc, eT[ei][:, 0, :], vtk[:, b, kvh * 64:(kvh + 1) * 64], True, False),
                       I_mm(oc, eT[ei][:, 1, :], vtk[:, b + 1, kvh * 64:(kvh + 1) * 64], False, True)]
                S.op("pe", fns, reads=[B_eT[ei], B_kv], writes=[B_D[d]])
                S.op("dve", I_ts(hbf[k][:, hq * 64:(hq + 1) * 64], oc, sm[:, 5:6], None, ALU.mult),
                     reads=[B_D[d], bsm], writes=[B_hbf[k], B_D[d]])

            for b in range(nb):
                k = state["dctr"] % 2
                state["dctr"] += 1
                use_first = is_first_pass and b == cfg.halo_blocks
                mask = m2fb if use_first else m2b
                ob = [None, None]
                stage1(b, 0, mask)
                for hq in range(16):
                    if hq + 1 < 16:
                        stage1(b, hq + 1, mask)
                    stage2(b, hq, k, ob)
                emit_transpose_block(b, k, mixT, B_mixT)
            S.op("dve", I_cp(kT2[:, :, 0:P], kT2[:, :, nb * P:(nb + 1) * P]), reads=[B_kv], writes=[B_kv])
            S.op("dve", I_cp(vtk[:, 0, :], vtk[:, nb, :]), reads=[B_kv], writes=[B_kv])
            ln_slot = next_ln()
            emit_outproj(nb, KC, mixT, lambda b: [B_mixT[b]], 1.0 / ALPHA, ln_slot, bias_tile=bot[:])
            issue_ln()

        subs = []
        for li in range(L):
            subs.append(("ffn", li, 0))
            subs.append(("mix", li, li % 3))
            subs.append(("ffn", li, 1))
        if cfg.sub_limit is not None:
            subs = subs[:cfg.sub_limit]
        npass = len(cfg.pass_blocks)
        wd_seq = []
        for _ in range(npass):
            for (kind, li, x_) in subs:
                if kind == "ffn":
                    for j in range(FC):
                        slab_plan.append(wgu_d[li * 2 + x_, j])
                    ln_plan.append(li * 3 + (0 if x_ == 0 else 2))
                    wd_seq.append((wd_d[li * 2 + x_], FC))
                else:
                    slot_ = li // 3
                    if x_ == 0:
                        for s_ in range(8):
                            slab_plan.append(gwin_d[slot_, s_])
                        ln_plan.append(L * 3 + slot_)
                        wd_seq.append((gwout_d[slot_], KC))
                    elif x_ == 1:
                        for s_ in range(16):
                            slab_plan.append(hwin_d[slot_, s_])
                        wd_seq.append((hwout_d[slot_], KC))
                    else:
                        for s_ in range(6):
                            slab_plan.append(awqkv_d[slot_, s_])
                        wd_seq.append((awo_d[slot_], KC))
                    ln_plan.append(li * 3 + 1)
        wd_ctr = [0]

        def issue_next_wd():
            if wd_ctr[0] < len(wd_seq):
                issue_wd(*wd_seq[wd_ctr[0]])
                wd_ctr[0] += 1

        for _ in range(NSLAB):
            issue_slab()
        issue_ln()
        issue_ln()
        issue_next_wd()

        out_evs = []
        blk0 = 0
        for pi, nb in enumerate(cfg.pass_blocks):
            src = x_d[blk0 * P:(blk0 + nb) * P, :].rearrange("(b p) d -> p b d", p=P)
            S.dma("sp", [(h_tok[:, 0:nb, :], src)], "xload", writes=B_h[:nb])
            for b in range(nb):
                k = state["dctr"] % 2
                state["dctr"] += 1
                S.op("act", I_acopy(hbf[k][:], h_tok[:, b, :]), reads=[B_h[b]], writes=[B_hbf[k]])
                emit_transpose_block(b, k, hT, B_hT)
            for (kind, li, x_) in subs:
                if kind == "ffn":
                    emit_ffn(nb)
                elif x_ == 0:
                    emit_gmlp(li // 3, nb)
                elif x_ == 1:
                    emit_hgrn(li // 3, nb, pi == 0)
                else:
                    emit_attn(li // 3, nb, pi == 0)
                issue_next_wd()
            pairs = []
            for b in range(nb):
                gb = blk0 + b
                if gb < cfg.halo_blocks:
                    continue
                ob_ = gb - cfg.halo_blocks
                pairs.append((out_d[ob_ * P:(ob_ + 1) * P, :], h_tok[:, b, :]))
            if pairs:
                ev = S.dma("sp", pairs, "store", reads=B_h[:nb])
                out_evs.append(ev)
            blk0 += nb
        S.wait_all("sp", out_evs)

        with nc.Block() as block:
            @block.tensor
            def _(h):
                S.replay("pe", h)

            @block.scalar
            def _(h):
                S.replay("act", h)

            @block.vector
            def _(h):
                S.replay("dve", h)

            @block.gpsimd
            def _(h):
                S.replay("pool", h)

            @block.sync
            def _(h):
                S.replay("sp", h)
    return nc


def make_in_maps(cfg, x_cores, inputs, L, flags=None):
    f32 = np.float32
    ncore = len(x_cores)
    if flags is None:
        flags = [1.0] * ncore
    NBM = max(cfg.pass_blocks)
    TM = NBM * P
    NG_, NH_, NA_ = n_gmlp(L), n_hgrn(L), n_attn(L)
    wgu = inputs["ffn_w_gate_up"]
    wd = inputs["ffn_w_down"]
    common = {
        "wgu": np.stack([slab_layout(gate_up_interleave(wgu[li, fi])) for li in range(L) for fi in range(2)]),
        "wd": np.ascontiguousarray(wd[:L].reshape(L * 2, DFF, D)),
        "lng": np.ascontiguousarray(np.concatenate(
            [inputs["ln_gain"][:L].reshape(L * 3, D), inputs["gmlp_ln_gain"][:2].reshape(-1, D)], 0)[:L * 3 + 2]),
        "lnb": np.ascontiguousarray(np.concatenate(
            [inputs["ln_bias"][:L].reshape(L * 3, D), inputs["gmlp_ln_bias"][:2].reshape(-1, D)], 0)[:L * 3 + 2]),
        "ident": np.eye(P, dtype=f32),
        "tril": np.tril(np.ones((P, P), f32)),
        "cm64": np.triu(np.ones((64, 64), f32)),
        "scanm": np.tile((np.arange(TM) % 64 != 0).astype(f32)[None, :], (P, 1)),
        "hlb": np.ascontiguousarray(inputs["hgrn_lb_logits"].reshape(DEPTH, 8, P).transpose(2, 0, 1)),
    }
    ng = max(NG_, 1)
    common["gwin"] = np.stack([slab_layout(inputs["gmlp_w_in"][s]) for s in range(ng)])
    common["gwout"] = np.ascontiguousarray(inputs["gmlp_w_out"][:ng])
    common["gwsp"] = np.ascontiguousarray(inputs["gmlp_w_spatial"][:ng])
    common["gbs"] = np.ascontiguousarray(inputs["gmlp_b_spatial"][:ng].reshape(ng, 8 * P))
    hw = inputs["hgrn_w_in"][0]
    q_, f_, i_, g_ = [hw[:, j * D:(j + 1) * D].reshape(D, 8, P) for j in range(4)]
    hperm = np.concatenate([np.concatenate([q_[:, h], f_[:, h], i_[:, h], g_[:, h]], axis=1) for h in range(8)], axis=1)
    common["hwin"] = slab_layout(hperm)[None]
    common["hwout"] = np.ascontiguousarray(inputs["hgrn_w_out"][:1])
    common["hng"] = np.ascontiguousarray(inputs["hgrn_norm_gain"][:1])
    aw = inputs["attn_w_qkv"][0]
    ab = inputs["attn_b_qkv"][0]
    k0, k1 = aw[:, 1024:1088], aw[:, 1088:1152]
    awp = np.concatenate([aw[:, :1024], k0, k0, k1, k1, aw[:, 1152:1280], np.zeros((D, 128), f32)], axis=1)
    common["awqkv"] = slab_layout(awp)[None]
    common["abq"] = np.ascontiguousarray(ab[:1024].reshape(8, P).T)[None]
    bk0, bk1 = ab[1024:1088], ab[1088:1152]
    common["abk"] = np.ascontiguousarray(np.stack([np.concatenate([bk0, bk0]), np.concatenate([bk1, bk1])], 1))[None]
    common["abv"] = np.ascontiguousarray(ab[1152:1280])[None]
    common["asink"] = np.ascontiguousarray(inputs["attn_sinks"][:1])
    common["awo"] = np.ascontiguousarray(inputs["attn_w_o"][:1])
    common["abo"] = np.ascontiguousarray(inputs["attn_b_o"][:1])
    NEG = -30000.0
    qi = np.arange(P)[:, None]
    kj = np.arange(P)[None, :]
    m_prev = np.where(kj > qi, 0.0, NEG).astype(f32)
    m_cur = np.where(kj <= qi, 0.0, NEG).astype(f32)
    common["m2"] = np.concatenate([m_prev, m_cur], 1)
    maps = []
    for c in range(ncore):
        m = dict(common)
        m["x"] = np.ascontiguousarray(x_cores[c], dtype=f32)
        m["hflag"] = np.full((P, 1), flags[c], f32)
        if flags[c] > 0:
            m["m2f"] = common["m2"]
        else:
            m["m2f"] = np.concatenate([np.full((P, P), NEG, f32), m_cur], 1)
        maps.append(m)
    return maps


PASS_BLOCKS = [6, 6, 6, 6, 6, 4]


def kernel(**inputs):
    inputs = {k: np.asarray(v) for k, v in inputs.items()}
    x = inputs["x"]
    cfg = Cfg(PASS_BLOCKS, HALO // P)
    x_cores, flags = [], []
    for c in range(NCORES):
        b, p = divmod(c, 4)
        xc = np.zeros((cfg.ntok, D), np.float32)
        s = p * CHUNK - HALO
        if s < 0:
            xc[HALO:] = x[b, 0:CHUNK]
            flags.append(0.0)
        else:
            xc[:] = x[b, s:s + cfg.ntok]
            flags.append(1.0)
        x_cores.append(xc)
    nc = build_program(cfg)
    in_maps = make_in_maps(cfg, x_cores, inputs, DEPTH, flags)
    res = run_bass_kernel_spmd(nc, in_maps, core_ids=list(range(NCORES)))
    out = np.empty((BATCH, SEQ, D), np.float32)
    for c in range(NCORES):
        b, p = divmod(c, 4)
        out[b, p * CHUNK:(p + 1) * CHUNK] = res.results[c]["out"]
    return out
```

```python
import contextlib
import numpy as np
import concourse.bass as bass
import concourse.mybir as mybir
from concourse.bass_utils import run_bass_kernel_spmd

F32 = mybir.dt.float32
BF16 = mybir.dt.bfloat16
AF = mybir.ActivationFunctionType
ALU = mybir.AluOpType
AX = mybir.AxisListType

P = 128
D = 1024
KC = 8
DFF = 2816
FC = 22
DEPTH = 4
ALPHA = (2.0 * DEPTH) ** 0.25
LN_EPS = 1e-5
RMS_EPS = 1e-6
SEQ = 16384
BATCH = 2
NCORES = 8
CHUNK = SEQ // 4
HALO = 256
SLABW = 256
LN_DELAY = 1


class Buf:
    __slots__ = ("name", "w", "r")

    def __init__(self, name):
        self.name = name
        self.w = None
        self.r = []


class Sched:
    ENGS = ("pe", "act", "dve", "pool", "sp")

    def __init__(self, nc, stack):
        self.nc = nc
        self.stack = stack
        self.streams = {e: [] for e in self.ENGS}
        self.esem = {}
        self.ecnt = {}
        self.eepoch = {e: 0 for e in self.ENGS}
        self.seen = {e: {} for e in self.ENGS}
        self.semobj = {}
        self.nsem = 0
        for e in self.ENGS:
            self._new_epoch(e)
        self.dsem = {}
        self.dcnt = {}

    def _mksem(self, name):
        s = self.stack.enter_context(self.nc.semaphore(name))
        self.nsem += 1
        self.semobj[name] = s
        return name

    def _new_epoch(self, e):
        self.eepoch[e] += 1
        self.esem[e] = self._mksem(f"e_{e}_{self.eepoch[e]}")
        self.ecnt[e] = 0

    def _waits(self, eng, deps):
        best = {}
        for d in deps:
            if d is None:
                continue
            s, v = d
            if v > best.get(s, 0):
                best[s] = v
        out = []
        seen = self.seen[eng]
        for s, v in best.items():
            if seen.get(s, 0) >= v:
                continue
            seen[s] = v
            out.append((s, v))
        return out

    def _deps(self, reads, writes):
        deps = []
        for b in reads:
            deps.append(b.w)
        for b in writes:
            deps.append(b.w)
            deps.extend(b.r)
        return deps

    def op(self, eng, fns, reads=(), writes=()):
        if callable(fns):
            fns = [fns]
        if self.ecnt[eng] > 30000:
            self._new_epoch(eng)
        st = self.streams[eng]
        for s, v in self._waits(eng, self._deps(reads, writes)):
            st.append(("w", s, v))
        for f in fns[:-1]:
            st.append(("i", f, None))
        self.ecnt[eng] += 1
        ev = (self.esem[eng], self.ecnt[eng])
        st.append(("i", fns[-1], ev))
        for b in reads:
            b.r.append(ev)
        for b in writes:
            b.w = ev
            b.r = []
        return ev

    def dma(self, eng, pairs, semkey, reads=(), writes=()):
        if semkey not in self.dsem:
            self.dsem[semkey] = self._mksem(f"d_{semkey}")
            self.dcnt[semkey] = 0
        st = self.streams[eng]
        for s, v in self._waits(eng, self._deps(reads, writes)):
            st.append(("w", s, v))
        s = self.dsem[semkey]
        for (o, i) in pairs:
            self.dcnt[semkey] += 16
            st.append(("d", (o, i), s))
        ev = (s, self.dcnt[semkey])
        for b in reads:
            b.r.append(ev)
        for b in writes:
            b.w = ev
            b.r = []
        return ev

    def wait_all(self, eng, evs):
        st = self.streams[eng]
        for s, v in self._waits(eng, evs):
            st.append(("w", s, v))

    def replay(self, eng, h):
        so = self.semobj
        for kind, a, b in self.streams[eng]:
            if kind == "w":
                h.wait_ge(so[a], b)
            elif kind == "i":
                ins = a(h)
                if b is not None:
                    ins.then_inc(so[b[0]], 1)
            else:
                o, i = a
                h.dma_start(out=o, in_=i).then_inc(so[b], 16)


def I_mm(out, lhsT, rhs, start, stop):
    return lambda h: h.matmul(out, lhsT, rhs, start=start, stop=stop)


def I_tr(out, in_, ident):
    return lambda h: h.transpose(out, in_, ident)


def I_act(out, in_, func, **kw):
    return lambda h: h.activation(out=out, in_=in_, func=func, **kw)


def I_acopy(out, in_):
    return lambda h: h.copy(out, in_)


def I_tt(out, in0, in1, op):
    return lambda h: h.tensor_tensor(out=out, in0=in0, in1=in1, op=op)


def I_ts(out, in0, s1, s2, op0, op1=None):
    if op1 is None:
        return lambda h: h.tensor_scalar(out=out, in0=in0, scalar1=s1, scalar2=None, op0=op0)
    return lambda h: h.tensor_scalar(out=out, in0=in0, scalar1=s1, scalar2=s2, op0=op0, op1=op1)


def I_stt(out, in0, scalar, in1, op0, op1):
    return lambda h: h.scalar_tensor_tensor(out=out, in0=in0, scalar=scalar, in1=in1, op0=op0, op1=op1)


def I_cp(out, in_):
    return lambda h: h.tensor_copy(out, in_)


def I_bnstats(out, in_):
    return lambda h: h.bn_stats(out, in_)


def I_bnaggr(out, in_):
    return lambda h: h.bn_aggr(out, in_)


def I_recip(out, in_):
    return lambda h: h.reciprocal(out, in_)


def I_memset(ap, v):
    return lambda h: h.memset(ap, v)

def slab_layout(w):
    k, n = w.shape
    assert k == D and n % SLABW == 0
    return np.ascontiguousarray(
        w.reshape(KC, P, n // SLABW, SLABW).transpose(2, 1, 0, 3).reshape(n // SLABW, P, KC * SLABW))


def gate_up_interleave(w):
    g = w[:, :DFF].reshape(D, FC, P)
    u = w[:, DFF:].reshape(D, FC, P)
    return np.concatenate([g, u], axis=2).reshape(D, 2 * DFF)


class Cfg:
    def __init__(self, pass_blocks, halo_blocks, n_layers=DEPTH, sub_limit=None):
        self.pass_blocks = list(pass_blocks)
        self.halo_blocks = halo_blocks
        self.n_layers = n_layers
        self.sub_limit = sub_limit
        self.nblk = sum(pass_blocks)
        self.ntok = self.nblk * P
        self.nout = (self.nblk - halo_blocks) * P


def token_tiles(nb):
    nt = (nb + 3) // 4
    base, rem = divmod(nb, nt)
    out, s = [], 0
    for i in range(nt):
        n = base + (1 if i < rem else 0)
        out.append((s, n))
        s += n
    return out


def bcast(ap, axis, n):
    dims = [list(d) for d in ap.ap]
    dims.insert(axis, [0, n])
    return bass.AP(ap.tensor, ap.offset, dims)


def n_gmlp(L):
    return (L + 2) // 3


def n_hgrn(L):
    return (L + 1) // 3


def n_attn(L):
    return L // 3


def build_program(cfg):
    nc = bass.Bass("TRN2", target_bir_lowering=False)
    NBM = max(cfg.pass_blocks)
    TM = NBM * P
    NCHM = TM // 64
    L = cfg.n_layers
    NG_, NH_, NA_ = n_gmlp(L), n_hgrn(L), n_attn(L)

    def din(name, shape, dt=F32):
        return nc.dram_tensor(name, list(shape), dt, kind="ExternalInput").ap()

    x_d = din("x", [cfg.ntok, D])
    wgu_d = din("wgu", [L * 2, FC, P, KC * SLABW])
    wd_d = din("wd", [L * 2, DFF, D])
    lng_d = din("lng", [L * 3 + 2, D])
    lnb_d = din("lnb", [L * 3 + 2, D])
    ident_d = din("ident", [P, P])
    gwin_d = din("gwin", [max(NG_, 1), 8, P, KC * SLABW])
    gwout_d = din("gwout", [max(NG_, 1), D, D])
    gwsp_d = din("gwsp", [max(NG_, 1), 8, P, P])
    gbs_d = din("gbs", [max(NG_, 1), 8 * P])
    tril_d = din("tril", [P, P])
    hwin_d = din("hwin", [max(NH_, 1), 16, P, KC * SLABW])
    hwout_d = din("hwout", [max(NH_, 1), D, D])
    hlb_d = din("hlb", [P, DEPTH, 8])
    hng_d = din("hng", [max(NH_, 1), P])
    hngc_d = din("hngc", [P, 1])
    cm64_d = din("cm64", [64, 64])
    scanm_d = din("scanm", [P, TM])
    hflag_d = din("hflag", [P, 1])
    awqkv_d = din("awqkv", [max(NA_, 1), 6, P, KC * SLABW])
    abq_d = din("abq", [max(NA_, 1), P, 8])
    abk_d = din("abk", [max(NA_, 1), P, 2])
    abv_d = din("abv", [max(NA_, 1), P])
    asink_d = din("asink", [max(NA_, 1), 16])
    awo_d = din("awo", [max(NA_, 1), D, D])
    abo_d = din("abo", [max(NA_, 1), D])
    m2_d = din("m2", [P, 256])
    m2f_d = din("m2f", [P, 256])
    out_d = nc.dram_tensor("out", [cfg.nout, D], F32, kind="ExternalOutput").ap()

    stack = contextlib.ExitStack()
    with stack:
        def sb(name, shape, dt):
            return stack.enter_context(nc.sbuf_tensor(name, list(shape), dt))

        def ps(name, shape, dt):
            return stack.enter_context(nc.psum_tensor(name, list(shape), dt))

        h_tok = sb("h_tok", [P, NBM, D], F32)
        hT = sb("hT", [P, KC, TM], BF16)
        mixT = sb("mixT", [P, KC, TM], BF16)
        act = sb("act", [P, FC, TM], BF16)
        SCR = FC * TM
        wd_sb = sb("wd_sb", [P, FC, D], BF16)
        NSLAB = 5
        slab_sb = [sb(f"slab{i}", [P, KC, SLABW], BF16) for i in range(NSLAB)]
        lng_sb = [sb(f"lng{i}", [P, D], F32) for i in range(2)]
        lnb_sb = [sb(f"lnb{i}", [P, D], F32) for i in range(2)]
        NHBF = 3
        hbf = [sb(f"hbf{i}", [P, D], BF16) for i in range(NHBF)]
        silu_t = [sb(f"silu{i}", [P, 512], F32) for i in range(2)]
        ident_f = sb("ident_f32", [P, P], F32)
        ident = sb("ident_bf", [P, P], BF16)
        stats = [sb(f"stats{i}", [P, 2, 6], F32) for i in range(2)]
        mv = [sb(f"mv{i}", [P, 2], F32) for i in range(2)]
        sd = [sb(f"sd{i}", [P, 1], F32) for i in range(2)]
        rstd = [sb(f"rstd{i}", [P, 1], F32) for i in range(2)]
        epsb = sb("epsb", [P, 2], F32)
        wmT = sb("wmT", [P, max(NG_, 1), 8, P], BF16)
        bs_sb = sb("bs_sb", [P, 8, P], F32)
        S_st = sb("S_st", [P, 8, P], F32)
        S_bf = sb("S_bf", [P, 4, P], BF16)
        scanm = sb("scanm_sb", [P, TM], F32)
        lbt = sb("lbt", [P, 8], F32)
        omlbt = sb("omlbt", [P, 8], F32)
        ngt = sb("ngt", [P, P], F32)
        cm64 = sb("cm64_sb", [64, 64], F32)
        hflag = sb("hflag_sb", [P, 1], F32)
        ss_t = sb("ss_t", [P, 16], F32)
        ngcol = sb("ngcol", [P, 1], F32)
        kT2 = sb("kT2", [P, 2, (NBM + 1) * P], BF16)
        vtk = sb("vtk", [P, NBM + 1, P], BF16)
        m2b = sb("m2b", [P, 256], BF16)
        m2fb = sb("m2fb", [P, 256], BF16)
        bq8 = sb("bq8", [P, 8], F32)
        bk2 = sb("bk2", [P, 2], F32)
        bvt = sb("bvt", [P, P], F32)
        sinkt = sb("sinkt", [P, 16], F32)
        bot = sb("bot", [P, D], F32)
        att_small = [sb(f"atts{i}", [P, 8], F32) for i in range(2)]

        bankA = [ps(f"bankA{i}", [P, 512], F32) for i in range(4)]
        bankD = [ps(f"bankD{i}", [P, 512], F32) for i in range(3)]
        bankT = ps("bankT", [P, KC, P], BF16)

        S = Sched(nc, stack)

        def carve(off, shape, dt):
            flat = act[:].rearrange("p c t -> p (c t)")
            n = 1
            for s_ in shape[1:]:
                n *= s_
            if dt == F32:
                assert off % 4 == 0
                v = flat[:, off // 2: off // 2 + 2 * n].bitcast(F32)
                nbytes = 4 * n
            else:
                v = flat[:, off // 2: off // 2 + n]
                nbytes = 2 * n
            assert off + nbytes <= SCR * 2, (off, nbytes, SCR * 2)
            if len(shape) == 3:
                v = v.rearrange("p (a b) -> p a b", b=shape[2])
            elif len(shape) == 4:
                v = v.rearrange("p (a b c) -> p a b c", b=shape[2], c=shape[3])
            return v

        B_h = [Buf(f"h{b}") for b in range(NBM)]
        B_hT = [Buf(f"hT{b}") for b in range(NBM)]
        B_mixT = [Buf(f"mixT{b}") for b in range(NBM)]
        B_scr = Buf("scratch")
        B_act = {}
        B_wd = Buf("wd")
        B_slab = [Buf(f"slab{i}") for i in range(NSLAB)]
        B_lngb = [Buf(f"lngb{i}") for i in range(2)]
        B_hbf = [Buf(f"hbf{i}") for i in range(NHBF)]
        B_silu = [Buf(f"silu{i}") for i in range(2)]
        B_A = [Buf(f"A{i}") for i in range(4)]
        B_D = [Buf(f"D{i}") for i in range(3)]
        B_T = Buf("T")
        B_ident = Buf("ident")
        B_small = [Buf(f"small{i}") for i in range(2)]
        B_rs = [Buf(f"rs{i}") for i in range(2)]
        B_const = Buf("const")
        B_bs = Buf("bs")
        B_S = Buf("S")
        B_Sbf = [Buf(f"Sbf{i}") for i in range(4)]
        B_kv = Buf("kv")
        B_vb = [Buf("vb0"), Buf("vb1")]
        B_el, B_ig, B_kd, B_at, B_ss = Buf("el"), Buf("ig"), Buf("kd"), Buf("at"), Buf("ss")
        B_iT, B_gs = Buf("iT"), Buf("gs")
        B_e = [Buf("e0"), Buf("e1")]
        B_eT = [Buf("eT0"), Buf("eT1")]

        def bact(j, t):
            k = (j, t)
            if k not in B_act:
                B_act[k] = Buf(f"act{k}")
            return B_act[k]

        def all_act():
            return list(B_act.values())

        state = {"slab_issue": 0, "slab_use": 0, "dctr": 0, "actr": 0, "ln_issue": 0, "ln_use": 0,
                 "dbank": 0, "sctr": 0, "scr_dirty": True}

        S.dma("sp", [(ident_f[:], ident_d[:, :])], "ident", writes=[B_ident])
        S.op("dve", I_cp(ident[:], ident_f[:]), reads=[B_ident], writes=[B_ident])
        S.op("dve", I_memset(epsb[:, 0:1], float(LN_EPS / ALPHA ** 2)), writes=[B_small[0], B_small[1]])
        S.op("dve", I_memset(epsb[:, 1:2], float(LN_EPS)), writes=[B_small[0], B_small[1]])
        cpairs = [(scanm[:], scanm_d[:, :]), (cm64[:], cm64_d[:, :]), (hflag[:], hflag_d[:, :])]
        tmp_f = carve(0, [P, 8, P], F32)
        tmp_f2 = carve(4096, [P, 8, P], F32)
        if NH_ > 0:
            cpairs.append((ngt[:], hng_d[0:1, :].partition_broadcast(P)))
            cpairs.append((ngcol[:], hngc_d[:, :]))
            cpairs.append((tmp_f[:, 0:DEPTH, 0:8], hlb_d[:, :, :]))
        if NA_ > 0:
            cpairs += [(bq8[:], abq_d[0]), (bk2[:], abk_d[0]),
                       (bvt[:], abv_d[0:1, :].partition_broadcast(P)),
                       (sinkt[:], asink_d[0:1, :].partition_broadcast(P)),
                       (bot[:], abo_d[0:1, :].partition_broadcast(P)),
                       (tmp_f2[:, 0, :], m2_d[:, 0:128]), (tmp_f2[:, 1, :], m2_d[:, 128:256]),
                       (tmp_f2[:, 2, :], m2f_d[:, 0:128]), (tmp_f2[:, 3, :], m2f_d[:, 128:256])]
        S.dma("sp", cpairs, "const", writes=[B_const, B_scr])
        if NH_ > 0:
            hl = 1
            e4 = tmp_f[:, 0:DEPTH, 0:8]
            mx = tmp_f[:, 4, 0:8]
            S.op("dve", I_tt(mx, tmp_f[:, 0, 0:8], tmp_f[:, 1, 0:8], ALU.max), reads=[B_const, B_scr],
                 writes=[B_scr])
            for l in range(2, DEPTH):
                S.op("dve", I_tt(mx, mx, tmp_f[:, l, 0:8], ALU.max), reads=[B_scr], writes=[B_scr])
            for l in range(DEPTH):
                S.op("dve", I_tt(tmp_f[:, l, 0:8], tmp_f[:, l, 0:8], mx, ALU.subtract), reads=[B_scr],
                     writes=[B_scr])
            for l in range(DEPTH):
                S.op("act", I_act(tmp_f[:, l, 0:8], tmp_f[:, l, 0:8], AF.Exp), reads=[B_scr], writes=[B_scr])
            den = tmp_f[:, 5, 0:8]
            S.op("dve", I_tt(den, tmp_f[:, 0, 0:8], tmp_f[:, 1, 0:8], ALU.add), reads=[B_scr], writes=[B_scr])
            for l in range(2, DEPTH):
                S.op("dve", I_tt(den, den, tmp_f[:, l, 0:8], ALU.add), reads=[B_scr], writes=[B_scr])
            S.op("dve", I_recip(den, den), reads=[B_scr], writes=[B_scr])
            num = tmp_f[:, 6, 0:8]
            S.op("dve", I_cp(num, tmp_f[:, 1, 0:8]), reads=[B_scr], writes=[B_scr])
            for l in range(2, hl + 1):
                S.op("dve", I_tt(num, num, tmp_f[:, l, 0:8], ALU.add), reads=[B_scr], writes=[B_scr])
            S.op("dve", I_tt(lbt[:], num, den, ALU.mult), reads=[B_scr], writes=[B_const])
            S.op("dve", I_ts(omlbt[:], lbt[:], -1.0, 1.0, ALU.mult, ALU.add), reads=[B_const], writes=[B_const])
            S.op("dve", I_memset(S_st[:], 0.0), writes=[B_S])
        if NA_ > 0:
            S.op("dve", I_cp(m2b[:], tmp_f2[:, 0:2, :].rearrange("p a b -> p (a b)")), reads=[B_const, B_scr],
                 writes=[B_const])
            S.op("dve", I_cp(m2fb[:], tmp_f2[:, 2:4, :].rearrange("p a b -> p (a b)")), reads=[B_const, B_scr],
                 writes=[B_const])
            S.op("dve", I_ts(bq8[:], bq8[:], 0.125, None, ALU.mult), reads=[B_const], writes=[B_const])
            S.op("dve", I_ts(bot[:], bot[:], float(1.0 / ALPHA), None, ALU.mult), reads=[B_const],
                 writes=[B_const])
            S.op("dve", I_memset(kT2[:], 0.0), writes=[B_kv])
            S.op("dve", I_memset(vtk[:], 0.0), writes=[B_kv])
        for gs in range(NG_):
            wsp_f = carve(8192, [P, 8, P], F32)
            wsp_b = carve(8192 + 4096, [P, 8, P], BF16)
            trl = carve(8192 + 4096 + 2048, [P, P], F32)
            S.dma("sp", [(wsp_f, gwsp_d[gs].rearrange("g t s -> t g s")), (trl, tril_d[:, :])], "const",
                  writes=[B_scr])
            S.op("dve", I_tt(wsp_b, wsp_f, bcast(trl, 1, 8), ALU.mult), reads=[B_scr], writes=[B_scr])
            fns = [I_tr(bankT[:, g, :], wsp_b[:, g, :], ident[:]) for g in range(8)]
            S.op("pe", fns, reads=[B_scr, B_ident], writes=[B_T])
            S.op("act", I_acopy(wmT[:, gs], bankT[:]), reads=[B_T], writes=[B_const])

        slab_plan = []
        ln_plan = []

        def issue_slab():
            n = state["slab_issue"]
            if n >= len(slab_plan):
                return
            slot = n % NSLAB
            S.dma("pool", [(slab_sb[slot][:].rearrange("p k c -> p (k c)"), slab_plan[n])], f"slab{slot}",
                  writes=[B_slab[slot]])
            state["slab_issue"] = n + 1

        def next_slab():
            n = state["slab_use"]
            state["slab_use"] = n + 1
            assert n < state["slab_issue"], "slab used before issued"
            return n % NSLAB

        def issue_ln():
            n = state["ln_issue"]
            if n >= len(ln_plan):
                return
            slot = n % 2
            r = ln_plan[n]
            S.dma("sp", [(lng_sb[slot][:], lng_d[r:r + 1, :].partition_broadcast(P)),
                         (lnb_sb[slot][:], lnb_d[r:r + 1, :].partition_broadcast(P))], f"lngb{slot}",
                  writes=[B_lngb[slot]])
            state["ln_issue"] = n + 1

        def next_ln():
            n = state["ln_use"]
            state["ln_use"] = n + 1
            return n % 2

        def issue_wd(src, nch):
            v = src.rearrange("(j p) n -> p j n", p=P)
            if nch > 8:
                hh = nch // 2
                pairs = [(wd_sb[:, 0:hh, :], v[:, 0:hh, :]), (wd_sb[:, hh:nch, :], v[:, hh:nch, :])]
            else:
                pairs = [(wd_sb[:, 0:nch, :], v)]
            S.dma("pool", pairs, "wd", writes=[B_wd])

        def emit_transpose_block(b, src_slot, dstT, dstB):
            fns = [I_tr(bankT[:, c, :], hbf[src_slot][:, c * P:(c + 1) * P], ident[:]) for c in range(KC)]
            S.op("pe", fns, reads=[B_hbf[src_slot], B_ident], writes=[B_T])
            S.op("act", I_acopy(dstT[:, :, b * P:(b + 1) * P], bankT[:]), reads=[B_T], writes=[dstB[b]])

        def ln_core(src, srcB, eps_col, ln_slot, out_f, out_fB, out_bf, out_bfB):
            k = state["sctr"] % 2
            state["sctr"] += 1
            for hf in range(2):
                S.op("dve", I_bnstats(stats[k][:, hf, :], src[:, hf * 512:(hf + 1) * 512]),
                     reads=[srcB], writes=[B_small[k]])
            S.op("dve", I_bnaggr(mv[k][:], stats[k][:].rearrange("p a s -> p (a s)")),
                 reads=[B_small[k]], writes=[B_small[k]])
            S.op("act", I_act(sd[k][:], mv[k][:, 1:2], AF.Sqrt, bias=epsb[:, eps_col:eps_col + 1], scale=1.0),
                 reads=[B_small[k]], writes=[B_rs[k]])
            S.op("dve", I_stt(src, src, mv[k][:, 0:1], lng_sb[ln_slot][:], ALU.subtract, ALU.mult),
                 reads=[B_small[k], B_lngb[ln_slot], srcB], writes=[srcB])
            S.op("dve", I_recip(rstd[k][:], sd[k][:]), reads=[B_rs[k]], writes=[B_rs[k]])
            S.op("dve", I_stt(out_bf, src, rstd[k][:], lnb_sb[ln_slot][:], ALU.mult, ALU.add),
                 reads=[B_rs[k], B_lngb[ln_slot], srcB], writes=[out_bfB])
            if out_f is not None:
                S.op("dve", I_stt(out_f, src, rstd[k][:], lnb_sb[ln_slot][:], ALU.mult, ALU.add),
                     reads=[B_rs[k], B_lngb[ln_slot], srcB], writes=[out_fB])

        def emit_ln_part1(b, banks, bbufs, coef, bias_tile=None):
            for hf in range(2):
                hs = h_tok[:, b, hf * 512:(hf + 1) * 512]
                S.op("dve", I_stt(hs, banks[hf][:], float(coef), hs, ALU.mult, ALU.add),
                     reads=[bbufs[hf]], writes=[B_h[b]])
            if bias_tile is not None:
                hb = h_tok[:, b, :]
                S.op("dve", I_tt(hb, hb, bias_tile, ALU.add), reads=[B_h[b], B_const], writes=[B_h[b]])

        def emit_ln_part2a(b, ln_slot):
            k = state["dctr"] % NHBF
            state["dctr"] += 1
            hb = h_tok[:, b, :]
            ln_core(hb, B_h[b], 0, ln_slot, hb, B_h[b], hbf[k][:], B_hbf[k])
            return k

        def emit_ln_part2b(b, k):
            emit_transpose_block(b, k, hT, B_hT)

        def emit_outproj(nb, nch, srcT, srcB, coef, ln_slot, bias_tile=None):
            pending = []
            for b in range(nb):
                banks, bbufs = [], []
                for hf in range(2):
                    d = state["dbank"] % 3
                    state["dbank"] += 1
                    fns = [I_mm(bankD[d][:], srcT[:, j, b * P:(b + 1) * P],
                                wd_sb[:, j, hf * 512:(hf + 1) * 512], j == 0, j == nch - 1)
                           for j in range(nch)]
                    S.op("pe", fns, reads=[B_wd] + srcB(b), writes=[B_D[d]])
                    banks.append(bankD[d])
                    bbufs.append(B_D[d])
                emit_ln_part1(b, banks, bbufs, coef, bias_tile)
                pending.append((b, emit_ln_part2a(b, ln_slot)))
                if len(pending) > LN_DELAY:
                    emit_ln_part2b(*pending.pop(0))
            while pending:
                emit_ln_part2b(*pending.pop(0))

        def emit_ffn(nb):
            tiles = token_tiles(nb)
            ln_slot = next_ln()
            for j in range(FC):
                slot = next_slab()
                for ti, (b0, nbt) in enumerate(tiles):
                    n = nbt * P
                    t0 = b0 * P
                    pa = state["actr"] % 2
                    state["actr"] += 1
                    bg, bu = bankA[2 * pa], bankA[2 * pa + 1]
                    fns = []
                    for kc in range(KC):
                        fns.append(I_mm(bg[:, :n], slab_sb[slot][:, kc, 0:P], hT[:, kc, t0:t0 + n],
                                        kc == 0, kc == KC - 1))
                    for kc in range(KC):
                        fns.append(I_mm(bu[:, :n], slab_sb[slot][:, kc, P:2 * P], hT[:, kc, t0:t0 + n],
                                        kc == 0, kc == KC - 1))
                    S.op("pe", fns, reads=[B_slab[slot]] + [B_hT[b] for b in range(b0, b0 + nbt)],
                         writes=[B_A[2 * pa], B_A[2 * pa + 1]])
                    S.op("act", I_act(silu_t[pa][:, :n], bg[:, :n], AF.Silu),
                         reads=[B_A[2 * pa]], writes=[B_silu[pa]])
                    wr = [bact(j, ti), B_A[2 * pa]]
                    if state["scr_dirty"]:
                        wr = wr + [B_scr] + all_act()
                        state["scr_dirty"] = False
                    S.op("dve", I_tt(act[:, j, t0:t0 + n], silu_t[pa][:, :n], bu[:, :n], ALU.mult),
                         reads=[B_silu[pa], B_A[2 * pa + 1]], writes=wr)
                issue_slab()

            def srcB(b):
                ti = [i for i, (b0, nbt) in enumerate(tiles) if b0 <= b < b0 + nbt][0]
                return [bact(j, ti) for j in range(FC)]
            emit_outproj(nb, FC, act, srcB, 0.5 / ALPHA, ln_slot)
            issue_ln()

        def emit_gmlp(gs, nb):
            state["scr_dirty"] = True
            T = nb * P
            tiles = token_tiles(nb)
            uT = carve(0, [P, KC, T], BF16)
            vtok = carve(2 * KC * TM, [P, NBM, D], BF16)
            vblk = [carve(4 * KC * TM + i * 4096, [P, D], F32) for i in range(2)]
            assert 4 * KC * TM + 8192 <= SCR * 2
            scr_deps = [B_scr] + all_act()
            ln_v = next_ln()
            S.dma("sp", [(bs_sb[:].rearrange("p g t -> p (g t)"), gbs_d[gs:gs + 1, :].partition_broadcast(P))],
                  "bs", writes=[B_bs])
            first = True
            for us in range(4):
                slot = next_slab()
                for ti, (b0, nbt) in enumerate(tiles):
                    n = nbt * P
                    t0 = b0 * P
                    for cc in range(2):
                        ch = us * 2 + cc
                        a = state["actr"] % 4
                        state["actr"] += 1
                        fns = [I_mm(bankA[a][:, :n], slab_sb[slot][:, kc, cc * P:(cc + 1) * P],
                                    hT[:, kc, t0:t0 + n], kc == 0, kc == KC - 1) for kc in range(KC)]
                        S.op("pe", fns, reads=[B_slab[slot]] + [B_hT[b] for b in range(b0, b0 + nbt)],
                             writes=[B_A[a]])
                        S.op("act", I_act(uT[:, ch, t0:t0 + n], bankA[a][:, :n], AF.Gelu), reads=[B_A[a]],
                             writes=(scr_deps if first else [B_scr]))
                        first = False
                issue_slab()
            vslots = [next_slab() for _ in range(4)]
            for b in range(nb):
                k = state["dctr"] % 2
                state["dctr"] += 1
                for vs in range(4):
                    a = state["actr"] % 4
                    state["actr"] += 1
                    fns = [I_mm(bankA[a][:, 0:SLABW], hT[:, kc, b * P:(b + 1) * P], slab_sb[vslots[vs]][:, kc, :],
                                kc == 0, kc == KC - 1) for kc in range(KC)]
                    S.op("pe", fns, reads=[B_slab[vslots[vs]], B_hT[b]], writes=[B_A[a]])
                    S.op("act", I_act(vblk[k][:, vs * SLABW:(vs + 1) * SLABW], bankA[a][:, 0:SLABW], AF.Gelu),
                         reads=[B_A[a]], writes=[B_vb[k]])
                ln_core(vblk[k], B_vb[k], 1, ln_v, None, None, vtok[:, b, :], B_scr)
            for _ in range(4):
                issue_slab()
            issue_ln()
            for b in range(nb):
                for hg in range(2):
                    a = state["actr"] % 4
                    state["actr"] += 1
                    fns = []
                    for g4 in range(4):
                        g = hg * 4 + g4
                        fns.append(I_mm(bankA[a][:, g4 * P:(g4 + 1) * P], vtok[:, b, g * P:(g + 1) * P],
                                        wmT[:, gs, g, :], True, True))
                    S.op("pe", fns, reads=[B_scr, B_const], writes=[B_A[a]])
                    tmpm = silu_t[a % 2]
                    S.op("dve", I_tt(tmpm[:], bankA[a][:], bs_sb[:, hg * 4:(hg + 1) * 4, :].rearrange("p g t -> p (g t)"),
                                     ALU.add), reads=[B_A[a], B_bs], writes=[B_silu[a % 2]])
                    uv = uT[:, hg * 4:(hg + 1) * 4, b * P:(b + 1) * P]
                    S.op("dve", I_tt(uv, tmpm[:].rearrange("p (g t) -> p g t", t=P), uv, ALU.mult),
                         reads=[B_silu[a % 2], B_scr], writes=[B_scr])
            ln_slot = next_ln()
            emit_outproj(nb, KC, uT, lambda b: [B_scr], 1.0 / ALPHA, ln_slot)
            issue_ln()

        def emit_hgrn(hs, nb, is_first_pass):
            state["scr_dirty"] = True
            T = nb * P
            NCH = T // 64
            tiles = token_tiles(nb)
            FB = 4 * TM
            qs = carve(0 * FB, [P, TM], F32)
            fv = carve(1 * FB, [P, TM], F32)
            lf = carve(2 * FB, [P, TM], F32)
            gc = carve(3 * FB, [P, TM], F32)
            eg = carve(4 * FB, [P, TM], F32)
            gsT = carve(5 * FB, [P, TM], F32)
            o_raw = carve(0, [P, NCHM, P], F32)
            sq_t = carve(4 * FB, [P, NCHM // 2, P], F32)
            o0 = 6 * FB
            qdT = carve(o0, [P, TM], BF16)
            kdT = carve(o0 + 2 * TM, [P, TM], BF16)
            kdecT = carve(o0 + 4 * TM, [P, TM], BF16)
            iT_bf = carve(o0 + 6 * TM, [P, TM], BF16)
            o1 = o0 + 8 * TM
            CB = NCHM * P * 2
            kdec64 = carve(o1, [P, NCHM, P], BF16)
            i64 = carve(o1 + CB, [P, NCHM, P], BF16)
            at_bf = carve(o1 + 2 * CB, [P, NCHM, 64], BF16)
            assert o1 + 2 * CB + NCHM * 64 * 2 <= SCR * 2
            ss = ss_t
            hh2 = NCH // 2
            for hd in range(8):
                slot = next_slab()
                slot2 = next_slab()
                for ti, (b0, nbt) in enumerate(tiles):
                    n = nbt * P
                    t0 = b0 * P
                    for (sl_, which) in ((slot, 0), (slot2, 1)):
                        pa = state["actr"] % 2
                        state["actr"] += 1
                        b0_, b1_ = bankA[2 * pa], bankA[2 * pa + 1]
                        fns = []
                        for kc in range(KC):
                            fns.append(I_mm(b0_[:, :n], slab_sb[sl_][:, kc, 0:P], hT[:, kc, t0:t0 + n],
                                            kc == 0, kc == KC - 1))
                        for kc in range(KC):
                            fns.append(I_mm(b1_[:, :n], slab_sb[sl_][:, kc, P:2 * P], hT[:, kc, t0:t0 + n],
                                            kc == 0, kc == KC - 1))
                        S.op("pe", fns, reads=[B_slab[sl_]] + [B_hT[b] for b in range(b0, b0 + nbt)],
                             writes=[B_A[2 * pa], B_A[2 * pa + 1]])
                        if which == 0:
                            S.op("act", I_act(qs[:, t0:t0 + n], b0_[:, :n], AF.Silu), reads=[B_A[2 * pa]],
                                 writes=[B_el])
                            S.op("act", I_act(fv[:, t0:t0 + n], b1_[:, :n], AF.Sigmoid), reads=[B_A[2 * pa + 1]],
                                 writes=[B_el])
                        else:
                            S.op("act", I_acopy(iT_bf[:, t0:t0 + n], b0_[:, :n]), reads=[B_A[2 * pa]],
                                 writes=[B_iT])
                            S.op("act", I_act(gsT[:, t0:t0 + n], b1_[:, :n], AF.Silu), reads=[B_A[2 * pa + 1]],
                                 writes=[B_gs])
                issue_slab()
                issue_slab()
                for half in range(2):
                    c0, c1 = (0, hh2) if half == 0 else (hh2, NCH)
                    fns = [I_tr(bankT[0:64, c - c0, :], iT_bf[:, c * 64:(c + 1) * 64], ident[:]) for c in range(c0, c1)]
                    S.op("pe", fns, reads=[B_iT, B_ident], writes=[B_T])
                    S.op("act", I_acopy(i64[0:64, c0:c1, :], bankT[0:64, 0:c1 - c0, :]), reads=[B_T],
                         writes=[B_ig])
                S.op("dve", I_ts(fv[:, :T], fv[:, :T], omlbt[:, hd:hd + 1], lbt[:, hd:hd + 1], ALU.mult, ALU.add),
                     reads=[B_el, B_const], writes=[B_el])
                S.op("act", I_act(lf[:, :T], fv[:, :T], AF.Ln), reads=[B_el], writes=[B_el])
                S.op("dve", I_ts(fv[:, :T], fv[:, :T], -1.0, 1.0, ALU.mult, ALU.add), reads=[B_el], writes=[B_el])
                S.op("dve", lambda h, o=gc[:, :T], d0=scanm[:, :T], d1=lf[:, :T]: h.tensor_tensor_scan(
                    o, d0, d1, 0.0, ALU.mult, ALU.add), reads=[B_el, B_const], writes=[B_el])
                S.op("act", I_act(eg[:, :T], gc[:, :T], AF.Exp), reads=[B_el], writes=[B_el])
                S.op("act", I_act(lf[:, :T], gc[:, :T], AF.Exp, scale=-1.0), reads=[B_el], writes=[B_el])
                S.op("dve", I_tt(qdT[:, :T], qs[:, :T], eg[:, :T], ALU.mult), reads=[B_el], writes=[B_el])
                S.op("dve", I_tt(fv[:, :T], fv[:, :T], lf[:, :T], ALU.mult), reads=[B_el], writes=[B_el])
                S.op("dve", I_cp(kdT[:, :T], fv[:, :T]), reads=[B_el], writes=[B_el])
                egl = eg[:, :T].rearrange("p (c s) -> p c s", s=64)[:, :, 63:64]
                egl_b = bass.AP(egl.tensor, egl.offset, [list(egl.ap[0]), list(egl.ap[1]), [0, 64]])
                S.op("dve", I_tt(kdecT[:, :T].rearrange("p (c s) -> p c s", s=64),
                                 fv[:, :T].rearrange("p (c s) -> p c s", s=64), egl_b, ALU.mult),
                     reads=[B_el], writes=[B_el])
                S.op("dve", I_cp(ss[:, 0:NCH], egl.rearrange("p c o -> p (c o)")), reads=[B_el], writes=[B_ss])
                for half in range(2):
                    c0, c1 = (0, hh2) if half == 0 else (hh2, NCH)
                    fns = [I_tr(bankT[0:64, c - c0, :], kdecT[:, c * 64:(c + 1) * 64], ident[:]) for c in range(c0, c1)]
                    S.op("pe", fns, reads=[B_el, B_ident], writes=[B_T])
                    S.op("act", I_acopy(kdec64[0:64, c0:c1, :], bankT[0:64, 0:c1 - c0, :]), reads=[B_T],
                         writes=[B_kd])
                for half in range(2):
                    c0, c1 = (0, min(8, NCH)) if half == 0 else (8, NCH)
                    if c1 <= c0:
                        continue
                    a = state["actr"] % 4
                    state["actr"] += 1
                    fns = [I_mm(bankA[a][0:64, (c - c0) * 64:(c - c0 + 1) * 64], kdT[:, c * 64:(c + 1) * 64],
                                qdT[:, c * 64:(c + 1) * 64], True, True) for c in range(c0, c1)]
                    S.op("pe", fns, reads=[B_el], writes=[B_A[a]])
                    S.op("dve", I_tt(at_bf[0:64, c0:c1, :],
                                     bankA[a][0:64, 0:(c1 - c0) * 64].rearrange("p (c t) -> p c t", t=64),
                                     bcast(cm64[:], 1, c1 - c0), ALU.mult),
                         reads=[B_A[a], B_const], writes=[B_at])
                dS = {}
                for c in range(NCH):
                    d = c // 4
                    dS[c] = bankD[d][:, (c % 4) * P:(c % 4 + 1) * P]
                for d in range((NCH + 3) // 4):
                    cs = [c for c in range(NCH) if c // 4 == d]
                    fns = [I_mm(dS[c], kdec64[0:64, c, :], i64[0:64, c, :], True, True) for c in cs]
                    S.op("pe", fns, reads=[B_kd, B_ig], writes=[B_D[d]])
                cur_a = None
                for c in range(NCH):
                    sl = state["sctr"] % 4
                    state["sctr"] += 1
                    if is_first_pass and c == 2 * cfg.halo_blocks and cfg.halo_blocks > 0:
                        S.op("dve", I_ts(S_st[:, hd, :], S_st[:, hd, :], hflag[:, 0:1], None, ALU.mult),
                             reads=[B_S, B_const], writes=[B_S])
                    S.op("dve", I_cp(S_bf[:, sl, :], S_st[:, hd, :]), reads=[B_S], writes=[B_Sbf[sl]])
                    if c % 4 == 0:
                        cur_a = state["actr"] % 4
                        state["actr"] += 1
                    oc = bankA[cur_a][0:64, (c % 4) * P:(c % 4 + 1) * P]
                    fns = [I_mm(oc, at_bf[0:64, c, :], i64[0:64, c, :], True, False),
                           I_mm(oc, qdT[:, c * 64:(c + 1) * 64], S_bf[:, sl, :], False, True)]
                    S.op("pe", fns, reads=[B_at, B_ig, B_el, B_Sbf[sl]], writes=[B_A[cur_a]])
                    S.op("dve", I_stt(S_st[:, hd, :], S_st[:, hd, :], ss[:, c:c + 1], dS[c], ALU.mult, ALU.add),
                         reads=[B_S, B_ss, B_D[c // 4]], writes=[B_S])
                    if c % 4 == 3 or c == NCH - 1:
                        cb = c - (c % 4)
                        S.op("act", I_acopy(o_raw[0:64, cb:c + 1, :],
                                            bankA[cur_a][0:64, 0:(c - cb + 1) * P].rearrange("p (c v) -> p c v", v=P)),
                             reads=[B_A[cur_a]], writes=[B_el])
                for half in range(2):
                    c0, c1 = (0, hh2) if half == 0 else (hh2, NCH)
                    S.op("dve", I_tt(sq_t[0:64, 0:c1 - c0, :], o_raw[0:64, c0:c1, :], o_raw[0:64, c0:c1, :], ALU.mult),
                         reads=[B_el], writes=[B_el])
                    S.op("dve", lambda h, o=ss[0:64, c0:c1], i=sq_t[0:64, 0:c1 - c0, :]: h.tensor_reduce(
                        out=o, in_=i, axis=AX.X, op=ALU.add), reads=[B_el], writes=[B_ss])
                S.op("dve", I_ts(ss[0:64, 0:NCH], ss[0:64, 0:NCH], 1.0 / P, float(RMS_EPS), ALU.mult, ALU.add),
                     reads=[B_ss], writes=[B_ss])
                S.op("act", I_act(ss[0:64, 0:NCH], ss[0:64, 0:NCH], AF.Sqrt), reads=[B_ss], writes=[B_ss])
                S.op("dve", I_recip(ss[0:64, 0:NCH], ss[0:64, 0:NCH]), reads=[B_ss], writes=[B_ss])
                ssb = ss[0:64, 0:NCH]
                ss_b = bass.AP(ssb.tensor, ssb.offset, [list(ssb.ap[0]), list(ssb.ap[1]), [0, P]])
                orw = o_raw[0:64, 0:NCH, :]
                S.op("dve", I_tt(orw, orw, ss_b, ALU.mult), reads=[B_el, B_ss], writes=[B_el])
                pa = state["actr"] % 2
                state["actr"] += 1
                ba = [bankA[2 * pa], bankA[2 * pa + 1]]
                fns = [I_tr(ba[c // 8][:, (c % 8) * 64:(c % 8 + 1) * 64], o_raw[0:64, c, :], ident_f[0:64, 0:64])
                       for c in range(NCH)]
                S.op("pe", fns, reads=[B_el, B_ident], writes=[B_A[2 * pa], B_A[2 * pa + 1]])
                n0 = min(T, 512)
                S.op("dve", I_stt(mixT[:, hd, 0:n0], ba[0][:, 0:n0], ngcol[:, 0:1], gsT[:, 0:n0], ALU.mult, ALU.mult),
                     reads=[B_A[2 * pa], B_gs, B_const], writes=B_mixT[:nb])
                if T > 512:
                    S.op("dve", I_stt(mixT[:, hd, 512:T], ba[1][:, 0:T - 512], ngcol[:, 0:1], gsT[:, 512:T],
                                      ALU.mult, ALU.mult),
                         reads=[B_A[2 * pa + 1], B_gs, B_const], writes=B_mixT[:nb])
            ln_slot = next_ln()
            emit_outproj(nb, KC, mixT, lambda b: [B_mixT[b]], 1.0 / ALPHA, ln_slot)
            issue_ln()

        def emit_attn(as_, nb, is_first_pass):
            state["scr_dirty"] = True
            T = nb * P
            tiles = token_tiles(nb)
            qT = carve(0, [P, KC, TM], BF16)
            o0 = 2 * KC * TM
            e_bf = [carve(o0 + i * 512, [P, 256], BF16) for i in range(2)]
            eT = [carve(o0 + 1024 + i * 512, [P, 2, P], BF16) for i in range(2)]
            scr_deps = [B_scr] + all_act()
            first = True
            for qs_ in range(4):
                slot = next_slab()
                for ti, (b0, nbt) in enumerate(tiles):
                    n = nbt * P
                    t0 = b0 * P
                    for cc in range(2):
                        ch = qs_ * 2 + cc
                        a = state["actr"] % 4
                        state["actr"] += 1
                        fns = [I_mm(bankA[a][:, :n], slab_sb[slot][:, kc, cc * P:(cc + 1) * P],
                                    hT[:, kc, t0:t0 + n], kc == 0, kc == KC - 1) for kc in range(KC)]
                        S.op("pe", fns, reads=[B_slab[slot]] + [B_hT[b] for b in range(b0, b0 + nbt)],
                             writes=[B_A[a]])
                        S.op("act", I_act(qT[:, ch, t0:t0 + n], bankA[a][:, :n], AF.Identity,
                                          bias=bq8[:, ch:ch + 1], scale=0.125), reads=[B_A[a], B_const],
                             writes=(scr_deps if first else [B_scr]))
                        first = False
                issue_slab()
            slot = next_slab()
            for ti, (b0, nbt) in enumerate(tiles):
                n = nbt * P
                t0 = b0 * P
                for kvh in range(2):
                    a = state["actr"] % 4
                    state["actr"] += 1
                    fns = [I_mm(bankA[a][:, :n], slab_sb[slot][:, kc, kvh * P:(kvh + 1) * P],
                                hT[:, kc, t0:t0 + n], kc == 0, kc == KC - 1) for kc in range(KC)]
                    S.op("pe", fns, reads=[B_slab[slot]] + [B_hT[b] for b in range(b0, b0 + nbt)],
                         writes=[B_A[a]])
                    S.op("act", I_act(kT2[:, kvh, P + t0:P + t0 + n], bankA[a][:, :n], AF.Identity,
                                      bias=bk2[:, kvh:kvh + 1], scale=1.0), reads=[B_A[a], B_const],
                         writes=[B_kv])
            issue_slab()
            slot = next_slab()
            for b in range(nb):
                a = state["actr"] % 4
                state["actr"] += 1
                fns = [I_mm(bankA[a][:, 0:P], hT[:, kc, b * P:(b + 1) * P], slab_sb[slot][:, kc, 0:P],
                            kc == 0, kc == KC - 1) for kc in range(KC)]
                S.op("pe", fns, reads=[B_slab[slot], B_hT[b]], writes=[B_A[a]])
                S.op("dve", I_tt(vtk[:, b + 1, :], bankA[a][:, 0:P], bvt[:], ALU.add), reads=[B_A[a], B_const],
                     writes=[B_kv])
            issue_slab()
            def stage1(b, hq, mask):
                kvh = hq // 8
                pb = (hq % 2) * 64
                a = state["actr"] % 4
                state["actr"] += 1
                sm = att_small[hq % 2]
                fns = [I_mm(bankA[a][:, 0:256], qT[pb:pb + 64, hq // 2, b * P:(b + 1) * P],
                            kT2[pb:pb + 64, kvh, b * P:(b + 2) * P], True, False),
                       I_mm(bankA[a][:, 0:256], ident[:], mask[:], False, True)]
                S.op("pe", fns, reads=[B_scr, B_kv, B_const, B_ident], writes=[B_A[a]])
                bsm = B_small[hq % 2]
                S.op("dve", lambda h, o=sm[:, 0:1], i=bankA[a][:, 0:256]: h.tensor_reduce(
                    out=o, in_=i, axis=AX.X, op=ALU.max), reads=[B_A[a]], writes=[bsm])
                S.op("dve", I_tt(sm[:, 0:1], sm[:, 0:1], sinkt[:, hq:hq + 1], ALU.max), reads=[bsm, B_const],
                     writes=[bsm])
                S.op("dve", I_ts(sm[:, 1:2], sm[:, 0:1], -1.0, None, ALU.mult), reads=[bsm], writes=[bsm])
                ei = hq % 2
                S.op("act", I_act(e_bf[ei][:], bankA[a][:, 0:256], AF.Exp, bias=sm[:, 1:2], scale=1.0,
                                  accum_out=sm[:, 2:3]), reads=[B_A[a], bsm], writes=[B_e[ei], bsm, B_A[a]])
                S.op("act", I_act(sm[:, 3:4], sinkt[:, hq:hq + 1], AF.Exp, bias=sm[:, 1:2], scale=1.0),
                     reads=[bsm, B_const], writes=[bsm])
                S.op("dve", I_tt(sm[:, 4:5], sm[:, 2:3], sm[:, 3:4], ALU.add), reads=[bsm], writes=[bsm])
                S.op("dve", I_recip(sm[:, 5:6], sm[:, 4:5]), reads=[bsm], writes=[bsm])

            def stage2(b, hq, k, ob):
                kvh = hq // 8
                ei = hq % 2
                sm = att_small[hq % 2]
                bsm = B_small[hq % 2]
                fns = [I_tr(bankT[:, j, :], e_bf[ei][:, j * P:(j + 1) * P], ident[:]) for j in range(2)]
                S.op("pe", fns, reads=[B_e[ei], B_ident], writes=[B_T])
                S.op("act", I_acopy(eT[ei][:], bankT[:, 0:2, :]), reads=[B_T], writes=[B_eT[ei]])
                if hq % 8 == 0:
                    d = state["dbank"] % 3
                    state["dbank"] += 1
                    ob[hq // 8] = d
                d = ob[hq // 8]
                oc = bankD[d][:, (hq % 8) * 64:(hq % 8 + 1) * 64]
                fns = [I_mm(oc, eT[ei][:, 0, :], vtk[:, b, kvh * 64:(kvh + 1) * 64], True, False),
                       I_mm(oc, eT[ei][:, 1, :], vtk[:, b + 1, kvh * 64:(kvh + 1) * 64], False, True)]
                S.op("pe", fns, reads=[B_eT[ei], B_kv], writes=[B_D[d]])
                S.op("dve", I_ts(hbf[k][:, hq * 64:(hq + 1) * 64], oc, sm[:, 5:6], None, ALU.mult),
                     reads=[B_D[d], bsm], writes=[B_hbf[k], B_D[d]])

            for b in range(nb):
                k = state["dctr"] % 2
                state["dctr"] += 1
                use_first = is_first_pass and b == cfg.halo_blocks
                mask = m2fb if use_first else m2b
                ob = [None, None]
                stage1(b, 0, mask)
                for hq in range(16):
                    if hq + 1 < 16:
                        stage1(b, hq + 1, mask)
                    stage2(b, hq, k, ob)
                emit_transpose_block(b, k, mixT, B_mixT)
            S.op("dve", I_cp(kT2[:, :, 0:P], kT2[:, :, nb * P:(nb + 1) * P]), reads=[B_kv], writes=[B_kv])
            S.op("dve", I_cp(vtk[:, 0, :], vtk[:, nb, :]), reads=[B_kv], writes=[B_kv])
            ln_slot = next_ln()
            emit_outproj(nb, KC, mixT, lambda b: [B_mixT[b]], 1.0 / ALPHA, ln_slot, bias_tile=bot[:])
            issue_ln()

        subs = []
        for li in range(L):
            subs.append(("ffn", li, 0))
            subs.append(("mix", li, li % 3))
            subs.append(("ffn", li, 1))
        if cfg.sub_limit is not None:
            subs = subs[:cfg.sub_limit]
        npass = len(cfg.pass_blocks)
        wd_seq = []
        for _ in range(npass):
            for (kind, li, x_) in subs:
                if kind == "ffn":
                    for j in range(FC):
                        slab_plan.append(wgu_d[li * 2 + x_, j])
                    ln_plan.append(li * 3 + (0 if x_ == 0 else 2))
                    wd_seq.append((wd_d[li * 2 + x_], FC))
                else:
                    slot_ = li // 3
                    if x_ == 0:
                        for s_ in range(8):
                            slab_plan.append(gwin_d[slot_, s_])
                        ln_plan.append(L * 3 + slot_)
                        wd_seq.append((gwout_d[slot_], KC))
                    elif x_ == 1:
                        for s_ in range(16):
                            slab_plan.append(hwin_d[slot_, s_])
                        wd_seq.append((hwout_d[slot_], KC))
                    else:
                        for s_ in range(6):
                            slab_plan.append(awqkv_d[slot_, s_])
                        wd_seq.append((awo_d[slot_], KC))
                    ln_plan.append(li * 3 + 1)
        wd_ctr = [0]

        def issue_next_wd():
            if wd_ctr[0] < len(wd_seq):
                issue_wd(*wd_seq[wd_ctr[0]])
                wd_ctr[0] += 1

        for _ in range(NSLAB):
            issue_slab()
        issue_ln()
        issue_ln()
        issue_next_wd()

        out_evs = []
        blk0 = 0
        for pi, nb in enumerate(cfg.pass_blocks):
            src = x_d[blk0 * P:(blk0 + nb) * P, :].rearrange("(b p) d -> p b d", p=P)
            S.dma("sp", [(h_tok[:, 0:nb, :], src)], "xload", writes=B_h[:nb])
            for b in range(nb):
                k = state["dctr"] % 2
                state["dctr"] += 1
                S.op("act", I_acopy(hbf[k][:], h_tok[:, b, :]), reads=[B_h[b]], writes=[B_hbf[k]])
                emit_transpose_block(b, k, hT, B_hT)
            for (kind, li, x_) in subs:
                if kind == "ffn":
                    emit_ffn(nb)
                elif x_ == 0:
                    emit_gmlp(li // 3, nb)
                elif x_ == 1:
                    emit_hgrn(li // 3, nb, pi == 0)
                else:
                    emit_attn(li // 3, nb, pi == 0)
                issue_next_wd()
            pairs = []
            for b in range(nb):
                gb = blk0 + b
                if gb < cfg.halo_blocks:
                    continue
                ob_ = gb - cfg.halo_blocks
                pairs.append((out_d[ob_ * P:(ob_ + 1) * P, :], h_tok[:, b, :]))
            if pairs:
                ev = S.dma("sp", pairs, "store", reads=B_h[:nb])
                out_evs.append(ev)
            blk0 += nb
        S.wait_all("sp", out_evs)

        with nc.Block() as block:
            @block.tensor
            def _(h):
                S.replay("pe", h)

            @block.scalar
            def _(h):
                S.replay("act", h)

            @block.vector
            def _(h):
                S.replay("dve", h)

            @block.gpsimd
            def _(h):
                S.replay("pool", h)

            @block.sync
            def _(h):
                S.replay("sp", h)
    return nc


def make_in_maps(cfg, x_cores, inputs, L, flags=None):
    f32 = np.float32
    ncore = len(x_cores)
    if flags is None:
        flags = [1.0] * ncore
    NBM = max(cfg.pass_blocks)
    TM = NBM * P
    NG_, NH_, NA_ = n_gmlp(L), n_hgrn(L), n_attn(L)
    wgu = inputs["ffn_w_gate_up"]
    wd = inputs["ffn_w_down"]
    common = {
        "wgu": np.stack([slab_layout(gate_up_interleave(wgu[li, fi])) for li in range(L) for fi in range(2)]),
        "wd": np.ascontiguousarray(wd[:L].reshape(L * 2, DFF, D)),
        "lng": np.ascontiguousarray(np.concatenate(
            [inputs["ln_gain"][:L].reshape(L * 3, D), inputs["gmlp_ln_gain"][:2].reshape(-1, D)], 0)[:L * 3 + 2]),
        "lnb": np.ascontiguousarray(np.concatenate(
            [inputs["ln_bias"][:L].reshape(L * 3, D), inputs["gmlp_ln_bias"][:2].reshape(-1, D)], 0)[:L * 3 + 2]),
        "ident": np.eye(P, dtype=f32),
        "tril": np.tril(np.ones((P, P), f32)),
        "cm64": np.triu(np.ones((64, 64), f32)),
        "scanm": np.tile((np.arange(TM) % 64 != 0).astype(f32)[None, :], (P, 1)),
        "hlb": np.ascontiguousarray(inputs["hgrn_lb_logits"].reshape(DEPTH, 8, P).transpose(2, 0, 1)),
    }
    ng = max(NG_, 1)
    common["gwin"] = np.stack([slab_layout(inputs["gmlp_w_in"][s]) for s in range(ng)])
    common["gwout"] = np.ascontiguousarray(inputs["gmlp_w_out"][:ng])
    common["gwsp"] = np.ascontiguousarray(inputs["gmlp_w_spatial"][:ng])
    common["gbs"] = np.ascontiguousarray(inputs["gmlp_b_spatial"][:ng].reshape(ng, 8 * P))
    hw = inputs["hgrn_w_in"][0]
    q_, f_, i_, g_ = [hw[:, j * D:(j + 1) * D].reshape(D, 8, P) for j in range(4)]
    hperm = np.concatenate([np.concatenate([q_[:, h], f_[:, h], i_[:, h], g_[:, h]], axis=1) for h in range(8)], axis=1)
    common["hwin"] = slab_layout(hperm)[None]
    common["hwout"] = np.ascontiguousarray(inputs["hgrn_w_out"][:1])
    common["hng"] = np.ascontiguousarray(inputs["hgrn_norm_gain"][:1])
    common["hngc"] = np.ascontiguousarray(inputs["hgrn_norm_gain"][0].reshape(P, 1))
    aw = inputs["attn_w_qkv"][0]
    ab = inputs["attn_b_qkv"][0]
    k0, k1 = aw[:, 1024:1088], aw[:, 1088:1152]
    awp = np.concatenate([aw[:, :1024], k0, k0, k1, k1, aw[:, 1152:1280], np.zeros((D, 128), f32)], axis=1)
    common["awqkv"] = slab_layout(awp)[None]
    common["abq"] = np.ascontiguousarray(ab[:1024].reshape(8, P).T)[None]
    bk0, bk1 = ab[1024:1088], ab[1088:1152]
    common["abk"] = np.ascontiguousarray(np.stack([np.concatenate([bk0, bk0]), np.concatenate([bk1, bk1])], 1))[None]
    common["abv"] = np.ascontiguousarray(ab[1152:1280])[None]
    common["asink"] = np.ascontiguousarray(inputs["attn_sinks"][:1])
    common["awo"] = np.ascontiguousarray(inputs["attn_w_o"][:1])
    common["abo"] = np.ascontiguousarray(inputs["attn_b_o"][:1])
    NEG = -30000.0
    qi = np.arange(P)[:, None]
    kj = np.arange(P)[None, :]
    m_prev = np.where(kj > qi, 0.0, NEG).astype(f32)
    m_cur = np.where(kj <= qi, 0.0, NEG).astype(f32)
    common["m2"] = np.concatenate([m_prev, m_cur], 1)
    maps = []
    for c in range(ncore):
        m = dict(common)
        m["x"] = np.ascontiguousarray(x_cores[c], dtype=f32)
        m["hflag"] = np.full((P, 1), flags[c], f32)
        if flags[c] > 0:
            m["m2f"] = common["m2"]
        else:
            m["m2f"] = np.concatenate([np.full((P, P), NEG, f32), m_cur], 1)
        maps.append(m)
    return maps


PASS_BLOCKS = [6, 6, 6, 6, 6, 4]


def kernel(**inputs):
    inputs = {k: np.asarray(v) for k, v in inputs.items()}
    x = inputs["x"]
    cfg = Cfg(PASS_BLOCKS, HALO // P)
    x_cores, flags = [], []
    for c in range(NCORES):
        b, p = divmod(c, 4)
        xc = np.zeros((cfg.ntok, D), np.float32)
        s = p * CHUNK - HALO
        if s < 0:
            xc[HALO:] = x[b, 0:CHUNK]
            flags.append(0.0)
        else:
            xc[:] = x[b, s:s + cfg.ntok]
            flags.append(1.0)
        x_cores.append(xc)
    nc = build_program(cfg)
    in_maps = make_in_maps(cfg, x_cores, inputs, DEPTH, flags)
    res = run_bass_kernel_spmd(nc, in_maps, core_ids=list(range(NCORES)))
    out = np.empty((BATCH, SEQ, D), np.float32)
    for c in range(NCORES):
        b, p = divmod(c, 4)
        out[b, p * CHUNK:(p + 1) * CHUNK] = res.results[c]["out"]
    return out
```

```python
import contextlib
import numpy as np
import concourse.bass as bass
import concourse.mybir as mybir
from concourse.bass_utils import run_bass_kernel_spmd

F32 = mybir.dt.float32
BF16 = mybir.dt.bfloat16
AF = mybir.ActivationFunctionType
ALU = mybir.AluOpType
AX = mybir.AxisListType

P = 128
D = 1024
KC = 8
DFF = 2816
FC = 22
DEPTH = 4
ALPHA = (2.0 * DEPTH) ** 0.25
LN_EPS = 1e-5
RMS_EPS = 1e-6
SEQ = 16384
BATCH = 2
NCORES = 8
CHUNK = SEQ // 4
HALO = 256
SLABW = 256
LN_DELAY = 1


class Buf:
    __slots__ = ("name", "w", "r")

    def __init__(self, name):
        self.name = name
        self.w = None
        self.r = []


class Sched:
    ENGS = ("pe", "act", "dve", "pool", "sp")

    def __init__(self, nc, stack):
        self.nc = nc
        self.stack = stack
        self.streams = {e: [] for e in self.ENGS}
        self.esem = {}
        self.ecnt = {}
        self.eepoch = {e: 0 for e in self.ENGS}
        self.seen = {e: {} for e in self.ENGS}
        self.semobj = {}
        self.nsem = 0
        for e in self.ENGS:
            self._new_epoch(e)
        self.dsem = {}
        self.dcnt = {}

    def _mksem(self, name):
        s = self.stack.enter_context(self.nc.semaphore(name))
        self.nsem += 1
        self.semobj[name] = s
        return name

    def _new_epoch(self, e):
        self.eepoch[e] += 1
        self.esem[e] = self._mksem(f"e_{e}_{self.eepoch[e]}")
        self.ecnt[e] = 0

    def _waits(self, eng, deps):
        best = {}
        for d in deps:
            if d is None:
                continue
            s, v = d
            if v > best.get(s, 0):
                best[s] = v
        out = []
        seen = self.seen[eng]
        for s, v in best.items():
            if seen.get(s, 0) >= v:
                continue
            seen[s] = v
            out.append((s, v))
        return out

    def _deps(self, reads, writes):
        deps = []
        for b in reads:
            deps.append(b.w)
        for b in writes:
            deps.append(b.w)
            deps.extend(b.r)
        return deps

    def op(self, eng, fns, reads=(), writes=()):
        if callable(fns):
            fns = [fns]
        if self.ecnt[eng] > 30000:
            self._new_epoch(eng)
        st = self.streams[eng]
        for s, v in self._waits(eng, self._deps(reads, writes)):
            st.append(("w", s, v))
        for f in fns[:-1]:
            st.append(("i", f, None))
        self.ecnt[eng] += 1
        ev = (self.esem[eng], self.ecnt[eng])
        st.append(("i", fns[-1], ev))
        for b in reads:
            b.r.append(ev)
        for b in writes:
            b.w = ev
            b.r = []
        return ev

    def dma(self, eng, pairs, semkey, reads=(), writes=()):
        if semkey not in self.dsem:
            self.dsem[semkey] = self._mksem(f"d_{semkey}")
            self.dcnt[semkey] = 0
        st = self.streams[eng]
        for s, v in self._waits(eng, self._deps(reads, writes)):
            st.append(("w", s, v))
        s = self.dsem[semkey]
        for (o, i) in pairs:
            self.dcnt[semkey] += 16
            st.append(("d", (o, i), s))
        ev = (s, self.dcnt[semkey])
        for b in reads:
            b.r.append(ev)
        for b in writes:
            b.w = ev
            b.r = []
        return ev

    def wait_all(self, eng, evs):
        st = self.streams[eng]
        for s, v in self._waits(eng, evs):
            st.append(("w", s, v))

    def replay(self, eng, h):
        so = self.semobj
        for kind, a, b in self.streams[eng]:
            if kind == "w":
                h.wait_ge(so[a], b)
            elif kind == "i":
                ins = a(h)
                if b is not None:
                    ins.then_inc(so[b[0]], 1)
            else:
                o, i = a
                h.dma_start(out=o, in_=i).then_inc(so[b], 16)


def I_mm(out, lhsT, rhs, start, stop):
    return lambda h: h.matmul(out, lhsT, rhs, start=start, stop=stop)


def I_tr(out, in_, ident):
    return lambda h: h.transpose(out, in_, ident)


def I_act(out, in_, func, **kw):
    return lambda h: h.activation(out=out, in_=in_, func=func, **kw)


def I_acopy(out, in_):
    return lambda h: h.copy(out, in_)


def I_tt(out, in0, in1, op):
    return lambda h: h.tensor_tensor(out=out, in0=in0, in1=in1, op=op)


def I_ts(out, in0, s1, s2, op0, op1=None):
    if op1 is None:
        return lambda h: h.tensor_scalar(out=out, in0=in0, scalar1=s1, scalar2=None, op0=op0)
    return lambda h: h.tensor_scalar(out=out, in0=in0, scalar1=s1, scalar2=s2, op0=op0, op1=op1)


def I_stt(out, in0, scalar, in1, op0, op1):
    return lambda h: h.scalar_tensor_tensor(out=out, in0=in0, scalar=scalar, in1=in1, op0=op0, op1=op1)


def I_cp(out, in_):
    return lambda h: h.tensor_copy(out, in_)


def I_bnstats(out, in_):
    return lambda h: h.bn_stats(out, in_)


def I_bnaggr(out, in_):
    return lambda h: h.bn_aggr(out, in_)


def I_recip(out, in_):
    return lambda h: h.reciprocal(out, in_)


def I_memset(ap, v):
    return lambda h: h.memset(ap, v)

def slab_layout(w):
    k, n = w.shape
    assert k == D and n % SLABW == 0
    return np.ascontiguousarray(
        w.reshape(KC, P, n // SLABW, SLABW).transpose(2, 1, 0, 3).reshape(n // SLABW, P, KC * SLABW))


def gate_up_interleave(w):
    g = w[:, :DFF].reshape(D, FC, P)
    u = w[:, DFF:].reshape(D, FC, P)
    return np.concatenate([g, u], axis=2).reshape(D, 2 * DFF)


class Cfg:
    def __init__(self, pass_blocks, halo_blocks, n_layers=DEPTH, sub_limit=None):
        self.pass_blocks = list(pass_blocks)
        self.halo_blocks = halo_blocks
        self.n_layers = n_layers
        self.sub_limit = sub_limit
        self.nblk = sum(pass_blocks)
        self.ntok = self.nblk * P
        self.nout = (self.nblk - halo_blocks) * P


def token_tiles(nb):
    nt = (nb + 3) // 4
    base, rem = divmod(nb, nt)
    out, s = [], 0
    for i in range(nt):
        n = base + (1 if i < rem else 0)
        out.append((s, n))
        s += n
    return out


def bcast(ap, axis, n):
    dims = [list(d) for d in ap.ap]
    dims.insert(axis, [0, n])
    return bass.AP(ap.tensor, ap.offset, dims)


def n_gmlp(L):
    return (L + 2) // 3


def n_hgrn(L):
    return (L + 1) // 3


def n_attn(L):
    return L // 3


def build_program(cfg):
    nc = bass.Bass("TRN2", target_bir_lowering=False)
    NBM = max(cfg.pass_blocks)
    TM = NBM * P
    NCHM = TM // 64
    L = cfg.n_layers
    NG_, NH_, NA_ = n_gmlp(L), n_hgrn(L), n_attn(L)

    def din(name, shape, dt=F32):
        return nc.dram_tensor(name, list(shape), dt, kind="ExternalInput").ap()

    x_d = din("x", [cfg.ntok, D])
    wgu_d = din("wgu", [L * 2, FC, P, KC * SLABW])
    wd_d = din("wd", [L * 2, DFF, D])
    lng_d = din("lng", [L * 3 + 2, D])
    lnb_d = din("lnb", [L * 3 + 2, D])
    ident_d = din("ident", [P, P])
    gwin_d = din("gwin", [max(NG_, 1), 8, P, KC * SLABW])
    gwout_d = din("gwout", [max(NG_, 1), D, D])
    gwsp_d = din("gwsp", [max(NG_, 1), 8, P, P])
    gbs_d = din("gbs", [max(NG_, 1), 8 * P])
    tril_d = din("tril", [P, P])
    hwin_d = din("hwin", [max(NH_, 1), 16, P, KC * SLABW])
    hwout_d = din("hwout", [max(NH_, 1), D, D])
    hlb_d = din("hlb", [P, DEPTH, 8])
    hng_d = din("hng", [max(NH_, 1), P])
    hngc_d = din("hngc", [P, 1])
    cm64_d = din("cm64", [64, 64])
    scanm_d = din("scanm", [P, TM])
    hflag_d = din("hflag", [P, 1])
    awqkv_d = din("awqkv", [max(NA_, 1), 6, P, KC * SLABW])
    abq_d = din("abq", [max(NA_, 1), P, 8])
    abk_d = din("abk", [max(NA_, 1), P, 2])
    abv_d = din("abv", [max(NA_, 1), P])
    asink_d = din("asink", [max(NA_, 1), 16])
    awo_d = din("awo", [max(NA_, 1), D, D])
    abo_d = din("abo", [max(NA_, 1), D])
    m2_d = din("m2", [P, 256])
    m2f_d = din("m2f", [P, 256])
    out_d = nc.dram_tensor("out", [cfg.nout, D], F32, kind="ExternalOutput").ap()

    stack = contextlib.ExitStack()
    with stack:
        def sb(name, shape, dt):
            return stack.enter_context(nc.sbuf_tensor(name, list(shape), dt))

        def ps(name, shape, dt):
            return stack.enter_context(nc.psum_tensor(name, list(shape), dt))

        h_tok = sb("h_tok", [P, NBM, D], F32)
        hT = sb("hT", [P, KC, TM], BF16)
        mixT = sb("mixT", [P, KC, TM], BF16)
        act = sb("act", [P, FC, TM], BF16)
        SCR = FC * TM
        wd_sb = sb("wd_sb", [P, FC, D], BF16)
        NSLAB = 5
        slab_sb = [sb(f"slab{i}", [P, KC, SLABW], BF16) for i in range(NSLAB)]
        lng_sb = [sb(f"lng{i}", [P, D], F32) for i in range(2)]
        lnb_sb = [sb(f"lnb{i}", [P, D], F32) for i in range(2)]
        NHBF = 3
        hbf = [sb(f"hbf{i}", [P, D], BF16) for i in range(NHBF)]
        silu_t = [sb(f"silu{i}", [P, 512], F32) for i in range(2)]
        ident_f = sb("ident_f32", [P, P], F32)
        ident = sb("ident_bf", [P, P], BF16)
        stats = [sb(f"stats{i}", [P, 2, 6], F32) for i in range(2)]
        mv = [sb(f"mv{i}", [P, 2], F32) for i in range(2)]
        sd = [sb(f"sd{i}", [P, 1], F32) for i in range(2)]
        rstd = [sb(f"rstd{i}", [P, 1], F32) for i in range(2)]
        epsb = sb("epsb", [P, 2], F32)
        wmT = sb("wmT", [P, max(NG_, 1), 8, P], BF16)
        bs_sb = sb("bs_sb", [P, 8, P], F32)
        S_st = sb("S_st", [P, 8, P], F32)
        S_bf = sb("S_bf", [P, 4, P], BF16)
        scanm = sb("scanm_sb", [P, TM], F32)
        lbt = sb("lbt", [P, 8], F32)
        omlbt = sb("omlbt", [P, 8], F32)
        ngt = sb("ngt", [P, P], F32)
        cm64 = sb("cm64_sb", [64, 64], F32)
        hflag = sb("hflag_sb", [P, 1], F32)
        ss_t = sb("ss_t", [P, 16], F32)
        ngcol = sb("ngcol", [P, 1], F32)
        kT2 = sb("kT2", [P, 2, (NBM + 1) * P], BF16)
        vtk = sb("vtk", [P, NBM + 1, P], BF16)
        m2b = sb("m2b", [P, 256], BF16)
        m2fb = sb("m2fb", [P, 256], BF16)
        bq8 = sb("bq8", [P, 8], F32)
        bk2 = sb("bk2", [P, 2], F32)
        bvt = sb("bvt", [P, P], F32)
        sinkt = sb("sinkt", [P, 16], F32)
        bot = sb("bot", [P, D], F32)
        att_small = [sb(f"atts{i}", [P, 16], F32) for i in range(2)]

        bankA = [ps(f"bankA{i}", [P, 512], F32) for i in range(4)]
        bankD = [ps(f"bankD{i}", [P, 512], F32) for i in range(3)]
        bankT = ps("bankT", [P, KC, P], BF16)

        S = Sched(nc, stack)

        def carve(off, shape, dt):
            flat = act[:].rearrange("p c t -> p (c t)")
            n = 1
            for s_ in shape[1:]:
                n *= s_
            if dt == F32:
                assert off % 4 == 0
                v = flat[:, off // 2: off // 2 + 2 * n].bitcast(F32)
                nbytes = 4 * n
            else:
                v = flat[:, off // 2: off // 2 + n]
                nbytes = 2 * n
            assert off + nbytes <= SCR * 2, (off, nbytes, SCR * 2)
            if len(shape) == 3:
                v = v.rearrange("p (a b) -> p a b", b=shape[2])
            elif len(shape) == 4:
                v = v.rearrange("p (a b c) -> p a b c", b=shape[2], c=shape[3])
            return v

        B_h = [Buf(f"h{b}") for b in range(NBM)]
        B_hT = [Buf(f"hT{b}") for b in range(NBM)]
        B_mixT = [Buf(f"mixT{b}") for b in range(NBM)]
        B_scr = Buf("scratch")
        B_act = {}
        B_wd = Buf("wd")
        B_slab = [Buf(f"slab{i}") for i in range(NSLAB)]
        B_lngb = [Buf(f"lngb{i}") for i in range(2)]
        B_hbf = [Buf(f"hbf{i}") for i in range(NHBF)]
        B_silu = [Buf(f"silu{i}") for i in range(2)]
        B_A = [Buf(f"A{i}") for i in range(4)]
        B_D = [Buf(f"D{i}") for i in range(3)]
        B_T = Buf("T")
        B_ident = Buf("ident")
        B_small = [Buf(f"small{i}") for i in range(2)]
        B_rs = [Buf(f"rs{i}") for i in range(2)]
        B_const = Buf("const")
        B_bs = Buf("bs")
        B_S = Buf("S")
        B_Sbf = [Buf(f"Sbf{i}") for i in range(4)]
        B_kv = Buf("kv")
        B_vb = [Buf("vb0"), Buf("vb1")]
        B_el, B_ig, B_kd, B_at, B_ss = Buf("el"), Buf("ig"), Buf("kd"), Buf("at"), Buf("ss")
        B_iT, B_gs = Buf("iT"), Buf("gs")
        B_e = [Buf("e0"), Buf("e1")]
        B_eT = [Buf("eT0"), Buf("eT1")]

        def bact(j, t):
            k = (j, t)
            if k not in B_act:
                B_act[k] = Buf(f"act{k}")
            return B_act[k]

        def all_act():
            return list(B_act.values())

        state = {"slab_issue": 0, "slab_use": 0, "dctr": 0, "actr": 0, "ln_issue": 0, "ln_use": 0,
                 "dbank": 0, "sctr": 0, "scr_dirty": True}

        S.dma("sp", [(ident_f[:], ident_d[:, :])], "ident", writes=[B_ident])
        S.op("dve", I_cp(ident[:], ident_f[:]), reads=[B_ident], writes=[B_ident])
        S.op("dve", I_memset(epsb[:, 0:1], float(LN_EPS / ALPHA ** 2)), writes=[B_small[0], B_small[1]])
        S.op("dve", I_memset(epsb[:, 1:2], float(LN_EPS)), writes=[B_small[0], B_small[1]])
        cpairs = [(scanm[:], scanm_d[:, :]), (cm64[:], cm64_d[:, :]), (hflag[:], hflag_d[:, :])]
        tmp_f = carve(0, [P, 8, P], F32)
        tmp_f2 = carve(4096, [P, 8, P], F32)
        if NH_ > 0:
            cpairs.append((ngt[:], hng_d[0:1, :].partition_broadcast(P)))
            cpairs.append((ngcol[:], hngc_d[:, :]))
            cpairs.append((tmp_f[:, 0:DEPTH, 0:8], hlb_d[:, :, :]))
        if NA_ > 0:
            cpairs += [(bq8[:], abq_d[0]), (bk2[:], abk_d[0]),
                       (bvt[:], abv_d[0:1, :].partition_broadcast(P)),
                       (sinkt[:], asink_d[0:1, :].partition_broadcast(P)),
                       (bot[:], abo_d[0:1, :].partition_broadcast(P)),
                       (tmp_f2[:, 0, :], m2_d[:, 0:128]), (tmp_f2[:, 1, :], m2_d[:, 128:256]),
                       (tmp_f2[:, 2, :], m2f_d[:, 0:128]), (tmp_f2[:, 3, :], m2f_d[:, 128:256])]
        S.dma("sp", cpairs, "const", writes=[B_const, B_scr])
        if NH_ > 0:
            hl = 1
            e4 = tmp_f[:, 0:DEPTH, 0:8]
            mx = tmp_f[:, 4, 0:8]
            S.op("dve", I_tt(mx, tmp_f[:, 0, 0:8], tmp_f[:, 1, 0:8], ALU.max), reads=[B_const, B_scr],
                 writes=[B_scr])
            for l in range(2, DEPTH):
                S.op("dve", I_tt(mx, mx, tmp_f[:, l, 0:8], ALU.max), reads=[B_scr], writes=[B_scr])
            for l in range(DEPTH):
                S.op("dve", I_tt(tmp_f[:, l, 0:8], tmp_f[:, l, 0:8], mx, ALU.subtract), reads=[B_scr],
                     writes=[B_scr])
            for l in range(DEPTH):
                S.op("act", I_act(tmp_f[:, l, 0:8], tmp_f[:, l, 0:8], AF.Exp), reads=[B_scr], writes=[B_scr])
            den = tmp_f[:, 5, 0:8]
            S.op("dve", I_tt(den, tmp_f[:, 0, 0:8], tmp_f[:, 1, 0:8], ALU.add), reads=[B_scr], writes=[B_scr])
            for l in range(2, DEPTH):
                S.op("dve", I_tt(den, den, tmp_f[:, l, 0:8], ALU.add), reads=[B_scr], writes=[B_scr])
            S.op("dve", I_recip(den, den), reads=[B_scr], writes=[B_scr])
            num = tmp_f[:, 6, 0:8]
            S.op("dve", I_cp(num, tmp_f[:, 1, 0:8]), reads=[B_scr], writes=[B_scr])
            for l in range(2, hl + 1):
                S.op("dve", I_tt(num, num, tmp_f[:, l, 0:8], ALU.add), reads=[B_scr], writes=[B_scr])
            S.op("dve", I_tt(lbt[:], num, den, ALU.mult), reads=[B_scr], writes=[B_const])
            S.op("dve", I_ts(omlbt[:], lbt[:], -1.0, 1.0, ALU.mult, ALU.add), reads=[B_const], writes=[B_const])
            S.op("dve", I_memset(S_st[:], 0.0), writes=[B_S])
        if NA_ > 0:
            S.op("dve", I_cp(m2b[:], tmp_f2[:, 0:2, :].rearrange("p a b -> p (a b)")), reads=[B_const, B_scr],
                 writes=[B_const])
            S.op("dve", I_cp(m2fb[:], tmp_f2[:, 2:4, :].rearrange("p a b -> p (a b)")), reads=[B_const, B_scr],
                 writes=[B_const])
            S.op("dve", I_ts(bq8[:], bq8[:], 0.125, None, ALU.mult), reads=[B_const], writes=[B_const])
            S.op("dve", I_ts(bot[:], bot[:], float(1.0 / ALPHA), None, ALU.mult), reads=[B_const],
                 writes=[B_const])
            S.op("dve", I_memset(kT2[:], 0.0), writes=[B_kv])
            S.op("dve", I_memset(vtk[:], 0.0), writes=[B_kv])
        for gs in range(NG_):
            wsp_f = carve(8192, [P, 8, P], F32)
            wsp_b = carve(8192 + 4096, [P, 8, P], BF16)
            trl = carve(8192 + 4096 + 2048, [P, P], F32)
            S.dma("sp", [(wsp_f, gwsp_d[gs].rearrange("g t s -> t g s")), (trl, tril_d[:, :])], "const",
                  writes=[B_scr])
            S.op("dve", I_tt(wsp_b, wsp_f, bcast(trl, 1, 8), ALU.mult), reads=[B_scr], writes=[B_scr])
            fns = [I_tr(bankT[:, g, :], wsp_b[:, g, :], ident[:]) for g in range(8)]
            S.op("pe", fns, reads=[B_scr, B_ident], writes=[B_T])
            S.op("act", I_acopy(wmT[:, gs], bankT[:]), reads=[B_T], writes=[B_const])

        slab_plan = []
        ln_plan = []

        def issue_slab():
            n = state["slab_issue"]
            if n >= len(slab_plan):
                return
            slot = n % NSLAB
            S.dma("pool", [(slab_sb[slot][:].rearrange("p k c -> p (k c)"), slab_plan[n])], f"slab{slot}",
                  writes=[B_slab[slot]])
            state["slab_issue"] = n + 1

        def next_slab():
            n = state["slab_use"]
            state["slab_use"] = n + 1
            assert n < state["slab_issue"], "slab used before issued"
            return n % NSLAB

        def issue_ln():
            n = state["ln_issue"]
            if n >= len(ln_plan):
                return
            slot = n % 2
            r = ln_plan[n]
            S.dma("sp", [(lng_sb[slot][:], lng_d[r:r + 1, :].partition_broadcast(P)),
                         (lnb_sb[slot][:], lnb_d[r:r + 1, :].partition_broadcast(P))], f"lngb{slot}",
                  writes=[B_lngb[slot]])
            state["ln_issue"] = n + 1

        def next_ln():
            n = state["ln_use"]
            state["ln_use"] = n + 1
            return n % 2

        def issue_wd(src, nch):
            v = src.rearrange("(j p) n -> p j n", p=P)
            if nch > 8:
                hh = nch // 2
                pairs = [(wd_sb[:, 0:hh, :], v[:, 0:hh, :]), (wd_sb[:, hh:nch, :], v[:, hh:nch, :])]
            else:
                pairs = [(wd_sb[:, 0:nch, :], v)]
            S.dma("pool", pairs, "wd", writes=[B_wd])

        def emit_transpose_block(b, src_slot, dstT, dstB):
            fns = [I_tr(bankT[:, c, :], hbf[src_slot][:, c * P:(c + 1) * P], ident[:]) for c in range(KC)]
            S.op("pe", fns, reads=[B_hbf[src_slot], B_ident], writes=[B_T])
            S.op("act", I_acopy(dstT[:, :, b * P:(b + 1) * P], bankT[:]), reads=[B_T], writes=[dstB[b]])

        def ln_core(src, srcB, eps_col, ln_slot, out_f, out_fB, out_bf, out_bfB):
            k = state["sctr"] % 2
            state["sctr"] += 1
            for hf in range(2):
                S.op("dve", I_bnstats(stats[k][:, hf, :], src[:, hf * 512:(hf + 1) * 512]),
                     reads=[srcB], writes=[B_small[k]])
            S.op("dve", I_bnaggr(mv[k][:], stats[k][:].rearrange("p a s -> p (a s)")),
                 reads=[B_small[k]], writes=[B_small[k]])
            S.op("act", I_act(sd[k][:], mv[k][:, 1:2], AF.Sqrt, bias=epsb[:, eps_col:eps_col + 1], scale=1.0),
                 reads=[B_small[k]], writes=[B_rs[k]])
            S.op("dve", I_stt(src, src, mv[k][:, 0:1], lng_sb[ln_slot][:], ALU.subtract, ALU.mult),
                 reads=[B_small[k], B_lngb[ln_slot], srcB], writes=[srcB])
            S.op("dve", I_recip(rstd[k][:], sd[k][:]), reads=[B_rs[k]], writes=[B_rs[k]])
            S.op("dve", I_stt(out_bf, src, rstd[k][:], lnb_sb[ln_slot][:], ALU.mult, ALU.add),
                 reads=[B_rs[k], B_lngb[ln_slot], srcB], writes=[out_bfB])
            if out_f is not None:
                S.op("dve", I_stt(out_f, src, rstd[k][:], lnb_sb[ln_slot][:], ALU.mult, ALU.add),
                     reads=[B_rs[k], B_lngb[ln_slot], srcB], writes=[out_fB])

        def emit_ln_part1(b, banks, bbufs, coef, bias_tile=None):
            for hf in range(2):
                hs = h_tok[:, b, hf * 512:(hf + 1) * 512]
                S.op("dve", I_stt(hs, banks[hf][:], float(coef), hs, ALU.mult, ALU.add),
                     reads=[bbufs[hf]], writes=[B_h[b]])
            if bias_tile is not None:
                hb = h_tok[:, b, :]
                S.op("dve", I_tt(hb, hb, bias_tile, ALU.add), reads=[B_h[b], B_const], writes=[B_h[b]])

        def emit_ln_part2a(b, ln_slot):
            k = state["dctr"] % NHBF
            state["dctr"] += 1
            hb = h_tok[:, b, :]
            ln_core(hb, B_h[b], 0, ln_slot, hb, B_h[b], hbf[k][:], B_hbf[k])
            return k

        def emit_ln_part2b(b, k):
            emit_transpose_block(b, k, hT, B_hT)

        def emit_outproj(nb, nch, srcT, srcB, coef, ln_slot, bias_tile=None):
            pending = []
            for b in range(nb):
                banks, bbufs = [], []
                for hf in range(2):
                    d = state["dbank"] % 3
                    state["dbank"] += 1
                    fns = [I_mm(bankD[d][:], srcT[:, j, b * P:(b + 1) * P],
                                wd_sb[:, j, hf * 512:(hf + 1) * 512], j == 0, j == nch - 1)
                           for j in range(nch)]
                    S.op("pe", fns, reads=[B_wd] + srcB(b), writes=[B_D[d]])
                    banks.append(bankD[d])
                    bbufs.append(B_D[d])
                emit_ln_part1(b, banks, bbufs, coef, bias_tile)
                pending.append((b, emit_ln_part2a(b, ln_slot)))
                if len(pending) > LN_DELAY:
                    emit_ln_part2b(*pending.pop(0))
            while pending:
                emit_ln_part2b(*pending.pop(0))

        def emit_ffn(nb):
            tiles = token_tiles(nb)
            ln_slot = next_ln()
            for j in range(FC):
                slot = next_slab()
                for ti, (b0, nbt) in enumerate(tiles):
                    n = nbt * P
                    t0 = b0 * P
                    pa = state["actr"] % 2
                    state["actr"] += 1
                    bg, bu = bankA[2 * pa], bankA[2 * pa + 1]
                    fns = []
                    for kc in range(KC):
                        fns.append(I_mm(bg[:, :n], slab_sb[slot][:, kc, 0:P], hT[:, kc, t0:t0 + n],
                                        kc == 0, kc == KC - 1))
                    for kc in range(KC):
                        fns.append(I_mm(bu[:, :n], slab_sb[slot][:, kc, P:2 * P], hT[:, kc, t0:t0 + n],
                                        kc == 0, kc == KC - 1))
                    S.op("pe", fns, reads=[B_slab[slot]] + [B_hT[b] for b in range(b0, b0 + nbt)],
                         writes=[B_A[2 * pa], B_A[2 * pa + 1]])
                    S.op("act", I_act(silu_t[pa][:, :n], bg[:, :n], AF.Silu),
                         reads=[B_A[2 * pa]], writes=[B_silu[pa]])
                    wr = [bact(j, ti), B_A[2 * pa]]
                    if state["scr_dirty"]:
                        wr = wr + [B_scr] + all_act()
                        state["scr_dirty"] = False
                    S.op("dve", I_tt(act[:, j, t0:t0 + n], silu_t[pa][:, :n], bu[:, :n], ALU.mult),
                         reads=[B_silu[pa], B_A[2 * pa + 1]], writes=wr)
                issue_slab()

            def srcB(b):
                ti = [i for i, (b0, nbt) in enumerate(tiles) if b0 <= b < b0 + nbt][0]
                return [bact(j, ti) for j in range(FC)]
            emit_outproj(nb, FC, act, srcB, 0.5 / ALPHA, ln_slot)
            issue_ln()

        def emit_gmlp(gs, nb):
            state["scr_dirty"] = True
            T = nb * P
            tiles = token_tiles(nb)
            uT = carve(0, [P, KC, T], BF16)
            vtok = carve(2 * KC * TM, [P, NBM, D], BF16)
            vblk = [carve(4 * KC * TM + i * 4096, [P, D], F32) for i in range(2)]
            assert 4 * KC * TM + 8192 <= SCR * 2
            scr_deps = [B_scr] + all_act()
            ln_v = next_ln()
            S.dma("sp", [(bs_sb[:].rearrange("p g t -> p (g t)"), gbs_d[gs:gs + 1, :].partition_broadcast(P))],
                  "bs", writes=[B_bs])
            first = True
            for us in range(4):
                slot = next_slab()
                for ti, (b0, nbt) in enumerate(tiles):
                    n = nbt * P
                    t0 = b0 * P
                    for cc in range(2):
                        ch = us * 2 + cc
                        a = state["actr"] % 4
                        state["actr"] += 1
                        fns = [I_mm(bankA[a][:, :n], slab_sb[slot][:, kc, cc * P:(cc + 1) * P],
                                    hT[:, kc, t0:t0 + n], kc == 0, kc == KC - 1) for kc in range(KC)]
                        S.op("pe", fns, reads=[B_slab[slot]] + [B_hT[b] for b in range(b0, b0 + nbt)],
                             writes=[B_A[a]])
                        S.op("act", I_act(uT[:, ch, t0:t0 + n], bankA[a][:, :n], AF.Gelu), reads=[B_A[a]],
                             writes=(scr_deps if first else [B_scr]))
                        first = False
                issue_slab()
            vslots = [next_slab() for _ in range(4)]
            for b in range(nb):
                k = state["dctr"] % 2
                state["dctr"] += 1
                for vs in range(4):
                    a = state["actr"] % 4
                    state["actr"] += 1
                    fns = [I_mm(bankA[a][:, 0:SLABW], hT[:, kc, b * P:(b + 1) * P], slab_sb[vslots[vs]][:, kc, :],
                                kc == 0, kc == KC - 1) for kc in range(KC)]
                    S.op("pe", fns, reads=[B_slab[vslots[vs]], B_hT[b]], writes=[B_A[a]])
                    S.op("act", I_act(vblk[k][:, vs * SLABW:(vs + 1) * SLABW], bankA[a][:, 0:SLABW], AF.Gelu),
                         reads=[B_A[a]], writes=[B_vb[k]])
                ln_core(vblk[k], B_vb[k], 1, ln_v, None, None, vtok[:, b, :], B_scr)
            for _ in range(4):
                issue_slab()
            issue_ln()
            for b in range(nb):
                for hg in range(2):
                    a = state["actr"] % 4
                    state["actr"] += 1
                    fns = []
                    for g4 in range(4):
                        g = hg * 4 + g4
                        fns.append(I_mm(bankA[a][:, g4 * P:(g4 + 1) * P], vtok[:, b, g * P:(g + 1) * P],
                                        wmT[:, gs, g, :], True, True))
                    S.op("pe", fns, reads=[B_scr, B_const], writes=[B_A[a]])
                    tmpm = silu_t[a % 2]
                    S.op("dve", I_tt(tmpm[:], bankA[a][:], bs_sb[:, hg * 4:(hg + 1) * 4, :].rearrange("p g t -> p (g t)"),
                                     ALU.add), reads=[B_A[a], B_bs], writes=[B_silu[a % 2]])
                    uv = uT[:, hg * 4:(hg + 1) * 4, b * P:(b + 1) * P]
                    S.op("dve", I_tt(uv, tmpm[:].rearrange("p (g t) -> p g t", t=P), uv, ALU.mult),
                         reads=[B_silu[a % 2], B_scr], writes=[B_scr])
            ln_slot = next_ln()
            emit_outproj(nb, KC, uT, lambda b: [B_scr], 1.0 / ALPHA, ln_slot)
            issue_ln()

        def emit_hgrn(hs, nb, is_first_pass):
            state["scr_dirty"] = True
            T = nb * P
            NCH = T // 64
            tiles = token_tiles(nb)
            FB = 4 * TM
            qs = carve(0 * FB, [P, TM], F32)
            fv = carve(1 * FB, [P, TM], F32)
            lf = carve(2 * FB, [P, TM], F32)
            gc = carve(3 * FB, [P, TM], F32)
            eg = carve(4 * FB, [P, TM], F32)
            gsT = carve(5 * FB, [P, TM], F32)
            o_raw = carve(0, [P, NCHM, P], F32)
            sq_t = carve(4 * FB, [P, NCHM // 2, P], F32)
            o0 = 6 * FB
            qdT = carve(o0, [P, TM], BF16)
            kdT = carve(o0 + 2 * TM, [P, TM], BF16)
            kdecT = carve(o0 + 4 * TM, [P, TM], BF16)
            iT_bf = carve(o0 + 6 * TM, [P, TM], BF16)
            o1 = o0 + 8 * TM
            CB = NCHM * P * 2
            kdec64 = carve(o1, [P, NCHM, P], BF16)
            i64 = carve(o1 + CB, [P, NCHM, P], BF16)
            at_bf = carve(o1 + 2 * CB, [P, NCHM, 64], BF16)
            assert o1 + 2 * CB + NCHM * 64 * 2 <= SCR * 2
            ss = ss_t
            hh2 = NCH // 2
            for hd in range(8):
                slot = next_slab()
                slot2 = next_slab()
                for ti, (b0, nbt) in enumerate(tiles):
                    n = nbt * P
                    t0 = b0 * P
                    for (sl_, which) in ((slot, 0), (slot2, 1)):
                        pa = state["actr"] % 2
                        state["actr"] += 1
                        b0_, b1_ = bankA[2 * pa], bankA[2 * pa + 1]
                        fns = []
                        for kc in range(KC):
                            fns.append(I_mm(b0_[:, :n], slab_sb[sl_][:, kc, 0:P], hT[:, kc, t0:t0 + n],
                                            kc == 0, kc == KC - 1))
                        for kc in range(KC):
                            fns.append(I_mm(b1_[:, :n], slab_sb[sl_][:, kc, P:2 * P], hT[:, kc, t0:t0 + n],
                                            kc == 0, kc == KC - 1))
                        S.op("pe", fns, reads=[B_slab[sl_]] + [B_hT[b] for b in range(b0, b0 + nbt)],
                             writes=[B_A[2 * pa], B_A[2 * pa + 1]])
                        if which == 0:
                            S.op("act", I_act(qs[:, t0:t0 + n], b0_[:, :n], AF.Silu), reads=[B_A[2 * pa]],
                                 writes=[B_el])
                            S.op("act", I_act(fv[:, t0:t0 + n], b1_[:, :n], AF.Sigmoid), reads=[B_A[2 * pa + 1]],
                                 writes=[B_el])
                        else:
                            S.op("act", I_acopy(iT_bf[:, t0:t0 + n], b0_[:, :n]), reads=[B_A[2 * pa]],
                                 writes=[B_iT])
                            S.op("act", I_act(gsT[:, t0:t0 + n], b1_[:, :n], AF.Silu), reads=[B_A[2 * pa + 1]],
                                 writes=[B_gs])
                issue_slab()
                issue_slab()
                for half in range(2):
                    c0, c1 = (0, hh2) if half == 0 else (hh2, NCH)
                    fns = [I_tr(bankT[0:64, c - c0, :], iT_bf[:, c * 64:(c + 1) * 64], ident[:]) for c in range(c0, c1)]
                    S.op("pe", fns, reads=[B_iT, B_ident], writes=[B_T])
                    S.op("act", I_acopy(i64[0:64, c0:c1, :], bankT[0:64, 0:c1 - c0, :]), reads=[B_T],
                         writes=[B_ig])
                S.op("dve", I_ts(fv[:, :T], fv[:, :T], omlbt[:, hd:hd + 1], lbt[:, hd:hd + 1], ALU.mult, ALU.add),
                     reads=[B_el, B_const], writes=[B_el])
                S.op("act", I_act(lf[:, :T], fv[:, :T], AF.Ln), reads=[B_el], writes=[B_el])
                S.op("dve", I_ts(fv[:, :T], fv[:, :T], -1.0, 1.0, ALU.mult, ALU.add), reads=[B_el], writes=[B_el])
                S.op("dve", lambda h, o=gc[:, :T], d0=scanm[:, :T], d1=lf[:, :T]: h.tensor_tensor_scan(
                    o, d0, d1, 0.0, ALU.mult, ALU.add), reads=[B_el, B_const], writes=[B_el])
                S.op("act", I_act(eg[:, :T], gc[:, :T], AF.Exp), reads=[B_el], writes=[B_el])
                S.op("act", I_act(lf[:, :T], gc[:, :T], AF.Exp, scale=-1.0), reads=[B_el], writes=[B_el])
                S.op("dve", I_tt(qdT[:, :T], qs[:, :T], eg[:, :T], ALU.mult), reads=[B_el], writes=[B_el])
                S.op("dve", I_tt(fv[:, :T], fv[:, :T], lf[:, :T], ALU.mult), reads=[B_el], writes=[B_el])
                S.op("dve", I_cp(kdT[:, :T], fv[:, :T]), reads=[B_el], writes=[B_el])
                egl = eg[:, :T].rearrange("p (c s) -> p c s", s=64)[:, :, 63:64]
                egl_b = bass.AP(egl.tensor, egl.offset, [list(egl.ap[0]), list(egl.ap[1]), [0, 64]])
                S.op("dve", I_tt(kdecT[:, :T].rearrange("p (c s) -> p c s", s=64),
                                 fv[:, :T].rearrange("p (c s) -> p c s", s=64), egl_b, ALU.mult),
                     reads=[B_el], writes=[B_el])
                S.op("dve", I_cp(ss[:, 0:NCH], egl.rearrange("p c o -> p (c o)")), reads=[B_el], writes=[B_ss])
                for half in range(2):
                    c0, c1 = (0, hh2) if half == 0 else (hh2, NCH)
                    fns = [I_tr(bankT[0:64, c - c0, :], kdecT[:, c * 64:(c + 1) * 64], ident[:]) for c in range(c0, c1)]
                    S.op("pe", fns, reads=[B_el, B_ident], writes=[B_T])
                    S.op("act", I_acopy(kdec64[0:64, c0:c1, :], bankT[0:64, 0:c1 - c0, :]), reads=[B_T],
                         writes=[B_kd])
                for half in range(2):
                    c0, c1 = (0, min(8, NCH)) if half == 0 else (8, NCH)
                    if c1 <= c0:
                        continue
                    a = state["actr"] % 4
                    state["actr"] += 1
                    fns = [I_mm(bankA[a][0:64, (c - c0) * 64:(c - c0 + 1) * 64], kdT[:, c * 64:(c + 1) * 64],
                                qdT[:, c * 64:(c + 1) * 64], True, True) for c in range(c0, c1)]
                    S.op("pe", fns, reads=[B_el], writes=[B_A[a]])
                    S.op("dve", I_tt(at_bf[0:64, c0:c1, :],
                                     bankA[a][0:64, 0:(c1 - c0) * 64].rearrange("p (c t) -> p c t", t=64),
                                     bcast(cm64[:], 1, c1 - c0), ALU.mult),
                         reads=[B_A[a], B_const], writes=[B_at])
                dS = {}
                for c in range(NCH):
                    d = c // 4
                    dS[c] = bankD[d][:, (c % 4) * P:(c % 4 + 1) * P]
                for d in range((NCH + 3) // 4):
                    cs = [c for c in range(NCH) if c // 4 == d]
                    fns = [I_mm(dS[c], kdec64[0:64, c, :], i64[0:64, c, :], True, True) for c in cs]
                    S.op("pe", fns, reads=[B_kd, B_ig], writes=[B_D[d]])
                cur_a = None
                for c in range(NCH):
                    sl = state["sctr"] % 4
                    state["sctr"] += 1
                    if is_first_pass and c == 2 * cfg.halo_blocks and cfg.halo_blocks > 0:
                        S.op("dve", I_ts(S_st[:, hd, :], S_st[:, hd, :], hflag[:, 0:1], None, ALU.mult),
                             reads=[B_S, B_const], writes=[B_S])
                    S.op("dve", I_cp(S_bf[:, sl, :], S_st[:, hd, :]), reads=[B_S], writes=[B_Sbf[sl]])
                    if c % 4 == 0:
                        cur_a = state["actr"] % 4
                        state["actr"] += 1
                    oc = bankA[cur_a][0:64, (c % 4) * P:(c % 4 + 1) * P]
                    fns = [I_mm(oc, at_bf[0:64, c, :], i64[0:64, c, :], True, False),
                           I_mm(oc, qdT[:, c * 64:(c + 1) * 64], S_bf[:, sl, :], False, True)]
                    S.op("pe", fns, reads=[B_at, B_ig, B_el, B_Sbf[sl]], writes=[B_A[cur_a]])
                    S.op("dve", I_stt(S_st[:, hd, :], S_st[:, hd, :], ss[:, c:c + 1], dS[c], ALU.mult, ALU.add),
                         reads=[B_S, B_ss, B_D[c // 4]], writes=[B_S])
                    if c % 4 == 3 or c == NCH - 1:
                        cb = c - (c % 4)
                        S.op("act", I_acopy(o_raw[0:64, cb:c + 1, :],
                                            bankA[cur_a][0:64, 0:(c - cb + 1) * P].rearrange("p (c v) -> p c v", v=P)),
                             reads=[B_A[cur_a]], writes=[B_el])
                for half in range(2):
                    c0, c1 = (0, hh2) if half == 0 else (hh2, NCH)
                    S.op("dve", I_tt(sq_t[0:64, 0:c1 - c0, :], o_raw[0:64, c0:c1, :], o_raw[0:64, c0:c1, :], ALU.mult),
                         reads=[B_el], writes=[B_el])
                    S.op("dve", lambda h, o=ss[0:64, c0:c1], i=sq_t[0:64, 0:c1 - c0, :]: h.tensor_reduce(
                        out=o, in_=i, axis=AX.X, op=ALU.add), reads=[B_el], writes=[B_ss])
                S.op("dve", I_ts(ss[0:64, 0:NCH], ss[0:64, 0:NCH], 1.0 / P, float(RMS_EPS), ALU.mult, ALU.add),
                     reads=[B_ss], writes=[B_ss])
                S.op("act", I_act(ss[0:64, 0:NCH], ss[0:64, 0:NCH], AF.Sqrt), reads=[B_ss], writes=[B_ss])
                S.op("dve", I_recip(ss[0:64, 0:NCH], ss[0:64, 0:NCH]), reads=[B_ss], writes=[B_ss])
                ssb = ss[0:64, 0:NCH]
                ss_b = bass.AP(ssb.tensor, ssb.offset, [list(ssb.ap[0]), list(ssb.ap[1]), [0, P]])
                orw = o_raw[0:64, 0:NCH, :]
                S.op("dve", I_tt(orw, orw, ss_b, ALU.mult), reads=[B_el, B_ss], writes=[B_el])
                pa = state["actr"] % 2
                state["actr"] += 1
                ba = [bankA[2 * pa], bankA[2 * pa + 1]]
                fns = [I_tr(ba[c // 8][:, (c % 8) * 64:(c % 8 + 1) * 64], o_raw[0:64, c, :], ident_f[0:64, 0:64])
                       for c in range(NCH)]
                S.op("pe", fns, reads=[B_el, B_ident], writes=[B_A[2 * pa], B_A[2 * pa + 1]])
                n0 = min(T, 512)
                S.op("dve", I_stt(mixT[:, hd, 0:n0], ba[0][:, 0:n0], ngcol[:, 0:1], gsT[:, 0:n0], ALU.mult, ALU.mult),
                     reads=[B_A[2 * pa], B_gs, B_const], writes=B_mixT[:nb])
                if T > 512:
                    S.op("dve", I_stt(mixT[:, hd, 512:T], ba[1][:, 0:T - 512], ngcol[:, 0:1], gsT[:, 512:T],
                                      ALU.mult, ALU.mult),
                         reads=[B_A[2 * pa + 1], B_gs, B_const], writes=B_mixT[:nb])
            ln_slot = next_ln()
            emit_outproj(nb, KC, mixT, lambda b: [B_mixT[b]], 1.0 / ALPHA, ln_slot)
            issue_ln()

        def emit_attn(as_, nb, is_first_pass):
            state["scr_dirty"] = True
            T = nb * P
            tiles = token_tiles(nb)
            qT = carve(0, [P, KC, TM], BF16)
            o0 = 2 * KC * TM
            e_bf = [carve(o0 + i * 1024, [P, 2, 256], BF16) for i in range(2)]
            eT = [carve(o0 + 2048 + i * 1024, [P, 4, P], BF16) for i in range(2)]
            scr_deps = [B_scr] + all_act()
            first = True
            for qs_ in range(4):
                slot = next_slab()
                for ti, (b0, nbt) in enumerate(tiles):
                    n = nbt * P
                    t0 = b0 * P
                    for cc in range(2):
                        ch = qs_ * 2 + cc
                        a = state["actr"] % 4
                        state["actr"] += 1
                        fns = [I_mm(bankA[a][:, :n], slab_sb[slot][:, kc, cc * P:(cc + 1) * P],
                                    hT[:, kc, t0:t0 + n], kc == 0, kc == KC - 1) for kc in range(KC)]
                        S.op("pe", fns, reads=[B_slab[slot]] + [B_hT[b] for b in range(b0, b0 + nbt)],
                             writes=[B_A[a]])
                        S.op("act", I_act(qT[:, ch, t0:t0 + n], bankA[a][:, :n], AF.Identity,
                                          bias=bq8[:, ch:ch + 1], scale=0.125), reads=[B_A[a], B_const],
                             writes=(scr_deps if first else [B_scr]))
                        first = False
                issue_slab()
            slot = next_slab()
            for ti, (b0, nbt) in enumerate(tiles):
                n = nbt * P
                t0 = b0 * P
                for kvh in range(2):
                    a = state["actr"] % 4
                    state["actr"] += 1
                    fns = [I_mm(bankA[a][:, :n], slab_sb[slot][:, kc, kvh * P:(kvh + 1) * P],
                                hT[:, kc, t0:t0 + n], kc == 0, kc == KC - 1) for kc in range(KC)]
                    S.op("pe", fns, reads=[B_slab[slot]] + [B_hT[b] for b in range(b0, b0 + nbt)],
                         writes=[B_A[a]])
                    S.op("act", I_act(kT2[:, kvh, P + t0:P + t0 + n], bankA[a][:, :n], AF.Identity,
                                      bias=bk2[:, kvh:kvh + 1], scale=1.0), reads=[B_A[a], B_const],
                         writes=[B_kv])
            issue_slab()
            slot = next_slab()
            for b in range(nb):
                a = state["actr"] % 4
                state["actr"] += 1
                fns = [I_mm(bankA[a][:, 0:P], hT[:, kc, b * P:(b + 1) * P], slab_sb[slot][:, kc, 0:P],
                            kc == 0, kc == KC - 1) for kc in range(KC)]
                S.op("pe", fns, reads=[B_slab[slot], B_hT[b]], writes=[B_A[a]])
                S.op("dve", I_tt(vtk[:, b + 1, :], bankA[a][:, 0:P], bvt[:], ALU.add), reads=[B_A[a], B_const],
                     writes=[B_kv])
            issue_slab()
            def stage1(b, hp, mask):
                kvh = hp // 4
                a = state["actr"] % 4
                state["actr"] += 1
                sm = att_small[hp % 2]
                bsm = B_small[hp % 2]
                ei = hp % 2
                fns = []
                for j in range(2):
                    pb = j * 64
                    oc_ = bankA[a][:, j * 256:(j + 1) * 256]
                    fns.append(I_mm(oc_, qT[pb:pb + 64, hp, b * P:(b + 1) * P],
                                    kT2[pb:pb + 64, kvh, b * P:(b + 2) * P], True, False))
                    fns.append(I_mm(oc_, ident[:], mask[:], False, True))
                S.op("pe", fns, reads=[B_scr, B_kv, B_const, B_ident], writes=[B_A[a]])
                S.op("dve", lambda h, o=sm[:, 0:2], i=bankA[a][:].rearrange("p (h k) -> p h k", k=256):
                     h.tensor_reduce(out=o, in_=i, axis=AX.X, op=ALU.max), reads=[B_A[a]], writes=[bsm])
                S.op("dve", I_tt(sm[:, 0:2], sm[:, 0:2], sinkt[:, 2 * hp:2 * hp + 2], ALU.max), reads=[bsm, B_const],
                     writes=[bsm])
                S.op("dve", I_ts(sm[:, 2:4], sm[:, 0:2], -1.0, None, ALU.mult), reads=[bsm], writes=[bsm])
                for j in range(2):
                    S.op("act", I_act(e_bf[ei][:, j, :], bankA[a][:, j * 256:(j + 1) * 256], AF.Exp,
                                      bias=sm[:, 2 + j:3 + j], scale=1.0, accum_out=sm[:, 4 + j:5 + j]),
                         reads=[B_A[a], bsm], writes=[B_e[ei], bsm, B_A[a]])
                S.op("dve", I_tt(sm[:, 6:8], sinkt[:, 2 * hp:2 * hp + 2], sm[:, 2:4], ALU.add), reads=[bsm, B_const],
                     writes=[bsm])
                S.op("act", I_act(sm[:, 6:8], sm[:, 6:8], AF.Exp), reads=[bsm], writes=[bsm])
                S.op("dve", I_tt(sm[:, 8:10], sm[:, 4:6], sm[:, 6:8], ALU.add), reads=[bsm], writes=[bsm])
                S.op("dve", I_recip(sm[:, 10:12], sm[:, 8:10]), reads=[bsm], writes=[bsm])

            def stage2(b, hp, k, ob):
                kvh = hp // 4
                ei = hp % 2
                sm = att_small[hp % 2]
                bsm = B_small[hp % 2]
                fns = [I_tr(bankT[:, j * 2 + kb, :], e_bf[ei][:, j, kb * P:(kb + 1) * P], ident[:])
                       for j in range(2) for kb in range(2)]
                S.op("pe", fns, reads=[B_e[ei], B_ident], writes=[B_T])
                S.op("act", I_acopy(eT[ei][:], bankT[:, 0:4, :]), reads=[B_T], writes=[B_eT[ei]])
                if hp % 4 == 0:
                    d = state["dbank"] % 3
                    state["dbank"] += 1
                    ob[hp // 4] = d
                d = ob[hp // 4]
                c0 = (hp % 4) * 128
                fns = []
                for j in range(2):
                    oc = bankD[d][:, c0 + j * 64:c0 + (j + 1) * 64]
                    fns.append(I_mm(oc, eT[ei][:, j * 2, :], vtk[:, b, kvh * 64:(kvh + 1) * 64], True, False))
                    fns.append(I_mm(oc, eT[ei][:, j * 2 + 1, :], vtk[:, b + 1, kvh * 64:(kvh + 1) * 64], False, True))
                S.op("pe", fns, reads=[B_eT[ei], B_kv], writes=[B_D[d]])
                rd = sm[:, 10:12]
                rd_b = bass.AP(rd.tensor, rd.offset, [list(rd.ap[0]), list(rd.ap[1]), [0, 64]])
                S.op("dve", I_tt(hbf[k][:, hp * 128:(hp + 1) * 128].rearrange("p (j d) -> p j d", d=64),
                                 bankD[d][:, c0:c0 + 128].rearrange("p (j d) -> p j d", d=64), rd_b, ALU.mult),
                     reads=[B_D[d], bsm], writes=[B_hbf[k], B_D[d]])

            for b in range(nb):
                k = state["dctr"] % 2
                state["dctr"] += 1
                use_first = is_first_pass and b == cfg.halo_blocks
                mask = m2fb if use_first else m2b
                ob = [None, None]
                stage1(b, 0, mask)
                for hp in range(8):
                    if hp + 1 < 8:
                        stage1(b, hp + 1, mask)
                    stage2(b, hp, k, ob)
                emit_transpose_block(b, k, mixT, B_mixT)
            S.op("dve", I_cp(kT2[:, :, 0:P], kT2[:, :, nb * P:(nb + 1) * P]), reads=[B_kv], writes=[B_kv])
            S.op("dve", I_cp(vtk[:, 0, :], vtk[:, nb, :]), reads=[B_kv], writes=[B_kv])
            ln_slot = next_ln()
            emit_outproj(nb, KC, mixT, lambda b: [B_mixT[b]], 1.0 / ALPHA, ln_slot, bias_tile=bot[:])
            issue_ln()

        subs = []
        for li in range(L):
            subs.append(("ffn", li, 0))
            subs.append(("mix", li, li % 3))
            subs.append(("ffn", li, 1))
        if cfg.sub_limit is not None:
            subs = subs[:cfg.sub_limit]
        npass = len(cfg.pass_blocks)
        wd_seq = []
        for _ in range(npass):
            for (kind, li, x_) in subs:
                if kind == "ffn":
                    for j in range(FC):
                        slab_plan.append(wgu_d[li * 2 + x_, j])
                    ln_plan.append(li * 3 + (0 if x_ == 0 else 2))
                    wd_seq.append((wd_d[li * 2 + x_], FC))
                else:
                    slot_ = li // 3
                    if x_ == 0:
                        for s_ in range(8):
                            slab_plan.append(gwin_d[slot_, s_])
                        ln_plan.append(L * 3 + slot_)
                        wd_seq.append((gwout_d[slot_], KC))
                    elif x_ == 1:
                        for s_ in range(16):
                            slab_plan.append(hwin_d[slot_, s_])
                        wd_seq.append((hwout_d[slot_], KC))
                    else:
                        for s_ in range(6):
                            slab_plan.append(awqkv_d[slot_, s_])
                        wd_seq.append((awo_d[slot_], KC))
                    ln_plan.append(li * 3 + 1)
        wd_ctr = [0]

        def issue_next_wd():
            if wd_ctr[0] < len(wd_seq):
                issue_wd(*wd_seq[wd_ctr[0]])
                wd_ctr[0] += 1

        for _ in range(NSLAB):
            issue_slab()
        issue_ln()
        issue_ln()
        issue_next_wd()

        out_evs = []
        blk0 = 0
        for pi, nb in enumerate(cfg.pass_blocks):
            src = x_d[blk0 * P:(blk0 + nb) * P, :].rearrange("(b p) d -> p b d", p=P)
            S.dma("sp", [(h_tok[:, 0:nb, :], src)], "xload", writes=B_h[:nb])
            for b in range(nb):
                k = state["dctr"] % 2
                state["dctr"] += 1
                S.op("act", I_acopy(hbf[k][:], h_tok[:, b, :]), reads=[B_h[b]], writes=[B_hbf[k]])
                emit_transpose_block(b, k, hT, B_hT)
            for (kind, li, x_) in subs:
                if kind == "ffn":
                    emit_ffn(nb)
                elif x_ == 0:
                    emit_gmlp(li // 3, nb)
                elif x_ == 1:
                    emit_hgrn(li // 3, nb, pi == 0)
                else:
                    emit_attn(li // 3, nb, pi == 0)
                issue_next_wd()
            pairs = []
            for b in range(nb):
                gb = blk0 + b
                if gb < cfg.halo_blocks:
                    continue
                ob_ = gb - cfg.halo_blocks
                pairs.append((out_d[ob_ * P:(ob_ + 1) * P, :], h_tok[:, b, :]))
            if pairs:
                ev = S.dma("sp", pairs, "store", reads=B_h[:nb])
                out_evs.append(ev)
            blk0 += nb
        S.wait_all("sp", out_evs)

        with nc.Block() as block:
            @block.tensor
            def _(h):
                S.replay("pe", h)

            @block.scalar
            def _(h):
                S.replay("act", h)

            @block.vector
            def _(h):
                S.replay("dve", h)

            @block.gpsimd
            def _(h):
                S.replay("pool", h)

            @block.sync
            def _(h):
                S.replay("sp", h)
    return nc


def make_in_maps(cfg, x_cores, inputs, L, flags=None):
    f32 = np.float32
    ncore = len(x_cores)
    if flags is None:
        flags = [1.0] * ncore
    NBM = max(cfg.pass_blocks)
    TM = NBM * P
    NG_, NH_, NA_ = n_gmlp(L), n_hgrn(L), n_attn(L)
    wgu = inputs["ffn_w_gate_up"]
    wd = inputs["ffn_w_down"]
    common = {
        "wgu": np.stack([slab_layout(gate_up_interleave(wgu[li, fi])) for li in range(L) for fi in range(2)]),
        "wd": np.ascontiguousarray(wd[:L].reshape(L * 2, DFF, D)),
        "lng": np.ascontiguousarray(np.concatenate(
            [inputs["ln_gain"][:L].reshape(L * 3, D), inputs["gmlp_ln_gain"][:2].reshape(-1, D)], 0)[:L * 3 + 2]),
        "lnb": np.ascontiguousarray(np.concatenate(
            [inputs["ln_bias"][:L].reshape(L * 3, D), inputs["gmlp_ln_bias"][:2].reshape(-1, D)], 0)[:L * 3 + 2]),
        "ident": np.eye(P, dtype=f32),
        "tril": np.tril(np.ones((P, P), f32)),
        "cm64": np.triu(np.ones((64, 64), f32)),
        "scanm": np.tile((np.arange(TM) % 64 != 0).astype(f32)[None, :], (P, 1)),
        "hlb": np.ascontiguousarray(inputs["hgrn_lb_logits"].reshape(DEPTH, 8, P).transpose(2, 0, 1)),
    }
    ng = max(NG_, 1)
    common["gwin"] = np.stack([slab_layout(inputs["gmlp_w_in"][s]) for s in range(ng)])
    common["gwout"] = np.ascontiguousarray(inputs["gmlp_w_out"][:ng])
    common["gwsp"] = np.ascontiguousarray(inputs["gmlp_w_spatial"][:ng])
    common["gbs"] = np.ascontiguousarray(inputs["gmlp_b_spatial"][:ng].reshape(ng, 8 * P))
    hw = inputs["hgrn_w_in"][0]
    q_, f_, i_, g_ = [hw[:, j * D:(j + 1) * D].reshape(D, 8, P) for j in range(4)]
    hperm = np.concatenate([np.concatenate([q_[:, h], f_[:, h], i_[:, h], g_[:, h]], axis=1) for h in range(8)], axis=1)
    common["hwin"] = slab_layout(hperm)[None]
    common["hwout"] = np.ascontiguousarray(inputs["hgrn_w_out"][:1])
    common["hng"] = np.ascontiguousarray(inputs["hgrn_norm_gain"][:1])
    common["hngc"] = np.ascontiguousarray(inputs["hgrn_norm_gain"][0].reshape(P, 1))
    aw = inputs["attn_w_qkv"][0]
    ab = inputs["attn_b_qkv"][0]
    k0, k1 = aw[:, 1024:1088], aw[:, 1088:1152]
    awp = np.concatenate([aw[:, :1024], k0, k0, k1, k1, aw[:, 1152:1280], np.zeros((D, 128), f32)], axis=1)
    common["awqkv"] = slab_layout(awp)[None]
    common["abq"] = np.ascontiguousarray(ab[:1024].reshape(8, P).T)[None]
    bk0, bk1 = ab[1024:1088], ab[1088:1152]
    common["abk"] = np.ascontiguousarray(np.stack([np.concatenate([bk0, bk0]), np.concatenate([bk1, bk1])], 1))[None]
    common["abv"] = np.ascontiguousarray(ab[1152:1280])[None]
    common["asink"] = np.ascontiguousarray(inputs["attn_sinks"][:1])
    common["awo"] = np.ascontiguousarray(inputs["attn_w_o"][:1])
    common["abo"] = np.ascontiguousarray(inputs["attn_b_o"][:1])
    NEG = -30000.0
    qi = np.arange(P)[:, None]
    kj = np.arange(P)[None, :]
    m_prev = np.where(kj > qi, 0.0, NEG).astype(f32)
    m_cur = np.where(kj <= qi, 0.0, NEG).astype(f32)
    common["m2"] = np.concatenate([m_prev, m_cur], 1)
    maps = []
    for c in range(ncore):
        m = dict(common)
        m["x"] = np.ascontiguousarray(x_cores[c], dtype=f32)
        m["hflag"] = np.full((P, 1), flags[c], f32)
        if flags[c] > 0:
            m["m2f"] = common["m2"]
        else:
            m["m2f"] = np.concatenate([np.full((P, P), NEG, f32), m_cur], 1)
        maps.append(m)
    return maps


PASS_BLOCKS = [6, 6, 6, 6, 6, 4]


def kernel(**inputs):
    inputs = {k: np.asarray(v) for k, v in inputs.items()}
    x = inputs["x"]
    cfg = Cfg(PASS_BLOCKS, HALO // P)
    x_cores, flags = [], []
    for c in range(NCORES):
        b, p = divmod(c, 4)
        xc = np.zeros((cfg.ntok, D), np.float32)
        s = p * CHUNK - HALO
        if s < 0:
            xc[HALO:] = x[b, 0:CHUNK]
            flags.append(0.0)
        else:
            xc[:] = x[b, s:s + cfg.ntok]
            flags.append(1.0)
        x_cores.append(xc)
    nc = build_program(cfg)
    in_maps = make_in_maps(cfg, x_cores, inputs, DEPTH, flags)
    res = run_bass_kernel_spmd(nc, in_maps, core_ids=list(range(NCORES)))
    out = np.empty((BATCH, SEQ, D), np.float32)
    for c in range(NCORES):
        b, p = divmod(c, 4)
        out[b, p * CHUNK:(p + 1) * CHUNK] = res.results[c]["out"]
    return out
```

```python
import contextlib
import numpy as np
import concourse.bass as bass
import concourse.mybir as mybir
from concourse.bass_utils import run_bass_kernel_spmd

F32 = mybir.dt.float32
BF16 = mybir.dt.bfloat16
AF = mybir.ActivationFunctionType
ALU = mybir.AluOpType
AX = mybir.AxisListType

P = 128
D = 1024
KC = 8
DFF = 2816
FC = 22
DEPTH = 4
ALPHA = (2.0 * DEPTH) ** 0.25
LN_EPS = 1e-5
RMS_EPS = 1e-6
SEQ = 16384
BATCH = 2
NCORES = 8
CHUNK = SEQ // 4
HALO = 256
SLABW = 256
LN_DELAY = 1


class Buf:
    __slots__ = ("name", "w", "r")

    def __init__(self, name):
        self.name = name
        self.w = None
        self.r = []


class Sched:
    ENGS = ("pe", "act", "dve", "pool", "sp")

    def __init__(self, nc, stack):
        self.nc = nc
        self.stack = stack
        self.streams = {e: [] for e in self.ENGS}
        self.esem = {}
        self.ecnt = {}
        self.eepoch = {e: 0 for e in self.ENGS}
        self.seen = {e: {} for e in self.ENGS}
        self.semobj = {}
        self.nsem = 0
        for e in self.ENGS:
            self._new_epoch(e)
        self.dsem = {}
        self.dcnt = {}

    def _mksem(self, name):
        s = self.stack.enter_context(self.nc.semaphore(name))
        self.nsem += 1
        self.semobj[name] = s
        return name

    def _new_epoch(self, e):
        self.eepoch[e] += 1
        self.esem[e] = self._mksem(f"e_{e}_{self.eepoch[e]}")
        self.ecnt[e] = 0

    def _waits(self, eng, deps):
        best = {}
        for d in deps:
            if d is None:
                continue
            s, v = d
            if v > best.get(s, 0):
                best[s] = v
        out = []
        seen = self.seen[eng]
        for s, v in best.items():
            if seen.get(s, 0) >= v:
                continue
            seen[s] = v
            out.append((s, v))
        return out

    def _deps(self, reads, writes):
        deps = []
        for b in reads:
            deps.append(b.w)
        for b in writes:
            deps.append(b.w)
            deps.extend(b.r)
        return deps

    def op(self, eng, fns, reads=(), writes=()):
        if callable(fns):
            fns = [fns]
        if self.ecnt[eng] > 30000:
            self._new_epoch(eng)
        st = self.streams[eng]
        for s, v in self._waits(eng, self._deps(reads, writes)):
            st.append(("w", s, v))
        for f in fns[:-1]:
            st.append(("i", f, None))
        self.ecnt[eng] += 1
        ev = (self.esem[eng], self.ecnt[eng])
        st.append(("i", fns[-1], ev))
        for b in reads:
            b.r.append(ev)
        for b in writes:
            b.w = ev
            b.r = []
        return ev

    def dma(self, eng, pairs, semkey, reads=(), writes=()):
        if semkey not in self.dsem:
            self.dsem[semkey] = self._mksem(f"d_{semkey}")
            self.dcnt[semkey] = 0
        st = self.streams[eng]
        for s, v in self._waits(eng, self._deps(reads, writes)):
            st.append(("w", s, v))
        s = self.dsem[semkey]
        for (o, i) in pairs:
            self.dcnt[semkey] += 16
            st.append(("d", (o, i), s))
        ev = (s, self.dcnt[semkey])
        for b in reads:
            b.r.append(ev)
        for b in writes:
            b.w = ev
            b.r = []
        return ev

    def wait_all(self, eng, evs):
        st = self.streams[eng]
        for s, v in self._waits(eng, evs):
            st.append(("w", s, v))

    def replay(self, eng, h):
        so = self.semobj
        for kind, a, b in self.streams[eng]:
            if kind == "w":
                h.wait_ge(so[a], b)
            elif kind == "i":
                ins = a(h)
                if b is not None:
                    ins.then_inc(so[b[0]], 1)
            else:
                o, i = a
                h.dma_start(out=o, in_=i).then_inc(so[b], 16)


def I_mm(out, lhsT, rhs, start, stop):
    return lambda h: h.matmul(out, lhsT, rhs, start=start, stop=stop)


def I_tr(out, in_, ident):
    return lambda h: h.transpose(out, in_, ident)


def I_act(out, in_, func, **kw):
    return lambda h: h.activation(out=out, in_=in_, func=func, **kw)


def I_acopy(out, in_):
    return lambda h: h.copy(out, in_)


def I_tt(out, in0, in1, op):
    return lambda h: h.tensor_tensor(out=out, in0=in0, in1=in1, op=op)


def I_ts(out, in0, s1, s2, op0, op1=None):
    if op1 is None:
        return lambda h: h.tensor_scalar(out=out, in0=in0, scalar1=s1, scalar2=None, op0=op0)
    return lambda h: h.tensor_scalar(out=out, in0=in0, scalar1=s1, scalar2=s2, op0=op0, op1=op1)


def I_stt(out, in0, scalar, in1, op0, op1):
    return lambda h: h.scalar_tensor_tensor(out=out, in0=in0, scalar=scalar, in1=in1, op0=op0, op1=op1)


def I_cp(out, in_):
    return lambda h: h.tensor_copy(out, in_)


def I_bnstats(out, in_):
    return lambda h: h.bn_stats(out, in_)


def I_bnaggr(out, in_):
    return lambda h: h.bn_aggr(out, in_)


def I_recip(out, in_):
    return lambda h: h.reciprocal(out, in_)


def I_memset(ap, v):
    return lambda h: h.memset(ap, v)

def slab_layout(w):
    k, n = w.shape
    assert k == D and n % SLABW == 0
    return np.ascontiguousarray(
        w.reshape(KC, P, n // SLABW, SLABW).transpose(2, 1, 0, 3).reshape(n // SLABW, P, KC * SLABW))


def gate_up_interleave(w):
    g = w[:, :DFF].reshape(D, FC, P)
    u = w[:, DFF:].reshape(D, FC, P)
    return np.concatenate([g, u], axis=2).reshape(D, 2 * DFF)


class Cfg:
    def __init__(self, pass_blocks, halo_blocks, n_layers=DEPTH, sub_limit=None):
        self.pass_blocks = list(pass_blocks)
        self.halo_blocks = halo_blocks
        self.n_layers = n_layers
        self.sub_limit = sub_limit
        self.nblk = sum(pass_blocks)
        self.ntok = self.nblk * P
        self.nout = (self.nblk - halo_blocks) * P


def token_tiles(nb):
    nt = (nb + 3) // 4
    base, rem = divmod(nb, nt)
    out, s = [], 0
    for i in range(nt):
        n = base + (1 if i < rem else 0)
        out.append((s, n))
        s += n
    return out


def bcast(ap, axis, n):
    dims = [list(d) for d in ap.ap]
    dims.insert(axis, [0, n])
    return bass.AP(ap.tensor, ap.offset, dims)


def n_gmlp(L):
    return (L + 2) // 3


def n_hgrn(L):
    return (L + 1) // 3


def n_attn(L):
    return L // 3


def build_program(cfg):
    nc = bass.Bass("TRN2", target_bir_lowering=False)
    NBM = max(cfg.pass_blocks)
    TM = NBM * P
    NCHM = TM // 64
    L = cfg.n_layers
    NG_, NH_, NA_ = n_gmlp(L), n_hgrn(L), n_attn(L)

    def din(name, shape, dt=F32):
        return nc.dram_tensor(name, list(shape), dt, kind="ExternalInput").ap()

    x_d = din("x", [cfg.ntok, D])
    wgu_d = din("wgu", [L * 2, FC, P, KC * SLABW])
    wd_d = din("wd", [L * 2, DFF, D])
    lng_d = din("lng", [L * 3 + 2, D])
    lnb_d = din("lnb", [L * 3 + 2, D])
    ident_d = din("ident", [P, P])
    gwin_d = din("gwin", [max(NG_, 1), 8, P, KC * SLABW])
    gwout_d = din("gwout", [max(NG_, 1), D, D])
    gwsp_d = din("gwsp", [max(NG_, 1), 8, P, P])
    gbs_d = din("gbs", [max(NG_, 1), 8 * P])
    tril_d = din("tril", [P, P])
    hwin_d = din("hwin", [max(NH_, 1), 16, P, KC * SLABW])
    hwout_d = din("hwout", [max(NH_, 1), D, D])
    hlb_d = din("hlb", [P, DEPTH, 8])
    hng_d = din("hng", [max(NH_, 1), P])
    hngc_d = din("hngc", [P, 1])
    cm64_d = din("cm64", [64, 64])
    scanm_d = din("scanm", [P, TM])
    hflag_d = din("hflag", [P, 1])
    awqkv_d = din("awqkv", [max(NA_, 1), 6, P, KC * SLABW])
    abq_d = din("abq", [max(NA_, 1), P, 8])
    abk_d = din("abk", [max(NA_, 1), P, 2])
    abv_d = din("abv", [max(NA_, 1), P])
    asink_d = din("asink", [max(NA_, 1), 16])
    awo_d = din("awo", [max(NA_, 1), D, D])
    abo_d = din("abo", [max(NA_, 1), D])
    m2_d = din("m2", [P, 256])
    m2f_d = din("m2f", [P, 256])
    out_d = nc.dram_tensor("out", [cfg.nout, D], F32, kind="ExternalOutput").ap()

    stack = contextlib.ExitStack()
    with stack:
        def sb(name, shape, dt):
            return stack.enter_context(nc.sbuf_tensor(name, list(shape), dt))

        def ps(name, shape, dt):
            return stack.enter_context(nc.psum_tensor(name, list(shape), dt))

        h_tok = sb("h_tok", [P, NBM, D], F32)
        hT = sb("hT", [P, KC, TM], BF16)
        mixT = sb("mixT", [P, KC, TM], BF16)
        act = sb("act", [P, FC, TM], BF16)
        SCR = FC * TM
        wd_sb = sb("wd_sb", [P, FC, D], BF16)
        NSLAB = 5
        slab_sb = [sb(f"slab{i}", [P, KC, SLABW], BF16) for i in range(NSLAB)]
        lng_sb = [sb(f"lng{i}", [P, D], F32) for i in range(2)]
        lnb_sb = [sb(f"lnb{i}", [P, D], F32) for i in range(2)]
        NHBF = 3
        hbf = [sb(f"hbf{i}", [P, D], BF16) for i in range(NHBF)]
        silu_t = [sb(f"silu{i}", [P, 512], F32) for i in range(2)]
        ident_f = sb("ident_f32", [P, P], F32)
        ident = sb("ident_bf", [P, P], BF16)
        stats = [sb(f"stats{i}", [P, 2, 6], F32) for i in range(2)]
        mv = [sb(f"mv{i}", [P, 2], F32) for i in range(2)]
        sd = [sb(f"sd{i}", [P, 1], F32) for i in range(2)]
        rstd = [sb(f"rstd{i}", [P, 1], F32) for i in range(2)]
        epsb = sb("epsb", [P, 2], F32)
        wmT = sb("wmT", [P, max(NG_, 1), 8, P], BF16)
        bs_sb = sb("bs_sb", [P, 8, P], F32)
        S_st = sb("S_st", [P, 8, P], F32)
        S_bf = sb("S_bf", [P, 4, P], BF16)
        scanm = sb("scanm_sb", [P, TM], F32)
        lbt = sb("lbt", [P, 8], F32)
        omlbt = sb("omlbt", [P, 8], F32)
        ngt = sb("ngt", [P, P], F32)
        cm64 = sb("cm64_sb", [64, 64], F32)
        hflag = sb("hflag_sb", [P, 1], F32)
        ss_t = sb("ss_t", [P, 16], F32)
        ngcol = sb("ngcol", [P, 1], F32)
        kT2 = sb("kT2", [P, 2, (NBM + 1) * P], BF16)
        vtk = sb("vtk", [P, NBM + 1, P], BF16)
        m2b = sb("m2b", [P, 256], BF16)
        m2fb = sb("m2fb", [P, 256], BF16)
        bq8 = sb("bq8", [P, 8], F32)
        bk2 = sb("bk2", [P, 2], F32)
        bvt = sb("bvt", [P, P], F32)
        sinkt = sb("sinkt", [P, 16], F32)
        bot = sb("bot", [P, D], F32)
        att_small = [sb(f"atts{i}", [P, 16], F32) for i in range(2)]

        bankA = [ps(f"bankA{i}", [P, 512], F32) for i in range(4)]
        bankD = [ps(f"bankD{i}", [P, 512], F32) for i in range(3)]
        bankT = ps("bankT", [P, KC, P], BF16)

        S = Sched(nc, stack)

        def carve(off, shape, dt):
            flat = act[:].rearrange("p c t -> p (c t)")
            n = 1
            for s_ in shape[1:]:
                n *= s_
            if dt == F32:
                assert off % 4 == 0
                v = flat[:, off // 2: off // 2 + 2 * n].bitcast(F32)
                nbytes = 4 * n
            else:
                v = flat[:, off // 2: off // 2 + n]
                nbytes = 2 * n
            assert off + nbytes <= SCR * 2, (off, nbytes, SCR * 2)
            if len(shape) == 3:
                v = v.rearrange("p (a b) -> p a b", b=shape[2])
            elif len(shape) == 4:
                v = v.rearrange("p (a b c) -> p a b c", b=shape[2], c=shape[3])
            return v

        B_h = [Buf(f"h{b}") for b in range(NBM)]
        B_hT = [Buf(f"hT{b}") for b in range(NBM)]
        B_mixT = [Buf(f"mixT{b}") for b in range(NBM)]
        B_scr = Buf("scratch")
        B_act = {}
        B_wd = Buf("wd")
        B_slab = [Buf(f"slab{i}") for i in range(NSLAB)]
        B_lngb = [Buf(f"lngb{i}") for i in range(2)]
        B_hbf = [Buf(f"hbf{i}") for i in range(NHBF)]
        B_silu = [Buf(f"silu{i}") for i in range(2)]
        B_A = [Buf(f"A{i}") for i in range(4)]
        B_D = [Buf(f"D{i}") for i in range(3)]
        B_T = Buf("T")
        B_ident = Buf("ident")
        B_small = [Buf(f"small{i}") for i in range(2)]
        B_rs = [Buf(f"rs{i}") for i in range(2)]
        B_const = Buf("const")
        B_bs = Buf("bs")
        B_S = Buf("S")
        B_Sbf = [Buf(f"Sbf{i}") for i in range(4)]
        B_kv = Buf("kv")
        B_vb = [Buf("vb0"), Buf("vb1")]
        B_vtok = [Buf(f"vtok{b}") for b in range(NBM)]
        B_uTb = [Buf(f"uTb{b}") for b in range(NBM)]
        B_el, B_ig, B_kd, B_at, B_ss = Buf("el"), Buf("ig"), Buf("kd"), Buf("at"), Buf("ss")
        B_iT, B_gs = Buf("iT"), Buf("gs")
        B_e = [Buf("e0"), Buf("e1")]
        B_eT = [Buf("eT0"), Buf("eT1")]

        def bact(j, t):
            k = (j, t)
            if k not in B_act:
                B_act[k] = Buf(f"act{k}")
            return B_act[k]

        def all_act():
            return list(B_act.values())

        state = {"slab_issue": 0, "slab_use": 0, "dctr": 0, "actr": 0, "ln_issue": 0, "ln_use": 0,
                 "dbank": 0, "sctr": 0, "scr_dirty": True}

        S.dma("sp", [(ident_f[:], ident_d[:, :])], "ident", writes=[B_ident])
        S.op("dve", I_cp(ident[:], ident_f[:]), reads=[B_ident], writes=[B_ident])
        S.op("dve", I_memset(epsb[:, 0:1], float(LN_EPS / ALPHA ** 2)), writes=[B_small[0], B_small[1]])
        S.op("dve", I_memset(epsb[:, 1:2], float(LN_EPS)), writes=[B_small[0], B_small[1]])
        cpairs = [(scanm[:], scanm_d[:, :]), (cm64[:], cm64_d[:, :]), (hflag[:], hflag_d[:, :])]
        tmp_f = carve(0, [P, 8, P], F32)
        tmp_f2 = carve(4096, [P, 8, P], F32)
        if NH_ > 0:
            cpairs.append((ngt[:], hng_d[0:1, :].partition_broadcast(P)))
            cpairs.append((ngcol[:], hngc_d[:, :]))
            cpairs.append((tmp_f[:, 0:DEPTH, 0:8], hlb_d[:, :, :]))
        if NA_ > 0:
            cpairs += [(bq8[:], abq_d[0]), (bk2[:], abk_d[0]),
                       (bvt[:], abv_d[0:1, :].partition_broadcast(P)),
                       (sinkt[:], asink_d[0:1, :].partition_broadcast(P)),
                       (bot[:], abo_d[0:1, :].partition_broadcast(P)),
                       (tmp_f2[:, 0, :], m2_d[:, 0:128]), (tmp_f2[:, 1, :], m2_d[:, 128:256]),
                       (tmp_f2[:, 2, :], m2f_d[:, 0:128]), (tmp_f2[:, 3, :], m2f_d[:, 128:256])]
        S.dma("sp", cpairs, "const", writes=[B_const, B_scr])
        if NH_ > 0:
            hl = 1
            e4 = tmp_f[:, 0:DEPTH, 0:8]
            mx = tmp_f[:, 4, 0:8]
            S.op("dve", I_tt(mx, tmp_f[:, 0, 0:8], tmp_f[:, 1, 0:8], ALU.max), reads=[B_const, B_scr],
                 writes=[B_scr])
            for l in range(2, DEPTH):
                S.op("dve", I_tt(mx, mx, tmp_f[:, l, 0:8], ALU.max), reads=[B_scr], writes=[B_scr])
            for l in range(DEPTH):
                S.op("dve", I_tt(tmp_f[:, l, 0:8], tmp_f[:, l, 0:8], mx, ALU.subtract), reads=[B_scr],
                     writes=[B_scr])
            for l in range(DEPTH):
                S.op("act", I_act(tmp_f[:, l, 0:8], tmp_f[:, l, 0:8], AF.Exp), reads=[B_scr], writes=[B_scr])
            den = tmp_f[:, 5, 0:8]
            S.op("dve", I_tt(den, tmp_f[:, 0, 0:8], tmp_f[:, 1, 0:8], ALU.add), reads=[B_scr], writes=[B_scr])
            for l in range(2, DEPTH):
                S.op("dve", I_tt(den, den, tmp_f[:, l, 0:8], ALU.add), reads=[B_scr], writes=[B_scr])
            S.op("dve", I_recip(den, den), reads=[B_scr], writes=[B_scr])
            num = tmp_f[:, 6, 0:8]
            S.op("dve", I_cp(num, tmp_f[:, 1, 0:8]), reads=[B_scr], writes=[B_scr])
            for l in range(2, hl + 1):
                S.op("dve", I_tt(num, num, tmp_f[:, l, 0:8], ALU.add), reads=[B_scr], writes=[B_scr])
            S.op("dve", I_tt(lbt[:], num, den, ALU.mult), reads=[B_scr], writes=[B_const])
            S.op("dve", I_ts(omlbt[:], lbt[:], -1.0, 1.0, ALU.mult, ALU.add), reads=[B_const], writes=[B_const])
            S.op("dve", I_memset(S_st[:], 0.0), writes=[B_S])
        if NA_ > 0:
            S.op("dve", I_cp(m2b[:], tmp_f2[:, 0:2, :].rearrange("p a b -> p (a b)")), reads=[B_const, B_scr],
                 writes=[B_const])
            S.op("dve", I_cp(m2fb[:], tmp_f2[:, 2:4, :].rearrange("p a b -> p (a b)")), reads=[B_const, B_scr],
                 writes=[B_const])
            S.op("dve", I_ts(bq8[:], bq8[:], 0.125, None, ALU.mult), reads=[B_const], writes=[B_const])
            S.op("dve", I_ts(bot[:], bot[:], float(1.0 / ALPHA), None, ALU.mult), reads=[B_const],
                 writes=[B_const])
            S.op("dve", I_memset(kT2[:], 0.0), writes=[B_kv])
            S.op("dve", I_memset(vtk[:], 0.0), writes=[B_kv])
        for gs in range(NG_):
            wsp_f = carve(8192, [P, 8, P], F32)
            wsp_b = carve(8192 + 4096, [P, 8, P], BF16)
            trl = carve(8192 + 4096 + 2048, [P, P], F32)
            S.dma("sp", [(wsp_f, gwsp_d[gs].rearrange("g t s -> t g s")), (trl, tril_d[:, :])], "const",
                  writes=[B_scr])
            S.op("dve", I_tt(wsp_b, wsp_f, bcast(trl, 1, 8), ALU.mult), reads=[B_scr], writes=[B_scr])
            fns = [I_tr(bankT[:, g, :], wsp_b[:, g, :], ident[:]) for g in range(8)]
            S.op("pe", fns, reads=[B_scr, B_ident], writes=[B_T])
            S.op("act", I_acopy(wmT[:, gs], bankT[:]), reads=[B_T], writes=[B_const])

        slab_plan = []
        ln_plan = []

        def issue_slab():
            n = state["slab_issue"]
            if n >= len(slab_plan):
                return
            slot = n % NSLAB
            S.dma("pool", [(slab_sb[slot][:].rearrange("p k c -> p (k c)"), slab_plan[n])], f"slab{slot}",
                  writes=[B_slab[slot]])
            state["slab_issue"] = n + 1

        def next_slab():
            n = state["slab_use"]
            state["slab_use"] = n + 1
            assert n < state["slab_issue"], "slab used before issued"
            return n % NSLAB

        def issue_ln():
            n = state["ln_issue"]
            if n >= len(ln_plan):
                return
            slot = n % 2
            r = ln_plan[n]
            S.dma("sp", [(lng_sb[slot][:], lng_d[r:r + 1, :].partition_broadcast(P)),
                         (lnb_sb[slot][:], lnb_d[r:r + 1, :].partition_broadcast(P))], f"lngb{slot}",
                  writes=[B_lngb[slot]])
            state["ln_issue"] = n + 1

        def next_ln():
            n = state["ln_use"]
            state["ln_use"] = n + 1
            return n % 2

        def issue_wd(src, nch):
            v = src.rearrange("(j p) n -> p j n", p=P)
            if nch > 8:
                hh = nch // 2
                pairs = [(wd_sb[:, 0:hh, :], v[:, 0:hh, :]), (wd_sb[:, hh:nch, :], v[:, hh:nch, :])]
            else:
                pairs = [(wd_sb[:, 0:nch, :], v)]
            S.dma("pool", pairs, "wd", writes=[B_wd])

        def emit_transpose_block(b, src_slot, dstT, dstB):
            fns = [I_tr(bankT[:, c, :], hbf[src_slot][:, c * P:(c + 1) * P], ident[:]) for c in range(KC)]
            S.op("pe", fns, reads=[B_hbf[src_slot], B_ident], writes=[B_T])
            S.op("act", I_acopy(dstT[:, :, b * P:(b + 1) * P], bankT[:]), reads=[B_T], writes=[dstB[b]])

        def ln_core(src, srcB, eps_col, ln_slot, out_f, out_fB, out_bf, out_bfB):
            k = state["sctr"] % 2
            state["sctr"] += 1
            for hf in range(2):
                S.op("dve", I_bnstats(stats[k][:, hf, :], src[:, hf * 512:(hf + 1) * 512]),
                     reads=[srcB], writes=[B_small[k]])
            S.op("dve", I_bnaggr(mv[k][:], stats[k][:].rearrange("p a s -> p (a s)")),
                 reads=[B_small[k]], writes=[B_small[k]])
            S.op("act", I_act(sd[k][:], mv[k][:, 1:2], AF.Sqrt, bias=epsb[:, eps_col:eps_col + 1], scale=1.0),
                 reads=[B_small[k]], writes=[B_rs[k]])
            S.op("dve", I_stt(src, src, mv[k][:, 0:1], lng_sb[ln_slot][:], ALU.subtract, ALU.mult),
                 reads=[B_small[k], B_lngb[ln_slot], srcB], writes=[srcB])
            S.op("dve", I_recip(rstd[k][:], sd[k][:]), reads=[B_rs[k]], writes=[B_rs[k]])
            S.op("dve", I_stt(out_bf, src, rstd[k][:], lnb_sb[ln_slot][:], ALU.mult, ALU.add),
                 reads=[B_rs[k], B_lngb[ln_slot], srcB], writes=[out_bfB])
            if out_f is not None:
                S.op("dve", I_stt(out_f, src, rstd[k][:], lnb_sb[ln_slot][:], ALU.mult, ALU.add),
                     reads=[B_rs[k], B_lngb[ln_slot], srcB], writes=[out_fB])

        def emit_ln_part1(b, banks, bbufs, coef, bias_tile=None):
            for hf in range(2):
                hs = h_tok[:, b, hf * 512:(hf + 1) * 512]
                S.op("dve", I_stt(hs, banks[hf][:], float(coef), hs, ALU.mult, ALU.add),
                     reads=[bbufs[hf]], writes=[B_h[b]])
            if bias_tile is not None:
                hb = h_tok[:, b, :]
                S.op("dve", I_tt(hb, hb, bias_tile, ALU.add), reads=[B_h[b], B_const], writes=[B_h[b]])

        def emit_ln_part2a(b, ln_slot):
            k = state["dctr"] % NHBF
            state["dctr"] += 1
            hb = h_tok[:, b, :]
            ln_core(hb, B_h[b], 0, ln_slot, hb, B_h[b], hbf[k][:], B_hbf[k])
            return k

        def emit_ln_part2b(b, k):
            emit_transpose_block(b, k, hT, B_hT)

        def emit_outproj(nb, nch, srcT, srcB, coef, ln_slot, bias_tile=None):
            pending = []
            for b in range(nb):
                banks, bbufs = [], []
                for hf in range(2):
                    d = state["dbank"] % 3
                    state["dbank"] += 1
                    fns = [I_mm(bankD[d][:], srcT[:, j, b * P:(b + 1) * P],
                                wd_sb[:, j, hf * 512:(hf + 1) * 512], j == 0, j == nch - 1)
                           for j in range(nch)]
                    S.op("pe", fns, reads=[B_wd] + srcB(b), writes=[B_D[d]])
                    banks.append(bankD[d])
                    bbufs.append(B_D[d])
                emit_ln_part1(b, banks, bbufs, coef, bias_tile)
                pending.append((b, emit_ln_part2a(b, ln_slot)))
                if len(pending) > LN_DELAY:
                    emit_ln_part2b(*pending.pop(0))
            while pending:
                emit_ln_part2b(*pending.pop(0))

        def emit_ffn(nb):
            tiles = token_tiles(nb)
            ln_slot = next_ln()
            for j in range(FC):
                slot = next_slab()
                for ti, (b0, nbt) in enumerate(tiles):
                    n = nbt * P
                    t0 = b0 * P
                    pa = state["actr"] % 2
                    state["actr"] += 1
                    bg, bu = bankA[2 * pa], bankA[2 * pa + 1]
                    fns = []
                    for kc in range(KC):
                        fns.append(I_mm(bg[:, :n], slab_sb[slot][:, kc, 0:P], hT[:, kc, t0:t0 + n],
                                        kc == 0, kc == KC - 1))
                    for kc in range(KC):
                        fns.append(I_mm(bu[:, :n], slab_sb[slot][:, kc, P:2 * P], hT[:, kc, t0:t0 + n],
                                        kc == 0, kc == KC - 1))
                    S.op("pe", fns, reads=[B_slab[slot]] + [B_hT[b] for b in range(b0, b0 + nbt)],
                         writes=[B_A[2 * pa], B_A[2 * pa + 1]])
                    S.op("act", I_act(silu_t[pa][:, :n], bg[:, :n], AF.Silu),
                         reads=[B_A[2 * pa]], writes=[B_silu[pa]])
                    wr = [bact(j, ti), B_A[2 * pa]]
                    if state["scr_dirty"]:
                        wr = wr + [B_scr] + all_act()
                        state["scr_dirty"] = False
                    S.op("dve", I_tt(act[:, j, t0:t0 + n], silu_t[pa][:, :n], bu[:, :n], ALU.mult),
                         reads=[B_silu[pa], B_A[2 * pa + 1]], writes=wr)
                issue_slab()

            def srcB(b):
                ti = [i for i, (b0, nbt) in enumerate(tiles) if b0 <= b < b0 + nbt][0]
                return [bact(j, ti) for j in range(FC)]
            emit_outproj(nb, FC, act, srcB, 0.5 / ALPHA, ln_slot)
            issue_ln()

        def emit_gmlp(gs, nb):
            state["scr_dirty"] = True
            T = nb * P
            tiles = token_tiles(nb)
            uT = carve(0, [P, KC, T], BF16)
            vtok = carve(2 * KC * TM, [P, NBM, D], BF16)
            vblk = [carve(4 * KC * TM + i * 4096, [P, D], F32) for i in range(2)]
            assert 4 * KC * TM + 8192 <= SCR * 2
            scr_deps = [B_scr] + all_act()
            ln_v = next_ln()
            S.dma("sp", [(bs_sb[:].rearrange("p g t -> p (g t)"), gbs_d[gs:gs + 1, :].partition_broadcast(P))],
                  "bs", writes=[B_bs])
            first = True
            for us in range(4):
                slot = next_slab()
                for ti, (b0, nbt) in enumerate(tiles):
                    n = nbt * P
                    t0 = b0 * P
                    for cc in range(2):
                        ch = us * 2 + cc
                        a = state["actr"] % 4
                        state["actr"] += 1
                        fns = [I_mm(bankA[a][:, :n], slab_sb[slot][:, kc, cc * P:(cc + 1) * P],
                                    hT[:, kc, t0:t0 + n], kc == 0, kc == KC - 1) for kc in range(KC)]
                        S.op("pe", fns, reads=[B_slab[slot]] + [B_hT[b] for b in range(b0, b0 + nbt)],
                             writes=[B_A[a]])
                        S.op("act", I_act(uT[:, ch, t0:t0 + n], bankA[a][:, :n], AF.Gelu), reads=[B_A[a]],
                             writes=(scr_deps if first else [B_scr]))
                        first = False
                issue_slab()
            vslots = [next_slab() for _ in range(4)]

            def mixing(b):
                for hg in range(2):
                    a = state["actr"] % 4
                    state["actr"] += 1
                    fns = []
                    for g4 in range(4):
                        g = hg * 4 + g4
                        fns.append(I_mm(bankA[a][:, g4 * P:(g4 + 1) * P], vtok[:, b, g * P:(g + 1) * P],
                                        wmT[:, gs, g, :], True, True))
                    S.op("pe", fns, reads=[B_vtok[b], B_const], writes=[B_A[a]])
                    tmpm = silu_t[a % 2]
                    S.op("dve", I_tt(tmpm[:], bankA[a][:], bs_sb[:, hg * 4:(hg + 1) * 4, :].rearrange("p g t -> p (g t)"),
                                     ALU.add), reads=[B_A[a], B_bs], writes=[B_silu[a % 2]])
                    uv = uT[:, hg * 4:(hg + 1) * 4, b * P:(b + 1) * P]
                    S.op("dve", I_tt(uv, tmpm[:].rearrange("p (g t) -> p g t", t=P), uv, ALU.mult),
                         reads=[B_silu[a % 2], B_scr], writes=[B_uTb[b]])

            for b in range(nb):
                k = state["dctr"] % 2
                state["dctr"] += 1
                for vs in range(4):
                    a = state["actr"] % 4
                    state["actr"] += 1
                    fns = [I_mm(bankA[a][:, 0:SLABW], hT[:, kc, b * P:(b + 1) * P], slab_sb[vslots[vs]][:, kc, :],
                                kc == 0, kc == KC - 1) for kc in range(KC)]
                    S.op("pe", fns, reads=[B_slab[vslots[vs]], B_hT[b]], writes=[B_A[a]])
                    S.op("act", I_act(vblk[k][:, vs * SLABW:(vs + 1) * SLABW], bankA[a][:, 0:SLABW], AF.Gelu),
                         reads=[B_A[a]], writes=[B_vb[k]])
                ln_core(vblk[k], B_vb[k], 1, ln_v, None, None, vtok[:, b, :], B_vtok[b])
                if b >= 1:
                    mixing(b - 1)
            mixing(nb - 1)
            for _ in range(4):
                issue_slab()
            issue_ln()
            ln_slot = next_ln()
            emit_outproj(nb, KC, uT, lambda b: [B_uTb[b]], 1.0 / ALPHA, ln_slot)
            issue_ln()

        def emit_hgrn(hs, nb, is_first_pass):
            state["scr_dirty"] = True
            T = nb * P
            NCH = T // 64
            tiles = token_tiles(nb)
            FB = 4 * TM
            qs = carve(0 * FB, [P, TM], F32)
            fv = carve(1 * FB, [P, TM], F32)
            lf = carve(2 * FB, [P, TM], F32)
            gc = carve(3 * FB, [P, TM], F32)
            eg = carve(4 * FB, [P, TM], F32)
            gsT = carve(5 * FB, [P, TM], F32)
            o_raw = carve(0, [P, NCHM, P], F32)
            sq_t = carve(4 * FB, [P, NCHM // 2, P], F32)
            o0 = 6 * FB
            qdT = carve(o0, [P, TM], BF16)
            kdT = carve(o0 + 2 * TM, [P, TM], BF16)
            kdecT = carve(o0 + 4 * TM, [P, TM], BF16)
            iT_bf = carve(o0 + 6 * TM, [P, TM], BF16)
            o1 = o0 + 8 * TM
            CB = NCHM * P * 2
            kdec64 = carve(o1, [P, NCHM, P], BF16)
            i64 = carve(o1 + CB, [P, NCHM, P], BF16)
            at_bf = carve(o1 + 2 * CB, [P, NCHM, 64], BF16)
            assert o1 + 2 * CB + NCHM * 64 * 2 <= SCR * 2
            ss = ss_t
            hh2 = NCH // 2
            for hd in range(8):
                slot = next_slab()
                slot2 = next_slab()
                for (sl_, col, kind_) in ((slot, 0, "q"), (slot2, 1, "g"), (slot, 1, "f"), (slot2, 0, "i")):
                    for ti, (b0, nbt) in enumerate(tiles):
                        n = nbt * P
                        t0 = b0 * P
                        a = state["actr"] % 4
                        state["actr"] += 1
                        fns = [I_mm(bankA[a][:, :n], slab_sb[sl_][:, kc, col * P:(col + 1) * P], hT[:, kc, t0:t0 + n],
                                    kc == 0, kc == KC - 1) for kc in range(KC)]
                        S.op("pe", fns, reads=[B_slab[sl_]] + [B_hT[b] for b in range(b0, b0 + nbt)],
                             writes=[B_A[a]])
                        if kind_ == "q":
                            S.op("act", I_act(qs[:, t0:t0 + n], bankA[a][:, :n], AF.Silu), reads=[B_A[a]],
                                 writes=[B_el])
                        elif kind_ == "g":
                            S.op("act", I_act(gsT[:, t0:t0 + n], bankA[a][:, :n], AF.Silu), reads=[B_A[a]],
                                 writes=[B_gs])
                        elif kind_ == "f":
                            S.op("act", I_act(fv[:, t0:t0 + n], bankA[a][:, :n], AF.Sigmoid), reads=[B_A[a]],
                                 writes=[B_el])
                        else:
                            S.op("act", I_acopy(iT_bf[:, t0:t0 + n], bankA[a][:, :n]), reads=[B_A[a]],
                                 writes=[B_iT])
                issue_slab()
                issue_slab()
                for half in range(2):
                    c0, c1 = (0, hh2) if half == 0 else (hh2, NCH)
                    fns = [I_tr(bankT[0:64, c - c0, :], iT_bf[:, c * 64:(c + 1) * 64], ident[:]) for c in range(c0, c1)]
                    S.op("pe", fns, reads=[B_iT, B_ident], writes=[B_T])
                    S.op("act", I_acopy(i64[0:64, c0:c1, :], bankT[0:64, 0:c1 - c0, :]), reads=[B_T],
                         writes=[B_ig])
                S.op("dve", I_ts(fv[:, :T], fv[:, :T], omlbt[:, hd:hd + 1], lbt[:, hd:hd + 1], ALU.mult, ALU.add),
                     reads=[B_el, B_const], writes=[B_el])
                S.op("act", I_act(lf[:, :T], fv[:, :T], AF.Ln), reads=[B_el], writes=[B_el])
                S.op("dve", I_ts(fv[:, :T], fv[:, :T], -1.0, 1.0, ALU.mult, ALU.add), reads=[B_el], writes=[B_el])
                S.op("dve", lambda h, o=gc[:, :T], d0=scanm[:, :T], d1=lf[:, :T]: h.tensor_tensor_scan(
                    o, d0, d1, 0.0, ALU.mult, ALU.add), reads=[B_el, B_const], writes=[B_el])
                S.op("act", I_act(eg[:, :T], gc[:, :T], AF.Exp), reads=[B_el], writes=[B_el])
                S.op("act", I_act(lf[:, :T], gc[:, :T], AF.Exp, scale=-1.0), reads=[B_el], writes=[B_el])
                S.op("dve", I_tt(qdT[:, :T], qs[:, :T], eg[:, :T], ALU.mult), reads=[B_el], writes=[B_el])
                S.op("dve", I_tt(fv[:, :T], fv[:, :T], lf[:, :T], ALU.mult), reads=[B_el], writes=[B_el])
                S.op("dve", I_cp(kdT[:, :T], fv[:, :T]), reads=[B_el], writes=[B_el])
                egl = eg[:, :T].rearrange("p (c s) -> p c s", s=64)[:, :, 63:64]
                egl_b = bass.AP(egl.tensor, egl.offset, [list(egl.ap[0]), list(egl.ap[1]), [0, 64]])
                S.op("dve", I_tt(kdecT[:, :T].rearrange("p (c s) -> p c s", s=64),
                                 fv[:, :T].rearrange("p (c s) -> p c s", s=64), egl_b, ALU.mult),
                     reads=[B_el], writes=[B_el])
                S.op("dve", I_cp(ss[:, 0:NCH], egl.rearrange("p c o -> p (c o)")), reads=[B_el], writes=[B_ss])
                for half in range(2):
                    c0, c1 = (0, hh2) if half == 0 else (hh2, NCH)
                    fns = [I_tr(bankT[0:64, c - c0, :], kdecT[:, c * 64:(c + 1) * 64], ident[:]) for c in range(c0, c1)]
                    S.op("pe", fns, reads=[B_el, B_ident], writes=[B_T])
                    S.op("act", I_acopy(kdec64[0:64, c0:c1, :], bankT[0:64, 0:c1 - c0, :]), reads=[B_T],
                         writes=[B_kd])
                for half in range(2):
                    c0, c1 = (0, min(8, NCH)) if half == 0 else (8, NCH)
                    if c1 <= c0:
                        continue
                    a = state["actr"] % 4
                    state["actr"] += 1
                    fns = [I_mm(bankA[a][0:64, (c - c0) * 64:(c - c0 + 1) * 64], kdT[:, c * 64:(c + 1) * 64],
                                qdT[:, c * 64:(c + 1) * 64], True, True) for c in range(c0, c1)]
                    S.op("pe", fns, reads=[B_el], writes=[B_A[a]])
                    S.op("dve", I_tt(at_bf[0:64, c0:c1, :],
                                     bankA[a][0:64, 0:(c1 - c0) * 64].rearrange("p (c t) -> p c t", t=64),
                                     bcast(cm64[:], 1, c1 - c0), ALU.mult),
                         reads=[B_A[a], B_const], writes=[B_at])
                dS = {}
                for c in range(NCH):
                    d = c // 4
                    dS[c] = bankD[d][:, (c % 4) * P:(c % 4 + 1) * P]
                for d in range((NCH + 3) // 4):
                    cs = [c for c in range(NCH) if c // 4 == d]
                    fns = [I_mm(dS[c], kdec64[0:64, c, :], i64[0:64, c, :], True, True) for c in cs]
                    S.op("pe", fns, reads=[B_kd, B_ig], writes=[B_D[d]])
                cur_a = None
                for c in range(NCH):
                    sl = state["sctr"] % 4
                    state["sctr"] += 1
                    if is_first_pass and c == 2 * cfg.halo_blocks and cfg.halo_blocks > 0:
                        S.op("dve", I_ts(S_st[:, hd, :], S_st[:, hd, :], hflag[:, 0:1], None, ALU.mult),
                             reads=[B_S, B_const], writes=[B_S])
                    S.op("dve", I_cp(S_bf[:, sl, :], S_st[:, hd, :]), reads=[B_S], writes=[B_Sbf[sl]])
                    if c % 4 == 0:
                        cur_a = state["actr"] % 4
                        state["actr"] += 1
                    oc = bankA[cur_a][0:64, (c % 4) * P:(c % 4 + 1) * P]
                    fns = [I_mm(oc, at_bf[0:64, c, :], i64[0:64, c, :], True, False),
                           I_mm(oc, qdT[:, c * 64:(c + 1) * 64], S_bf[:, sl, :], False, True)]
                    S.op("pe", fns, reads=[B_at, B_ig, B_el, B_Sbf[sl]], writes=[B_A[cur_a]])
                    S.op("dve", I_stt(S_st[:, hd, :], S_st[:, hd, :], ss[:, c:c + 1], dS[c], ALU.mult, ALU.add),
                         reads=[B_S, B_ss, B_D[c // 4]], writes=[B_S])
                    if c % 4 == 3 or c == NCH - 1:
                        cb = c - (c % 4)
                        S.op("act", I_acopy(o_raw[0:64, cb:c + 1, :],
                                            bankA[cur_a][0:64, 0:(c - cb + 1) * P].rearrange("p (c v) -> p c v", v=P)),
                             reads=[B_A[cur_a]], writes=[B_el])
                for half in range(2):
                    c0, c1 = (0, hh2) if half == 0 else (hh2, NCH)
                    S.op("dve", I_tt(sq_t[0:64, 0:c1 - c0, :], o_raw[0:64, c0:c1, :], o_raw[0:64, c0:c1, :], ALU.mult),
                         reads=[B_el], writes=[B_el])
                    S.op("dve", lambda h, o=ss[0:64, c0:c1], i=sq_t[0:64, 0:c1 - c0, :]: h.tensor_reduce(
                        out=o, in_=i, axis=AX.X, op=ALU.add), reads=[B_el], writes=[B_ss])
                S.op("dve", I_ts(ss[0:64, 0:NCH], ss[0:64, 0:NCH], 1.0 / P, float(RMS_EPS), ALU.mult, ALU.add),
                     reads=[B_ss], writes=[B_ss])
                S.op("act", I_act(ss[0:64, 0:NCH], ss[0:64, 0:NCH], AF.Ln), reads=[B_ss], writes=[B_ss])
                S.op("act", I_act(ss[0:64, 0:NCH], ss[0:64, 0:NCH], AF.Exp, scale=-0.5), reads=[B_ss], writes=[B_ss])
                ssb = ss[0:64, 0:NCH]
                ss_b = bass.AP(ssb.tensor, ssb.offset, [list(ssb.ap[0]), list(ssb.ap[1]), [0, P]])
                orw = o_raw[0:64, 0:NCH, :]
                S.op("dve", I_tt(orw, orw, ss_b, ALU.mult), reads=[B_el, B_ss], writes=[B_el])
                pa = state["actr"] % 2
                state["actr"] += 1
                ba = [bankA[2 * pa], bankA[2 * pa + 1]]
                fns = [I_tr(ba[c // 8][:, (c % 8) * 64:(c % 8 + 1) * 64], o_raw[0:64, c, :], ident_f[0:64, 0:64])
                       for c in range(NCH)]
                S.op("pe", fns, reads=[B_el, B_ident], writes=[B_A[2 * pa], B_A[2 * pa + 1]])
                n0 = min(T, 512)
                S.op("dve", I_stt(mixT[:, hd, 0:n0], ba[0][:, 0:n0], ngcol[:, 0:1], gsT[:, 0:n0], ALU.mult, ALU.mult),
                     reads=[B_A[2 * pa], B_gs, B_const], writes=B_mixT[:nb])
                if T > 512:
                    S.op("dve", I_stt(mixT[:, hd, 512:T], ba[1][:, 0:T - 512], ngcol[:, 0:1], gsT[:, 512:T],
                                      ALU.mult, ALU.mult),
                         reads=[B_A[2 * pa + 1], B_gs, B_const], writes=B_mixT[:nb])
            ln_slot = next_ln()
            emit_outproj(nb, KC, mixT, lambda b: [B_mixT[b]], 1.0 / ALPHA, ln_slot)
            issue_ln()

        def emit_attn(as_, nb, is_first_pass):
            state["scr_dirty"] = True
            T = nb * P
            tiles = token_tiles(nb)
            qT = carve(0, [P, KC, TM], BF16)
            o0 = 2 * KC * TM
            e_bf = [carve(o0 + i * 1024, [P, 2, 256], BF16) for i in range(2)]
            eT = [carve(o0 + 2048 + i * 1024, [P, 4, P], BF16) for i in range(2)]
            scr_deps = [B_scr] + all_act()
            first = True
            for qs_ in range(4):
                slot = next_slab()
                for ti, (b0, nbt) in enumerate(tiles):
                    n = nbt * P
                    t0 = b0 * P
                    for cc in range(2):
                        ch = qs_ * 2 + cc
                        a = state["actr"] % 4
                        state["actr"] += 1
                        fns = [I_mm(bankA[a][:, :n], slab_sb[slot][:, kc, cc * P:(cc + 1) * P],
                                    hT[:, kc, t0:t0 + n], kc == 0, kc == KC - 1) for kc in range(KC)]
                        S.op("pe", fns, reads=[B_slab[slot]] + [B_hT[b] for b in range(b0, b0 + nbt)],
                             writes=[B_A[a]])
                        S.op("act", I_act(qT[:, ch, t0:t0 + n], bankA[a][:, :n], AF.Identity,
                                          bias=bq8[:, ch:ch + 1], scale=0.125), reads=[B_A[a], B_const],
                             writes=(scr_deps if first else [B_scr]))
                        first = False
                issue_slab()
            slot = next_slab()
            for ti, (b0, nbt) in enumerate(tiles):
                n = nbt * P
                t0 = b0 * P
                for kvh in range(2):
                    a = state["actr"] % 4
                    state["actr"] += 1
                    fns = [I_mm(bankA[a][:, :n], slab_sb[slot][:, kc, kvh * P:(kvh + 1) * P],
                                hT[:, kc, t0:t0 + n], kc == 0, kc == KC - 1) for kc in range(KC)]
                    S.op("pe", fns, reads=[B_slab[slot]] + [B_hT[b] for b in range(b0, b0 + nbt)],
                         writes=[B_A[a]])
                    S.op("act", I_act(kT2[:, kvh, P + t0:P + t0 + n], bankA[a][:, :n], AF.Identity,
                                      bias=bk2[:, kvh:kvh + 1], scale=1.0), reads=[B_A[a], B_const],
                         writes=[B_kv])
            issue_slab()
            slot = next_slab()
            for b in range(nb):
                a = state["actr"] % 4
                state["actr"] += 1
                fns = [I_mm(bankA[a][:, 0:P], hT[:, kc, b * P:(b + 1) * P], slab_sb[slot][:, kc, 0:P],
                            kc == 0, kc == KC - 1) for kc in range(KC)]
                S.op("pe", fns, reads=[B_slab[slot], B_hT[b]], writes=[B_A[a]])
                S.op("dve", I_tt(vtk[:, b + 1, :], bankA[a][:, 0:P], bvt[:], ALU.add), reads=[B_A[a], B_const],
                     writes=[B_kv])
            issue_slab()
            def stage1(b, hp, mask):
                kvh = hp // 4
                a = state["actr"] % 4
                state["actr"] += 1
                sm = att_small[hp % 2]
                bsm = B_small[hp % 2]
                ei = hp % 2
                fns = []
                for j in range(2):
                    pb = j * 64
                    oc_ = bankA[a][:, j * 256:(j + 1) * 256]
                    fns.append(I_mm(oc_, qT[pb:pb + 64, hp, b * P:(b + 1) * P],
                                    kT2[pb:pb + 64, kvh, b * P:(b + 2) * P], True, False))
                    fns.append(I_mm(oc_, ident[:], mask[:], False, True))
                S.op("pe", fns, reads=[B_scr, B_kv, B_const, B_ident], writes=[B_A[a]])
                S.op("dve", lambda h, o=sm[:, 0:2], i=bankA[a][:].rearrange("p (h k) -> p h k", k=256):
                     h.tensor_reduce(out=o, in_=i, axis=AX.X, op=ALU.max), reads=[B_A[a]], writes=[bsm])
                S.op("dve", I_tt(sm[:, 0:2], sm[:, 0:2], sinkt[:, 2 * hp:2 * hp + 2], ALU.max), reads=[bsm, B_const],
                     writes=[bsm])
                S.op("dve", I_ts(sm[:, 2:4], sm[:, 0:2], -1.0, None, ALU.mult), reads=[bsm], writes=[bsm])
                for j in range(2):
                    S.op("act", I_act(e_bf[ei][:, j, :], bankA[a][:, j * 256:(j + 1) * 256], AF.Exp,
                                      bias=sm[:, 2 + j:3 + j], scale=1.0, accum_out=sm[:, 4 + j:5 + j]),
                         reads=[B_A[a], bsm], writes=[B_e[ei], bsm, B_A[a]])
                S.op("dve", I_tt(sm[:, 6:8], sinkt[:, 2 * hp:2 * hp + 2], sm[:, 2:4], ALU.add), reads=[bsm, B_const],
                     writes=[bsm])
                S.op("act", I_act(sm[:, 6:8], sm[:, 6:8], AF.Exp), reads=[bsm], writes=[bsm])
                S.op("dve", I_tt(sm[:, 8:10], sm[:, 4:6], sm[:, 6:8], ALU.add), reads=[bsm], writes=[bsm])
                S.op("dve", I_recip(sm[:, 10:12], sm[:, 8:10]), reads=[bsm], writes=[bsm])

            def stage2(b, hp, k, ob):
                kvh = hp // 4
                ei = hp % 2
                sm = att_small[hp % 2]
                bsm = B_small[hp % 2]
                fns = [I_tr(bankT[:, j * 2 + kb, :], e_bf[ei][:, j, kb * P:(kb + 1) * P], ident[:])
                       for j in range(2) for kb in range(2)]
                S.op("pe", fns, reads=[B_e[ei], B_ident], writes=[B_T])
                S.op("act", I_acopy(eT[ei][:], bankT[:, 0:4, :]), reads=[B_T], writes=[B_eT[ei]])
                if hp % 4 == 0:
                    d = state["dbank"] % 3
                    state["dbank"] += 1
                    ob[hp // 4] = d
                d = ob[hp // 4]
                c0 = (hp % 4) * 128
                fns = []
                for j in range(2):
                    oc = bankD[d][:, c0 + j * 64:c0 + (j + 1) * 64]
                    fns.append(I_mm(oc, eT[ei][:, j * 2, :], vtk[:, b, kvh * 64:(kvh + 1) * 64], True, False))
                    fns.append(I_mm(oc, eT[ei][:, j * 2 + 1, :], vtk[:, b + 1, kvh * 64:(kvh + 1) * 64], False, True))
                S.op("pe", fns, reads=[B_eT[ei], B_kv], writes=[B_D[d]])
                rd = sm[:, 10:12]
                rd_b = bass.AP(rd.tensor, rd.offset, [list(rd.ap[0]), list(rd.ap[1]), [0, 64]])
                S.op("dve", I_tt(hbf[k][:, hp * 128:(hp + 1) * 128].rearrange("p (j d) -> p j d", d=64),
                                 bankD[d][:, c0:c0 + 128].rearrange("p (j d) -> p j d", d=64), rd_b, ALU.mult),
                     reads=[B_D[d], bsm], writes=[B_hbf[k], B_D[d]])

            for b in range(nb):
                k = state["dctr"] % 2
                state["dctr"] += 1
                use_first = is_first_pass and b == cfg.halo_blocks
                mask = m2fb if use_first else m2b
                ob = [None, None]
                stage1(b, 0, mask)
                for hp in range(8):
                    if hp + 1 < 8:
                        stage1(b, hp + 1, mask)
                    stage2(b, hp, k, ob)
                emit_transpose_block(b, k, mixT, B_mixT)
            S.op("dve", I_cp(kT2[:, :, 0:P], kT2[:, :, nb * P:(nb + 1) * P]), reads=[B_kv], writes=[B_kv])
            S.op("dve", I_cp(vtk[:, 0, :], vtk[:, nb, :]), reads=[B_kv], writes=[B_kv])
            ln_slot = next_ln()
            emit_outproj(nb, KC, mixT, lambda b: [B_mixT[b]], 1.0 / ALPHA, ln_slot, bias_tile=bot[:])
            issue_ln()

        subs = []
        for li in range(L):
            subs.append(("ffn", li, 0))
            subs.append(("mix", li, li % 3))
            subs.append(("ffn", li, 1))
        if cfg.sub_limit is not None:
            subs = subs[:cfg.sub_limit]
        npass = len(cfg.pass_blocks)
        wd_seq = []
        for _ in range(npass):
            for (kind, li, x_) in subs:
                if kind == "ffn":
                    for j in range(FC):
                        slab_plan.append(wgu_d[li * 2 + x_, j])
                    ln_plan.append(li * 3 + (0 if x_ == 0 else 2))
                    wd_seq.append((wd_d[li * 2 + x_], FC))
                else:
                    slot_ = li // 3
                    if x_ == 0:
                        for s_ in range(8):
                            slab_plan.append(gwin_d[slot_, s_])
                        ln_plan.append(L * 3 + slot_)
                        wd_seq.append((gwout_d[slot_], KC))
                    elif x_ == 1:
                        for s_ in range(16):
                            slab_plan.append(hwin_d[slot_, s_])
                        wd_seq.append((hwout_d[slot_], KC))
                    else:
                        for s_ in range(6):
                            slab_plan.append(awqkv_d[slot_, s_])
                        wd_seq.append((awo_d[slot_], KC))
                    ln_plan.append(li * 3 + 1)
        wd_ctr = [0]

        def issue_next_wd():
            if wd_ctr[0] < len(wd_seq):
                issue_wd(*wd_seq[wd_ctr[0]])
                wd_ctr[0] += 1

        for _ in range(NSLAB):
            issue_slab()
        issue_ln()
        issue_ln()
        issue_next_wd()

        out_evs = []
        blk0 = 0
        for pi, nb in enumerate(cfg.pass_blocks):
            src = x_d[blk0 * P:(blk0 + nb) * P, :].rearrange("(b p) d -> p b d", p=P)
            S.dma("sp", [(h_tok[:, 0:nb, :], src)], "xload", writes=B_h[:nb])
            for b in range(nb):
                k = state["dctr"] % 2
                state["dctr"] += 1
                S.op("act", I_acopy(hbf[k][:], h_tok[:, b, :]), reads=[B_h[b]], writes=[B_hbf[k]])
                emit_transpose_block(b, k, hT, B_hT)
            for (kind, li, x_) in subs:
                if kind == "ffn":
                    emit_ffn(nb)
                elif x_ == 0:
                    emit_gmlp(li // 3, nb)
                elif x_ == 1:
                    emit_hgrn(li // 3, nb, pi == 0)
                else:
                    emit_attn(li // 3, nb, pi == 0)
                issue_next_wd()
            pairs = []
            for b in range(nb):
                gb = blk0 + b
                if gb < cfg.halo_blocks:
                    continue
                ob_ = gb - cfg.halo_blocks
                pairs.append((out_d[ob_ * P:(ob_ + 1) * P, :], h_tok[:, b, :]))
            if pairs:
                ev = S.dma("sp", pairs, "store", reads=B_h[:nb])
                out_evs.append(ev)
            blk0 += nb
        S.wait_all("sp", out_evs)

        with nc.Block() as block:
            @block.tensor
            def _(h):
                S.replay("pe", h)

            @block.scalar
            def _(h):
                S.replay("act", h)

            @block.vector
            def _(h):
                S.replay("dve", h)

            @block.gpsimd
            def _(h):
                S.replay("pool", h)

            @block.sync
            def _(h):
                S.replay("sp", h)
    return nc


def make_in_maps(cfg, x_cores, inputs, L, flags=None):
    f32 = np.float32
    ncore = len(x_cores)
    if flags is None:
        flags = [1.0] * ncore
    NBM = max(cfg.pass_blocks)
    TM = NBM * P
    NG_, NH_, NA_ = n_gmlp(L), n_hgrn(L), n_attn(L)
    wgu = inputs["ffn_w_gate_up"]
    wd = inputs["ffn_w_down"]
    common = {
        "wgu": np.stack([slab_layout(gate_up_interleave(wgu[li, fi])) for li in range(L) for fi in range(2)]),
        "wd": np.ascontiguousarray(wd[:L].reshape(L * 2, DFF, D)),
        "lng": np.ascontiguousarray(np.concatenate(
            [inputs["ln_gain"][:L].reshape(L * 3, D), inputs["gmlp_ln_gain"][:2].reshape(-1, D)], 0)[:L * 3 + 2]),
        "lnb": np.ascontiguousarray(np.concatenate(
            [inputs["ln_bias"][:L].reshape(L * 3, D), inputs["gmlp_ln_bias"][:2].reshape(-1, D)], 0)[:L * 3 + 2]),
        "ident": np.eye(P, dtype=f32),
        "tril": np.tril(np.ones((P, P), f32)),
        "cm64": np.triu(np.ones((64, 64), f32)),
        "scanm": np.tile((np.arange(TM) % 64 != 0).astype(f32)[None, :], (P, 1)),
        "hlb": np.ascontiguousarray(inputs["hgrn_lb_logits"].reshape(DEPTH, 8, P).transpose(2, 0, 1)),
    }
    ng = max(NG_, 1)
    common["gwin"] = np.stack([slab_layout(inputs["gmlp_w_in"][s]) for s in range(ng)])
    common["gwout"] = np.ascontiguousarray(inputs["gmlp_w_out"][:ng])
    common["gwsp"] = np.ascontiguousarray(inputs["gmlp_w_spatial"][:ng])
    common["gbs"] = np.ascontiguousarray(inputs["gmlp_b_spatial"][:ng].reshape(ng, 8 * P))
    hw = inputs["hgrn_w_in"][0]
    q_, f_, i_, g_ = [hw[:, j * D:(j + 1) * D].reshape(D, 8, P) for j in range(4)]
    hperm = np.concatenate([np.concatenate([q_[:, h], f_[:, h], i_[:, h], g_[:, h]], axis=1) for h in range(8)], axis=1)
    common["hwin"] = slab_layout(hperm)[None]
    common["hwout"] = np.ascontiguousarray(inputs["hgrn_w_out"][:1])
    common["hng"] = np.ascontiguousarray(inputs["hgrn_norm_gain"][:1])
    common["hngc"] = np.ascontiguousarray(inputs["hgrn_norm_gain"][0].reshape(P, 1))
    aw = inputs["attn_w_qkv"][0]
    ab = inputs["attn_b_qkv"][0]
    k0, k1 = aw[:, 1024:1088], aw[:, 1088:1152]
    awp = np.concatenate([aw[:, :1024], k0, k0, k1, k1, aw[:, 1152:1280], np.zeros((D, 128), f32)], axis=1)
    common["awqkv"] = slab_layout(awp)[None]
    common["abq"] = np.ascontiguousarray(ab[:1024].reshape(8, P).T)[None]
    bk0, bk1 = ab[1024:1088], ab[1088:1152]
    common["abk"] = np.ascontiguousarray(np.stack([np.concatenate([bk0, bk0]), np.concatenate([bk1, bk1])], 1))[None]
    common["abv"] = np.ascontiguousarray(ab[1152:1280])[None]
    common["asink"] = np.ascontiguousarray(inputs["attn_sinks"][:1])
    common["awo"] = np.ascontiguousarray(inputs["attn_w_o"][:1])
    common["abo"] = np.ascontiguousarray(inputs["attn_b_o"][:1])
    NEG = -30000.0
    qi = np.arange(P)[:, None]
    kj = np.arange(P)[None, :]
    m_prev = np.where(kj > qi, 0.0, NEG).astype(f32)
    m_cur = np.where(kj <= qi, 0.0, NEG).astype(f32)
    common["m2"] = np.concatenate([m_prev, m_cur], 1)
    maps = []
    for c in range(ncore):
        m = dict(common)
        m["x"] = np.ascontiguousarray(x_cores[c], dtype=f32)
        m["hflag"] = np.full((P, 1), flags[c], f32)
        if flags[c] > 0:
            m["m2f"] = common["m2"]
        else:
            m["m2f"] = np.concatenate([np.full((P, P), NEG, f32), m_cur], 1)
        maps.append(m)
    return maps


PASS_BLOCKS = [6, 6, 6, 6, 6, 4]


def kernel(**inputs):
    inputs = {k: np.asarray(v) for k, v in inputs.items()}
    x = inputs["x"]
    cfg = Cfg(PASS_BLOCKS, HALO // P)
    x_cores, flags = [], []
    for c in range(NCORES):
        b, p = divmod(c, 4)
        xc = np.zeros((cfg.ntok, D), np.float32)
        s = p * CHUNK - HALO
        if s < 0:
            xc[HALO:] = x[b, 0:CHUNK]
            flags.append(0.0)
        else:
            xc[:] = x[b, s:s + cfg.ntok]
            flags.append(1.0)
        x_cores.append(xc)
    nc = build_program(cfg)
    in_maps = make_in_maps(cfg, x_cores, inputs, DEPTH, flags)
    res = run_bass_kernel_spmd(nc, in_maps, core_ids=list(range(NCORES)))
    out = np.empty((BATCH, SEQ, D), np.float32)
    for c in range(NCORES):
        b, p = divmod(c, 4)
        out[b, p * CHUNK:(p + 1) * CHUNK] = res.results[c]["out"]
    return out
```

```python
import contextlib
import numpy as np
import concourse.bass as bass
import concourse.mybir as mybir
from concourse.bass_utils import run_bass_kernel_spmd

F32 = mybir.dt.float32
BF16 = mybir.dt.bfloat16
AF = mybir.ActivationFunctionType
ALU = mybir.AluOpType
AX = mybir.AxisListType

P = 128
D = 1024
KC = 8
DFF = 2816
FC = 22
DEPTH = 4
ALPHA = (2.0 * DEPTH) ** 0.25
LN_EPS = 1e-5
RMS_EPS = 1e-6
SEQ = 16384
BATCH = 2
NCORES = 8
CHUNK = SEQ // 4
HALO = 256
SLABW = 256
TRIM_HALO = True
LN_DELAY = 1


class Buf:
    __slots__ = ("name", "w", "r")

    def __init__(self, name):
        self.name = name
        self.w = None
        self.r = []


class Sched:
    ENGS = ("pe", "act", "dve", "pool", "sp")

    def __init__(self, nc, stack):
        self.nc = nc
        self.stack = stack
        self.streams = {e: [] for e in self.ENGS}
        self.esem = {}
        self.ecnt = {}
        self.eepoch = {e: 0 for e in self.ENGS}
        self.seen = {e: {} for e in self.ENGS}
        self.semobj = {}
        self.nsem = 0
        for e in self.ENGS:
            self._new_epoch(e)
        self.dsem = {}
        self.dcnt = {}

    def _mksem(self, name):
        s = self.stack.enter_context(self.nc.semaphore(name))
        self.nsem += 1
        self.semobj[name] = s
        return name

    def _new_epoch(self, e):
        self.eepoch[e] += 1
        self.esem[e] = self._mksem(f"e_{e}_{self.eepoch[e]}")
        self.ecnt[e] = 0

    def _waits(self, eng, deps):
        best = {}
        for d in deps:
            if d is None:
                continue
            s, v = d
            if v > best.get(s, 0):
                best[s] = v
        out = []
        seen = self.seen[eng]
        for s, v in best.items():
            if seen.get(s, 0) >= v:
                continue
            seen[s] = v
            out.append((s, v))
        return out

    def _deps(self, reads, writes):
        deps = []
        for b in reads:
            deps.append(b.w)
        for b in writes:
            deps.append(b.w)
            deps.extend(b.r)
        return deps

    def op(self, eng, fns, reads=(), writes=()):
        if callable(fns):
            fns = [fns]
        if self.ecnt[eng] > 30000:
            self._new_epoch(eng)
        st = self.streams[eng]
        for s, v in self._waits(eng, self._deps(reads, writes)):
            st.append(("w", s, v))
        for f in fns[:-1]:
            st.append(("i", f, None))
        self.ecnt[eng] += 1
        ev = (self.esem[eng], self.ecnt[eng])
        st.append(("i", fns[-1], ev))
        for b in reads:
            b.r.append(ev)
        for b in writes:
            b.w = ev
            b.r = []
        return ev

    def dma(self, eng, pairs, semkey, reads=(), writes=()):
        if semkey not in self.dsem:
            self.dsem[semkey] = self._mksem(f"d_{semkey}")
            self.dcnt[semkey] = 0
        st = self.streams[eng]
        for s, v in self._waits(eng, self._deps(reads, writes)):
            st.append(("w", s, v))
        s = self.dsem[semkey]
        for (o, i) in pairs:
            self.dcnt[semkey] += 16
            st.append(("d", (o, i), s))
        ev = (s, self.dcnt[semkey])
        for b in reads:
            b.r.append(ev)
        for b in writes:
            b.w = ev
            b.r = []
        return ev

    def wait_all(self, eng, evs):
        st = self.streams[eng]
        for s, v in self._waits(eng, evs):
            st.append(("w", s, v))

    def replay(self, eng, h):
        so = self.semobj
        for kind, a, b in self.streams[eng]:
            if kind == "w":
                h.wait_ge(so[a], b)
            elif kind == "i":
                ins = a(h)
                if b is not None:
                    ins.then_inc(so[b[0]], 1)
            else:
                o, i = a
                h.dma_start(out=o, in_=i).then_inc(so[b], 16)


def I_mm(out, lhsT, rhs, start, stop):
    return lambda h: h.matmul(out, lhsT, rhs, start=start, stop=stop)


def I_tr(out, in_, ident):
    return lambda h: h.transpose(out, in_, ident)


def I_act(out, in_, func, **kw):
    return lambda h: h.activation(out=out, in_=in_, func=func, **kw)


def I_acopy(out, in_):
    return lambda h: h.copy(out, in_)


def I_tt(out, in0, in1, op):
    return lambda h: h.tensor_tensor(out=out, in0=in0, in1=in1, op=op)


def I_ts(out, in0, s1, s2, op0, op1=None):
    if op1 is None:
        return lambda h: h.tensor_scalar(out=out, in0=in0, scalar1=s1, scalar2=None, op0=op0)
    return lambda h: h.tensor_scalar(out=out, in0=in0, scalar1=s1, scalar2=s2, op0=op0, op1=op1)


def I_stt(out, in0, scalar, in1, op0, op1):
    return lambda h: h.scalar_tensor_tensor(out=out, in0=in0, scalar=scalar, in1=in1, op0=op0, op1=op1)


def I_cp(out, in_):
    return lambda h: h.tensor_copy(out, in_)


def I_bnstats(out, in_):
    return lambda h: h.bn_stats(out, in_)


def I_bnaggr(out, in_):
    return lambda h: h.bn_aggr(out, in_)


def I_recip(out, in_):
    return lambda h: h.reciprocal(out, in_)


def I_memset(ap, v):
    return lambda h: h.memset(ap, v)

def slab_layout(w):
    k, n = w.shape
    assert k == D and n % SLABW == 0
    return np.ascontiguousarray(
        w.reshape(KC, P, n // SLABW, SLABW).transpose(2, 1, 0, 3).reshape(n // SLABW, P, KC * SLABW))


def gate_up_interleave(w):
    g = w[:, :DFF].reshape(D, FC, P)
    u = w[:, DFF:].reshape(D, FC, P)
    return np.concatenate([g, u], axis=2).reshape(D, 2 * DFF)


class Cfg:
    def __init__(self, pass_blocks, halo_blocks, n_layers=DEPTH, sub_limit=None):
        self.pass_blocks = list(pass_blocks)
        self.halo_blocks = halo_blocks
        self.n_layers = n_layers
        self.sub_limit = sub_limit
        self.nblk = sum(pass_blocks)
        self.ntok = self.nblk * P
        self.nout = (self.nblk - halo_blocks) * P


def token_tiles(nb):
    nt = (nb + 3) // 4
    base, rem = divmod(nb, nt)
    out, s = [], 0
    for i in range(nt):
        n = base + (1 if i < rem else 0)
        out.append((s, n))
        s += n
    return out


def bcast(ap, axis, n):
    dims = [list(d) for d in ap.ap]
    dims.insert(axis, [0, n])
    return bass.AP(ap.tensor, ap.offset, dims)


def n_gmlp(L):
    return (L + 2) // 3


def n_hgrn(L):
    return (L + 1) // 3


def n_attn(L):
    return L // 3


def build_program(cfg):
    nc = bass.Bass("TRN2", target_bir_lowering=False)
    NBM = max(cfg.pass_blocks)
    TM = NBM * P
    NCHM = TM // 64
    L = cfg.n_layers
    NG_, NH_, NA_ = n_gmlp(L), n_hgrn(L), n_attn(L)

    def din(name, shape, dt=F32):
        return nc.dram_tensor(name, list(shape), dt, kind="ExternalInput").ap()

    x_d = din("x", [cfg.ntok, D])
    wgu_d = din("wgu", [L * 2, FC, P, KC * SLABW])
    wd_d = din("wd", [L * 2, DFF, D])
    lng_d = din("lng", [L * 3 + 2, D])
    lnb_d = din("lnb", [L * 3 + 2, D])
    ident_d = din("ident", [P, P])
    gwin_d = din("gwin", [max(NG_, 1), 8, P, KC * SLABW])
    gwout_d = din("gwout", [max(NG_, 1), D, D])
    gwsp_d = din("gwsp", [max(NG_, 1), 8, P, P])
    gbs_d = din("gbs", [max(NG_, 1), 8 * P])
    tril_d = din("tril", [P, P])
    hwin_d = din("hwin", [max(NH_, 1), 16, P, KC * SLABW])
    hwout_d = din("hwout", [max(NH_, 1), D, D])
    hlb_d = din("hlb", [P, DEPTH, 8])
    hng_d = din("hng", [max(NH_, 1), P])
    hngc_d = din("hngc", [P, 1])
    cm64_d = din("cm64", [64, 64])
    scanm_d = din("scanm", [P, TM])
    hflag_d = din("hflag", [P, 1])
    awqkv_d = din("awqkv", [max(NA_, 1), 6, P, KC * SLABW])
    abq_d = din("abq", [max(NA_, 1), P, 8])
    abk_d = din("abk", [max(NA_, 1), P, 2])
    abv_d = din("abv", [max(NA_, 1), P])
    asink_d = din("asink", [max(NA_, 1), 16])
    awo_d = din("awo", [max(NA_, 1), D, D])
    abo_d = din("abo", [max(NA_, 1), D])
    m2_d = din("m2", [P, 256])
    m2f_d = din("m2f", [P, 256])
    out_d = nc.dram_tensor("out", [cfg.nout, D], F32, kind="ExternalOutput").ap()

    stack = contextlib.ExitStack()
    with stack:
        def sb(name, shape, dt):
            return stack.enter_context(nc.sbuf_tensor(name, list(shape), dt))

        def ps(name, shape, dt):
            return stack.enter_context(nc.psum_tensor(name, list(shape), dt))

        h_tok = sb("h_tok", [P, NBM, D], F32)
        hT = sb("hT", [P, KC, TM], BF16)
        mixT = sb("mixT", [P, KC, TM], BF16)
        act = sb("act", [P, FC, TM], BF16)
        SCR = FC * TM
        wd_sb = sb("wd_sb", [P, FC, D], BF16)
        NSLAB = 5
        slab_sb = [sb(f"slab{i}", [P, KC, SLABW], BF16) for i in range(NSLAB)]
        lng_sb = [sb(f"lng{i}", [P, D], F32) for i in range(2)]
        lnb_sb = [sb(f"lnb{i}", [P, D], F32) for i in range(2)]
        NHBF = 3
        hbf = [sb(f"hbf{i}", [P, D], BF16) for i in range(NHBF)]
        silu_t = [sb(f"silu{i}", [P, 512], F32) for i in range(2)]
        ident_f = sb("ident_f32", [P, P], F32)
        ident = sb("ident_bf", [P, P], BF16)
        stats = [sb(f"stats{i}", [P, 2, 6], F32) for i in range(2)]
        mv = [sb(f"mv{i}", [P, 2], F32) for i in range(2)]
        sd = [sb(f"sd{i}", [P, 1], F32) for i in range(2)]
        rstd = [sb(f"rstd{i}", [P, 1], F32) for i in range(2)]
        epsb = sb("epsb", [P, 2], F32)
        wmT = sb("wmT", [P, max(NG_, 1), 8, P], BF16)
        bs_sb = sb("bs_sb", [P, 8, P], F32)
        S_st = sb("S_st", [P, 8, P], F32)
        S_bf = sb("S_bf", [P, 4, P], BF16)
        scanm = sb("scanm_sb", [P, TM], F32)
        lbt = sb("lbt", [P, 8], F32)
        omlbt = sb("omlbt", [P, 8], F32)
        ngt = sb("ngt", [P, P], F32)
        cm64 = sb("cm64_sb", [64, 64], F32)
        hflag = sb("hflag_sb", [P, 1], F32)
        ss_t = sb("ss_t", [P, 16], F32)
        ngcol = sb("ngcol", [P, 1], F32)
        kT2 = sb("kT2", [P, 2, (NBM + 1) * P], BF16)
        vtk = sb("vtk", [P, NBM + 1, P], BF16)
        m2b = sb("m2b", [P, 256], BF16)
        m2fb = sb("m2fb", [P, 256], BF16)
        bq8 = sb("bq8", [P, 8], F32)
        bk2 = sb("bk2", [P, 2], F32)
        bvt = sb("bvt", [P, P], F32)
        sinkt = sb("sinkt", [P, 16], F32)
        bot = sb("bot", [P, D], F32)
        att_small = [sb(f"atts{i}", [P, 16], F32) for i in range(2)]

        bankA = [ps(f"bankA{i}", [P, 512], F32) for i in range(4)]
        bankD = [ps(f"bankD{i}", [P, 512], F32) for i in range(3)]
        bankT = ps("bankT", [P, KC, P], BF16)

        S = Sched(nc, stack)

        def carve(off, shape, dt):
            flat = act[:].rearrange("p c t -> p (c t)")
            n = 1
            for s_ in shape[1:]:
                n *= s_
            if dt == F32:
                assert off % 4 == 0
                v = flat[:, off // 2: off // 2 + 2 * n].bitcast(F32)
                nbytes = 4 * n
            else:
                v = flat[:, off // 2: off // 2 + n]
                nbytes = 2 * n
            assert off + nbytes <= SCR * 2, (off, nbytes, SCR * 2)
            if len(shape) == 3:
                v = v.rearrange("p (a b) -> p a b", b=shape[2])
            elif len(shape) == 4:
                v = v.rearrange("p (a b c) -> p a b c", b=shape[2], c=shape[3])
            return v

        B_h = [Buf(f"h{b}") for b in range(NBM)]
        B_hT = [Buf(f"hT{b}") for b in range(NBM)]
        B_mixT = [Buf(f"mixT{b}") for b in range(NBM)]
        B_scr = Buf("scratch")
        B_act = {}
        B_wd = Buf("wd")
        B_slab = [Buf(f"slab{i}") for i in range(NSLAB)]
        B_lngb = [Buf(f"lngb{i}") for i in range(2)]
        B_hbf = [Buf(f"hbf{i}") for i in range(NHBF)]
        B_silu = [Buf(f"silu{i}") for i in range(2)]
        B_A = [Buf(f"A{i}") for i in range(4)]
        B_D = [Buf(f"D{i}") for i in range(3)]
        B_T = Buf("T")
        B_ident = Buf("ident")
        B_small = [Buf(f"small{i}") for i in range(2)]
        B_rs = [Buf(f"rs{i}") for i in range(2)]
        B_const = Buf("const")
        B_bs = Buf("bs")
        B_S = Buf("S")
        B_Sbf = [Buf(f"Sbf{i}") for i in range(4)]
        B_kv = Buf("kv")
        B_vb = [Buf("vb0"), Buf("vb1")]
        B_vtok = [Buf(f"vtok{b}") for b in range(NBM)]
        B_uTb = [Buf(f"uTb{b}") for b in range(NBM)]
        B_el, B_ig, B_kd, B_at, B_ss = Buf("el"), Buf("ig"), Buf("kd"), Buf("at"), Buf("ss")
        B_iT, B_gs = Buf("iT"), Buf("gs")
        B_qf = Buf("qf")
        B_gs2 = [Buf("gs0"), Buf("gs1")]
        B_e = [Buf("e0"), Buf("e1")]
        B_eT = [Buf("eT0"), Buf("eT1")]

        def bact(j, t):
            k = (j, t)
            if k not in B_act:
                B_act[k] = Buf(f"act{k}")
            return B_act[k]

        def all_act():
            return list(B_act.values())

        state = {"slab_issue": 0, "slab_use": 0, "dctr": 0, "actr": 0, "ln_issue": 0, "ln_use": 0,
                 "dbank": 0, "sctr": 0, "scr_dirty": True}

        S.dma("sp", [(ident_f[:], ident_d[:, :])], "ident", writes=[B_ident])
        S.op("dve", I_cp(ident[:], ident_f[:]), reads=[B_ident], writes=[B_ident])
        S.op("dve", I_memset(epsb[:, 0:1], float(LN_EPS / ALPHA ** 2)), writes=[B_small[0], B_small[1]])
        S.op("dve", I_memset(epsb[:, 1:2], float(LN_EPS)), writes=[B_small[0], B_small[1]])
        cpairs = [(scanm[:], scanm_d[:, :]), (cm64[:], cm64_d[:, :]), (hflag[:], hflag_d[:, :])]
        tmp_f = carve(0, [P, 8, P], F32)
        tmp_f2 = carve(4096, [P, 8, P], F32)
        if NH_ > 0:
            cpairs.append((ngt[:], hng_d[0:1, :].partition_broadcast(P)))
            cpairs.append((ngcol[:], hngc_d[:, :]))
            cpairs.append((tmp_f[:, 0:DEPTH, 0:8], hlb_d[:, :, :]))
        if NA_ > 0:
            cpairs += [(bq8[:], abq_d[0]), (bk2[:], abk_d[0]),
                       (bvt[:], abv_d[0:1, :].partition_broadcast(P)),
                       (sinkt[:], asink_d[0:1, :].partition_broadcast(P)),
                       (bot[:], abo_d[0:1, :].partition_broadcast(P)),
                       (tmp_f2[:, 0, :], m2_d[:, 0:128]), (tmp_f2[:, 1, :], m2_d[:, 128:256]),
                       (tmp_f2[:, 2, :], m2f_d[:, 0:128]), (tmp_f2[:, 3, :], m2f_d[:, 128:256])]
        S.dma("sp", cpairs, "const", writes=[B_const, B_scr])
        if NH_ > 0:
            hl = 1
            e4 = tmp_f[:, 0:DEPTH, 0:8]
            mx = tmp_f[:, 4, 0:8]
            S.op("dve", I_tt(mx, tmp_f[:, 0, 0:8], tmp_f[:, 1, 0:8], ALU.max), reads=[B_const, B_scr],
                 writes=[B_scr])
            for l in range(2, DEPTH):
                S.op("dve", I_tt(mx, mx, tmp_f[:, l, 0:8], ALU.max), reads=[B_scr], writes=[B_scr])
            for l in range(DEPTH):
                S.op("dve", I_tt(tmp_f[:, l, 0:8], tmp_f[:, l, 0:8], mx, ALU.subtract), reads=[B_scr],
                     writes=[B_scr])
            for l in range(DEPTH):
                S.op("act", I_act(tmp_f[:, l, 0:8], tmp_f[:, l, 0:8], AF.Exp), reads=[B_scr], writes=[B_scr])
            den = tmp_f[:, 5, 0:8]
            S.op("dve", I_tt(den, tmp_f[:, 0, 0:8], tmp_f[:, 1, 0:8], ALU.add), reads=[B_scr], writes=[B_scr])
            for l in range(2, DEPTH):
                S.op("dve", I_tt(den, den, tmp_f[:, l, 0:8], ALU.add), reads=[B_scr], writes=[B_scr])
            S.op("dve", I_recip(den, den), reads=[B_scr], writes=[B_scr])
            num = tmp_f[:, 6, 0:8]
            S.op("dve", I_cp(num, tmp_f[:, 1, 0:8]), reads=[B_scr], writes=[B_scr])
            for l in range(2, hl + 1):
                S.op("dve", I_tt(num, num, tmp_f[:, l, 0:8], ALU.add), reads=[B_scr], writes=[B_scr])
            S.op("dve", I_tt(lbt[:], num, den, ALU.mult), reads=[B_scr], writes=[B_const])
            S.op("dve", I_ts(omlbt[:], lbt[:], -1.0, 1.0, ALU.mult, ALU.add), reads=[B_const], writes=[B_const])
            S.op("dve", I_memset(S_st[:], 0.0), writes=[B_S])
        if NA_ > 0:
            S.op("dve", I_cp(m2b[:], tmp_f2[:, 0:2, :].rearrange("p a b -> p (a b)")), reads=[B_const, B_scr],
                 writes=[B_const])
            S.op("dve", I_cp(m2fb[:], tmp_f2[:, 2:4, :].rearrange("p a b -> p (a b)")), reads=[B_const, B_scr],
                 writes=[B_const])
            S.op("dve", I_ts(bq8[:], bq8[:], 0.125, None, ALU.mult), reads=[B_const], writes=[B_const])
            S.op("dve", I_ts(bot[:], bot[:], float(1.0 / ALPHA), None, ALU.mult), reads=[B_const],
                 writes=[B_const])
            S.op("dve", I_memset(kT2[:], 0.0), writes=[B_kv])
            S.op("dve", I_memset(vtk[:], 0.0), writes=[B_kv])
        for gs in range(NG_):
            wsp_f = carve(8192, [P, 8, P], F32)
            wsp_b = carve(8192 + 4096, [P, 8, P], BF16)
            trl = carve(8192 + 4096 + 2048, [P, P], F32)
            S.dma("sp", [(wsp_f, gwsp_d[gs].rearrange("g t s -> t g s")), (trl, tril_d[:, :])], "const",
                  writes=[B_scr])
            S.op("dve", I_tt(wsp_b, wsp_f, bcast(trl, 1, 8), ALU.mult), reads=[B_scr], writes=[B_scr])
            fns = [I_tr(bankT[:, g, :], wsp_b[:, g, :], ident[:]) for g in range(8)]
            S.op("pe", fns, reads=[B_scr, B_ident], writes=[B_T])
            S.op("act", I_acopy(wmT[:, gs], bankT[:]), reads=[B_T], writes=[B_const])

        slab_plan = []
        ln_plan = []

        def issue_slab():
            n = state["slab_issue"]
            if n >= len(slab_plan):
                return
            slot = n % NSLAB
            S.dma("pool", [(slab_sb[slot][:].rearrange("p k c -> p (k c)"), slab_plan[n])], f"slab{slot}",
                  writes=[B_slab[slot]])
            state["slab_issue"] = n + 1

        def next_slab():
            n = state["slab_use"]
            state["slab_use"] = n + 1
            assert n < state["slab_issue"], "slab used before issued"
            return n % NSLAB

        def issue_ln():
            n = state["ln_issue"]
            if n >= len(ln_plan):
                return
            slot = n % 2
            r = ln_plan[n]
            S.dma("sp", [(lng_sb[slot][:], lng_d[r:r + 1, :].partition_broadcast(P)),
                         (lnb_sb[slot][:], lnb_d[r:r + 1, :].partition_broadcast(P))], f"lngb{slot}",
                  writes=[B_lngb[slot]])
            state["ln_issue"] = n + 1

        def next_ln():
            n = state["ln_use"]
            state["ln_use"] = n + 1
            return n % 2

        def issue_wd(src, nch):
            v = src.rearrange("(j p) n -> p j n", p=P)
            if nch > 8:
                hh = nch // 2
                pairs = [(wd_sb[:, 0:hh, :], v[:, 0:hh, :]), (wd_sb[:, hh:nch, :], v[:, hh:nch, :])]
            else:
                pairs = [(wd_sb[:, 0:nch, :], v)]
            S.dma("pool", pairs, "wd", writes=[B_wd])

        def emit_transpose_block(b, src_slot, dstT, dstB):
            fns = [I_tr(bankT[:, c, :], hbf[src_slot][:, c * P:(c + 1) * P], ident[:]) for c in range(KC)]
            S.op("pe", fns, reads=[B_hbf[src_slot], B_ident], writes=[B_T])
            S.op("act", I_acopy(dstT[:, :, b * P:(b + 1) * P], bankT[:]), reads=[B_T], writes=[dstB[b]])

        def ln_core(src, srcB, eps_col, ln_slot, out_f, out_fB, out_bf, out_bfB):
            k = state["sctr"] % 2
            state["sctr"] += 1
            for hf in range(2):
                S.op("dve", I_bnstats(stats[k][:, hf, :], src[:, hf * 512:(hf + 1) * 512]),
                     reads=[srcB], writes=[B_small[k]])
            S.op("dve", I_bnaggr(mv[k][:], stats[k][:].rearrange("p a s -> p (a s)")),
                 reads=[B_small[k]], writes=[B_small[k]])
            S.op("act", I_act(sd[k][:], mv[k][:, 1:2], AF.Sqrt, bias=epsb[:, eps_col:eps_col + 1], scale=1.0),
                 reads=[B_small[k]], writes=[B_rs[k]])
            S.op("dve", I_stt(src, src, mv[k][:, 0:1], lng_sb[ln_slot][:], ALU.subtract, ALU.mult),
                 reads=[B_small[k], B_lngb[ln_slot], srcB], writes=[srcB])
            S.op("dve", I_recip(rstd[k][:], sd[k][:]), reads=[B_rs[k]], writes=[B_rs[k]])
            S.op("dve", I_stt(out_bf, src, rstd[k][:], lnb_sb[ln_slot][:], ALU.mult, ALU.add),
                 reads=[B_rs[k], B_lngb[ln_slot], srcB], writes=[out_bfB])
            if out_f is not None:
                S.op("dve", I_stt(out_f, src, rstd[k][:], lnb_sb[ln_slot][:], ALU.mult, ALU.add),
                     reads=[B_rs[k], B_lngb[ln_slot], srcB], writes=[out_fB])

        def emit_ln_part1(b, banks, bbufs, coef, bias_tile=None):
            for hf in range(2):
                hs = h_tok[:, b, hf * 512:(hf + 1) * 512]
                S.op("dve", I_stt(hs, banks[hf][:], float(coef), hs, ALU.mult, ALU.add),
                     reads=[bbufs[hf]], writes=[B_h[b]])
            if bias_tile is not None:
                hb = h_tok[:, b, :]
                S.op("dve", I_tt(hb, hb, bias_tile, ALU.add), reads=[B_h[b], B_const], writes=[B_h[b]])

        def emit_ln_part2a(b, ln_slot):
            k = state["dctr"] % NHBF
            state["dctr"] += 1
            hb = h_tok[:, b, :]
            ln_core(hb, B_h[b], 0, ln_slot, hb, B_h[b], hbf[k][:], B_hbf[k])
            return k

        def emit_ln_part2b(b, k):
            emit_transpose_block(b, k, hT, B_hT)

        def emit_outproj(nb, nch, srcT, srcB, coef, ln_slot, bias_tile=None, b_lo=0):
            pending = []
            for b in range(b_lo, nb):
                banks, bbufs = [], []
                for hf in range(2):
                    d = state["dbank"] % 3
                    state["dbank"] += 1
                    fns = [I_mm(bankD[d][:], srcT[:, j, b * P:(b + 1) * P],
                                wd_sb[:, j, hf * 512:(hf + 1) * 512], j == 0, j == nch - 1)
                           for j in range(nch)]
                    S.op("pe", fns, reads=[B_wd] + srcB(b), writes=[B_D[d]])
                    banks.append(bankD[d])
                    bbufs.append(B_D[d])
                emit_ln_part1(b, banks, bbufs, coef, bias_tile)
                pending.append((b, emit_ln_part2a(b, ln_slot)))
                if len(pending) > LN_DELAY:
                    emit_ln_part2b(*pending.pop(0))
            while pending:
                emit_ln_part2b(*pending.pop(0))

        def emit_ffn(nb, b_lo=0):
            tiles = [(b0 + b_lo, n_) for (b0, n_) in token_tiles(nb - b_lo)]
            ln_slot = next_ln()
            for j in range(FC):
                slot = next_slab()
                for ti, (b0, nbt) in enumerate(tiles):
                    n = nbt * P
                    t0 = b0 * P
                    pa = state["actr"] % 2
                    state["actr"] += 1
                    bg, bu = bankA[2 * pa], bankA[2 * pa + 1]
                    fns = []
                    for kc in range(KC):
                        fns.append(I_mm(bg[:, :n], slab_sb[slot][:, kc, 0:P], hT[:, kc, t0:t0 + n],
                                        kc == 0, kc == KC - 1))
                    for kc in range(KC):
                        fns.append(I_mm(bu[:, :n], slab_sb[slot][:, kc, P:2 * P], hT[:, kc, t0:t0 + n],
                                        kc == 0, kc == KC - 1))
                    S.op("pe", fns, reads=[B_slab[slot]] + [B_hT[b] for b in range(b0, b0 + nbt)],
                         writes=[B_A[2 * pa], B_A[2 * pa + 1]])
                    S.op("act", I_act(silu_t[pa][:, :n], bg[:, :n], AF.Silu),
                         reads=[B_A[2 * pa]], writes=[B_silu[pa]])
                    wr = [bact(j, ti), B_A[2 * pa]]
                    if state["scr_dirty"]:
                        wr = wr + [B_scr] + all_act()
                        state["scr_dirty"] = False
                    S.op("dve", I_tt(act[:, j, t0:t0 + n], silu_t[pa][:, :n], bu[:, :n], ALU.mult),
                         reads=[B_silu[pa], B_A[2 * pa + 1]], writes=wr)
                issue_slab()

            def srcB(b):
                ti = [i for i, (b0, nbt) in enumerate(tiles) if b0 <= b < b0 + nbt][0]
                return [bact(j, ti) for j in range(FC)]
            emit_outproj(nb, FC, act, srcB, 0.5 / ALPHA, ln_slot, b_lo=b_lo)
            issue_ln()

        def emit_gmlp(gs, nb, b_lo=0):
            state["scr_dirty"] = True
            T = nb * P
            tiles = [(b0 + b_lo, n_) for (b0, n_) in token_tiles(nb - b_lo)]
            uT = carve(0, [P, KC, T], BF16)
            vtok = carve(2 * KC * TM, [P, NBM, D], BF16)
            vblk = [carve(4 * KC * TM + i * 4096, [P, D], F32) for i in range(2)]
            assert 4 * KC * TM + 8192 <= SCR * 2
            scr_deps = [B_scr] + all_act()
            ln_v = next_ln()
            S.dma("sp", [(bs_sb[:].rearrange("p g t -> p (g t)"), gbs_d[gs:gs + 1, :].partition_broadcast(P))],
                  "bs", writes=[B_bs])
            first = True
            for us in range(4):
                slot = next_slab()
                for ti, (b0, nbt) in enumerate(tiles):
                    n = nbt * P
                    t0 = b0 * P
                    for cc in range(2):
                        ch = us * 2 + cc
                        a = state["actr"] % 4
                        state["actr"] += 1
                        fns = [I_mm(bankA[a][:, :n], slab_sb[slot][:, kc, cc * P:(cc + 1) * P],
                                    hT[:, kc, t0:t0 + n], kc == 0, kc == KC - 1) for kc in range(KC)]
                        S.op("pe", fns, reads=[B_slab[slot]] + [B_hT[b] for b in range(b0, b0 + nbt)],
                             writes=[B_A[a]])
                        S.op("act", I_act(uT[:, ch, t0:t0 + n], bankA[a][:, :n], AF.Gelu), reads=[B_A[a]],
                             writes=(scr_deps if first else [B_scr]))
                        first = False
                issue_slab()
            vslots = [next_slab() for _ in range(4)]

            def mixing(b):
                for hg in range(2):
                    a = state["actr"] % 4
                    state["actr"] += 1
                    fns = []
                    for g4 in range(4):
                        g = hg * 4 + g4
                        fns.append(I_mm(bankA[a][:, g4 * P:(g4 + 1) * P], vtok[:, b, g * P:(g + 1) * P],
                                        wmT[:, gs, g, :], True, True))
                    S.op("pe", fns, reads=[B_vtok[b], B_const], writes=[B_A[a]])
                    tmpm = silu_t[a % 2]
                    S.op("dve", I_tt(tmpm[:], bankA[a][:], bs_sb[:, hg * 4:(hg + 1) * 4, :].rearrange("p g t -> p (g t)"),
                                     ALU.add), reads=[B_A[a], B_bs], writes=[B_silu[a % 2]])
                    uv = uT[:, hg * 4:(hg + 1) * 4, b * P:(b + 1) * P]
                    S.op("dve", I_tt(uv, tmpm[:].rearrange("p (g t) -> p g t", t=P), uv, ALU.mult),
                         reads=[B_silu[a % 2], B_scr], writes=[B_uTb[b]])

            for b in range(b_lo, nb):
                k = state["dctr"] % 2
                state["dctr"] += 1
                for vs in range(4):
                    a = state["actr"] % 4
                    state["actr"] += 1
                    fns = [I_mm(bankA[a][:, 0:SLABW], hT[:, kc, b * P:(b + 1) * P], slab_sb[vslots[vs]][:, kc, :],
                                kc == 0, kc == KC - 1) for kc in range(KC)]
                    S.op("pe", fns, reads=[B_slab[vslots[vs]], B_hT[b]], writes=[B_A[a]])
                    S.op("act", I_act(vblk[k][:, vs * SLABW:(vs + 1) * SLABW], bankA[a][:, 0:SLABW], AF.Gelu),
                         reads=[B_A[a]], writes=[B_vb[k]])
                ln_core(vblk[k], B_vb[k], 1, ln_v, None, None, vtok[:, b, :], B_vtok[b])
                if b >= b_lo + 1:
                    mixing(b - 1)
            mixing(nb - 1)
            for _ in range(4):
                issue_slab()
            issue_ln()
            ln_slot = next_ln()
            emit_outproj(nb, KC, uT, lambda b: [B_uTb[b]], 1.0 / ALPHA, ln_slot, b_lo=b_lo)
            issue_ln()

        def emit_hgrn(hs, nb, is_first_pass):
            state["scr_dirty"] = True
            T = nb * P
            NCH = T // 64
            tiles = token_tiles(nb)
            FB = 4 * TM
            qs = carve(0 * FB, [P, TM], F32)
            fv = carve(1 * FB, [P, TM], F32)
            lf = carve(2 * FB, [P, TM], F32)
            gc = carve(3 * FB, [P, TM], F32)
            eg = carve(4 * FB, [P, TM], F32)
            gsT2 = [carve(5 * FB, [P, TM], F32), carve(6 * FB, [P, TM], F32)]
            o_raw = carve(2 * FB, [P, NCHM, P], F32)
            sq_t = carve(4 * FB, [P, NCHM // 2, P], F32)
            o0 = 7 * FB
            qdT = carve(o0, [P, TM], BF16)
            kdT = carve(o0 + 2 * TM, [P, TM], BF16)
            kdecT = carve(o0 + 4 * TM, [P, TM], BF16)
            iT_bf = kdecT
            o1 = o0 + 6 * TM
            CB = NCHM * P * 2
            kdec64 = carve(o1, [P, NCHM, P], BF16)
            i64 = carve(o1 + CB, [P, NCHM, P], BF16)
            at_bf = carve(o1 + 2 * CB, [P, NCHM, 64], BF16)
            assert o1 + 2 * CB + NCHM * 64 * 2 <= SCR * 2
            ss = ss_t
            hh2 = NCH // 2

            def proj(hd):
                gsT = gsT2[hd % 2]
                slot = next_slab()
                slot2 = next_slab()
                for (sl_, col, kind_) in ((slot, 0, "q"), (slot2, 1, "g"), (slot, 1, "f"), (slot2, 0, "i")):
                    for ti, (b0, nbt) in enumerate(tiles):
                        n = nbt * P
                        t0 = b0 * P
                        a = state["actr"] % 4
                        state["actr"] += 1
                        fns = [I_mm(bankA[a][:, :n], slab_sb[sl_][:, kc, col * P:(col + 1) * P], hT[:, kc, t0:t0 + n],
                                    kc == 0, kc == KC - 1) for kc in range(KC)]
                        S.op("pe", fns, reads=[B_slab[sl_]] + [B_hT[b] for b in range(b0, b0 + nbt)],
                             writes=[B_A[a]])
                        if kind_ == "q":
                            S.op("act", I_act(qs[:, t0:t0 + n], bankA[a][:, :n], AF.Silu), reads=[B_A[a]],
                                 writes=[B_qf])
                        elif kind_ == "g":
                            S.op("act", I_act(gsT[:, t0:t0 + n], bankA[a][:, :n], AF.Silu), reads=[B_A[a]],
                                 writes=[B_gs2[hd % 2]])
                        elif kind_ == "f":
                            S.op("act", I_act(fv[:, t0:t0 + n], bankA[a][:, :n], AF.Sigmoid), reads=[B_A[a]],
                                 writes=[B_qf])
                        else:
                            S.op("act", I_acopy(iT_bf[:, t0:t0 + n], bankA[a][:, :n]), reads=[B_A[a]],
                                 writes=[B_iT])
                issue_slab()
                issue_slab()

            def itrans(hd):
                for half in range(2):
                    c0, c1 = (0, hh2) if half == 0 else (hh2, NCH)
                    fns = [I_tr(bankT[0:64, c - c0, :], iT_bf[:, c * 64:(c + 1) * 64], ident[:]) for c in range(c0, c1)]
                    S.op("pe", fns, reads=[B_iT, B_ident], writes=[B_T])
                    S.op("act", I_acopy(i64[0:64, c0:c1, :], bankT[0:64, 0:c1 - c0, :]), reads=[B_T],
                         writes=[B_ig])

            proj(0)
            itrans(0)
            for hd in range(8):
                gsT = gsT2[hd % 2]
                S.op("dve", I_ts(fv[:, :T], fv[:, :T], omlbt[:, hd:hd + 1], lbt[:, hd:hd + 1], ALU.mult, ALU.add),
                     reads=[B_qf, B_const], writes=[B_qf])
                S.op("act", I_act(lf[:, :T], fv[:, :T], AF.Ln), reads=[B_qf], writes=[B_el])
                S.op("dve", I_ts(fv[:, :T], fv[:, :T], -1.0, 1.0, ALU.mult, ALU.add), reads=[B_qf], writes=[B_qf])
                S.op("dve", lambda h, o=gc[:, :T], d0=scanm[:, :T], d1=lf[:, :T]: h.tensor_tensor_scan(
                    o, d0, d1, 0.0, ALU.mult, ALU.add), reads=[B_el, B_const], writes=[B_el])
                S.op("act", I_act(eg[:, :T], gc[:, :T], AF.Exp), reads=[B_el], writes=[B_el])
                S.op("act", I_act(lf[:, :T], gc[:, :T], AF.Exp, scale=-1.0), reads=[B_el], writes=[B_el])
                S.op("dve", I_tt(qdT[:, :T], qs[:, :T], eg[:, :T], ALU.mult), reads=[B_el, B_qf], writes=[B_el])
                S.op("dve", I_tt(fv[:, :T], fv[:, :T], lf[:, :T], ALU.mult), reads=[B_el, B_qf], writes=[B_qf])
                S.op("dve", I_cp(kdT[:, :T], fv[:, :T]), reads=[B_qf], writes=[B_el])
                egl = eg[:, :T].rearrange("p (c s) -> p c s", s=64)[:, :, 63:64]
                egl_b = bass.AP(egl.tensor, egl.offset, [list(egl.ap[0]), list(egl.ap[1]), [0, 64]])
                S.op("dve", I_tt(kdecT[:, :T].rearrange("p (c s) -> p c s", s=64),
                                 fv[:, :T].rearrange("p (c s) -> p c s", s=64), egl_b, ALU.mult),
                     reads=[B_el, B_qf], writes=[B_iT])
                S.op("dve", I_cp(ss[:, 0:NCH], egl.rearrange("p c o -> p (c o)")), reads=[B_el], writes=[B_ss])
                for half in range(2):
                    c0, c1 = (0, hh2) if half == 0 else (hh2, NCH)
                    fns = [I_tr(bankT[0:64, c - c0, :], kdecT[:, c * 64:(c + 1) * 64], ident[:]) for c in range(c0, c1)]
                    S.op("pe", fns, reads=[B_iT, B_ident], writes=[B_T])
                    S.op("act", I_acopy(kdec64[0:64, c0:c1, :], bankT[0:64, 0:c1 - c0, :]), reads=[B_T],
                         writes=[B_kd])
                for half in range(2):
                    c0, c1 = (0, min(8, NCH)) if half == 0 else (8, NCH)
                    if c1 <= c0:
                        continue
                    a = state["actr"] % 4
                    state["actr"] += 1
                    fns = [I_mm(bankA[a][0:64, (c - c0) * 64:(c - c0 + 1) * 64], kdT[:, c * 64:(c + 1) * 64],
                                qdT[:, c * 64:(c + 1) * 64], True, True) for c in range(c0, c1)]
                    S.op("pe", fns, reads=[B_el], writes=[B_A[a]])
                    S.op("dve", I_tt(at_bf[0:64, c0:c1, :],
                                     bankA[a][0:64, 0:(c1 - c0) * 64].rearrange("p (c t) -> p c t", t=64),
                                     bcast(cm64[:], 1, c1 - c0), ALU.mult),
                         reads=[B_A[a], B_const], writes=[B_at])
                dS = {}
                for c in range(NCH):
                    d = c // 4
                    dS[c] = bankD[d][:, (c % 4) * P:(c % 4 + 1) * P]
                for d in range((NCH + 3) // 4):
                    cs = [c for c in range(NCH) if c // 4 == d]
                    fns = [I_mm(dS[c], kdec64[0:64, c, :], i64[0:64, c, :], True, True) for c in cs]
                    S.op("pe", fns, reads=[B_kd, B_ig], writes=[B_D[d]])
                cur_a = None
                for c in range(NCH):
                    sl = state["sctr"] % 4
                    state["sctr"] += 1
                    if is_first_pass and c == 2 * cfg.halo_blocks and cfg.halo_blocks > 0:
                        S.op("dve", I_ts(S_st[:, hd, :], S_st[:, hd, :], hflag[:, 0:1], None, ALU.mult),
                             reads=[B_S, B_const], writes=[B_S])
                    S.op("dve", I_cp(S_bf[:, sl, :], S_st[:, hd, :]), reads=[B_S], writes=[B_Sbf[sl]])
                    if c % 4 == 0:
                        cur_a = state["actr"] % 4
                        state["actr"] += 1
                    oc = bankA[cur_a][0:64, (c % 4) * P:(c % 4 + 1) * P]
                    fns = [I_mm(oc, at_bf[0:64, c, :], i64[0:64, c, :], True, False),
                           I_mm(oc, qdT[:, c * 64:(c + 1) * 64], S_bf[:, sl, :], False, True)]
                    S.op("pe", fns, reads=[B_at, B_ig, B_el, B_Sbf[sl]], writes=[B_A[cur_a]])
                    S.op("dve", I_stt(S_st[:, hd, :], S_st[:, hd, :], ss[:, c:c + 1], dS[c], ALU.mult, ALU.add),
                         reads=[B_S, B_ss, B_D[c // 4]], writes=[B_S])
                    if c % 4 == 3 or c == NCH - 1:
                        cb = c - (c % 4)
                        S.op("act", I_acopy(o_raw[0:64, cb:c + 1, :],
                                            bankA[cur_a][0:64, 0:(c - cb + 1) * P].rearrange("p (c v) -> p c v", v=P)),
                             reads=[B_A[cur_a]], writes=[B_el])
                if hd + 1 < 8:
                    proj(hd + 1)
                for half in range(2):
                    c0, c1 = (0, hh2) if half == 0 else (hh2, NCH)
                    S.op("dve", I_tt(sq_t[0:64, 0:c1 - c0, :], o_raw[0:64, c0:c1, :], o_raw[0:64, c0:c1, :], ALU.mult),
                         reads=[B_el], writes=[B_el])
                    S.op("dve", lambda h, o=ss[0:64, c0:c1], i=sq_t[0:64, 0:c1 - c0, :]: h.tensor_reduce(
                        out=o, in_=i, axis=AX.X, op=ALU.add), reads=[B_el], writes=[B_ss])
                S.op("dve", I_ts(ss[0:64, 0:NCH], ss[0:64, 0:NCH], 1.0 / P, float(RMS_EPS), ALU.mult, ALU.add),
                     reads=[B_ss], writes=[B_ss])
                S.op("act", I_act(ss[0:64, 0:NCH], ss[0:64, 0:NCH], AF.Ln), reads=[B_ss], writes=[B_ss])
                S.op("act", I_act(ss[0:64, 0:NCH], ss[0:64, 0:NCH], AF.Exp, scale=-0.5), reads=[B_ss], writes=[B_ss])
                ssb = ss[0:64, 0:NCH]
                ss_b = bass.AP(ssb.tensor, ssb.offset, [list(ssb.ap[0]), list(ssb.ap[1]), [0, P]])
                orw = o_raw[0:64, 0:NCH, :]
                S.op("dve", I_tt(orw, orw, ss_b, ALU.mult), reads=[B_el, B_ss], writes=[B_el])
                pa = state["actr"] % 2
                state["actr"] += 1
                ba = [bankA[2 * pa], bankA[2 * pa + 1]]
                fns = [I_tr(ba[c // 8][:, (c % 8) * 64:(c % 8 + 1) * 64], o_raw[0:64, c, :], ident_f[0:64, 0:64])
                       for c in range(NCH)]
                S.op("pe", fns, reads=[B_el, B_ident], writes=[B_A[2 * pa], B_A[2 * pa + 1]])
                n0 = min(T, 512)
                S.op("dve", I_stt(mixT[:, hd, 0:n0], ba[0][:, 0:n0], ngcol[:, 0:1], gsT[:, 0:n0], ALU.mult, ALU.mult),
                     reads=[B_A[2 * pa], B_gs2[hd % 2], B_const], writes=B_mixT[:nb])
                if T > 512:
                    S.op("dve", I_stt(mixT[:, hd, 512:T], ba[1][:, 0:T - 512], ngcol[:, 0:1], gsT[:, 512:T],
                                      ALU.mult, ALU.mult),
                         reads=[B_A[2 * pa + 1], B_gs2[hd % 2], B_const], writes=B_mixT[:nb])
                if hd + 1 < 8:
                    itrans(hd + 1)
            ln_slot = next_ln()
            emit_outproj(nb, KC, mixT, lambda b: [B_mixT[b]], 1.0 / ALPHA, ln_slot)
            issue_ln()

        def emit_attn(as_, nb, is_first_pass):
            state["scr_dirty"] = True
            T = nb * P
            tiles = token_tiles(nb)
            qT = carve(0, [P, KC, TM], BF16)
            o0 = 2 * KC * TM
            e_bf = [carve(o0 + i * 1024, [P, 2, 256], BF16) for i in range(2)]
            eT = [carve(o0 + 2048 + i * 1024, [P, 4, P], BF16) for i in range(2)]
            scr_deps = [B_scr] + all_act()
            first = True
            for qs_ in range(4):
                slot = next_slab()
                for ti, (b0, nbt) in enumerate(tiles):
                    n = nbt * P
                    t0 = b0 * P
                    for cc in range(2):
                        ch = qs_ * 2 + cc
                        a = state["actr"] % 4
                        state["actr"] += 1
                        fns = [I_mm(bankA[a][:, :n], slab_sb[slot][:, kc, cc * P:(cc + 1) * P],
                                    hT[:, kc, t0:t0 + n], kc == 0, kc == KC - 1) for kc in range(KC)]
                        S.op("pe", fns, reads=[B_slab[slot]] + [B_hT[b] for b in range(b0, b0 + nbt)],
                             writes=[B_A[a]])
                        S.op("act", I_act(qT[:, ch, t0:t0 + n], bankA[a][:, :n], AF.Identity,
                                          bias=bq8[:, ch:ch + 1], scale=0.125), reads=[B_A[a], B_const],
                             writes=(scr_deps if first else [B_scr]))
                        first = False
                issue_slab()
            slot = next_slab()
            for ti, (b0, nbt) in enumerate(tiles):
                n = nbt * P
                t0 = b0 * P
                for kvh in range(2):
                    a = state["actr"] % 4
                    state["actr"] += 1
                    fns = [I_mm(bankA[a][:, :n], slab_sb[slot][:, kc, kvh * P:(kvh + 1) * P],
                                hT[:, kc, t0:t0 + n], kc == 0, kc == KC - 1) for kc in range(KC)]
                    S.op("pe", fns, reads=[B_slab[slot]] + [B_hT[b] for b in range(b0, b0 + nbt)],
                         writes=[B_A[a]])
                    S.op("act", I_act(kT2[:, kvh, P + t0:P + t0 + n], bankA[a][:, :n], AF.Identity,
                                      bias=bk2[:, kvh:kvh + 1], scale=1.0), reads=[B_A[a], B_const],
                         writes=[B_kv])
            issue_slab()
            slot = next_slab()
            for b in range(nb):
                a = state["actr"] % 4
                state["actr"] += 1
                fns = [I_mm(bankA[a][:, 0:P], hT[:, kc, b * P:(b + 1) * P], slab_sb[slot][:, kc, 0:P],
                            kc == 0, kc == KC - 1) for kc in range(KC)]
                S.op("pe", fns, reads=[B_slab[slot], B_hT[b]], writes=[B_A[a]])
                S.op("dve", I_tt(vtk[:, b + 1, :], bankA[a][:, 0:P], bvt[:], ALU.add), reads=[B_A[a], B_const],
                     writes=[B_kv])
            issue_slab()
            def stage1(b, hp, mask):
                kvh = hp // 4
                a = state["actr"] % 4
                state["actr"] += 1
                sm = att_small[hp % 2]
                bsm = B_small[hp % 2]
                ei = hp % 2
                fns = []
                for j in range(2):
                    pb = j * 64
                    oc_ = bankA[a][:, j * 256:(j + 1) * 256]
                    fns.append(I_mm(oc_, qT[pb:pb + 64, hp, b * P:(b + 1) * P],
                                    kT2[pb:pb + 64, kvh, b * P:(b + 2) * P], True, False))
                    fns.append(I_mm(oc_, ident[:], mask[:], False, True))
                S.op("pe", fns, reads=[B_scr, B_kv, B_const, B_ident], writes=[B_A[a]])
                S.op("dve", lambda h, o=sm[:, 0:2], i=bankA[a][:].rearrange("p (h k) -> p h k", k=256):
                     h.tensor_reduce(out=o, in_=i, axis=AX.X, op=ALU.max), reads=[B_A[a]], writes=[bsm])
                S.op("dve", I_tt(sm[:, 0:2], sm[:, 0:2], sinkt[:, 2 * hp:2 * hp + 2], ALU.max), reads=[bsm, B_const],
                     writes=[bsm])
                S.op("dve", I_ts(sm[:, 2:4], sm[:, 0:2], -1.0, None, ALU.mult), reads=[bsm], writes=[bsm])
                for j in range(2):
                    S.op("act", I_act(e_bf[ei][:, j, :], bankA[a][:, j * 256:(j + 1) * 256], AF.Exp,
                                      bias=sm[:, 2 + j:3 + j], scale=1.0, accum_out=sm[:, 4 + j:5 + j]),
                         reads=[B_A[a], bsm], writes=[B_e[ei], bsm, B_A[a]])
                S.op("dve", I_tt(sm[:, 6:8], sinkt[:, 2 * hp:2 * hp + 2], sm[:, 2:4], ALU.add), reads=[bsm, B_const],
                     writes=[bsm])
                S.op("act", I_act(sm[:, 6:8], sm[:, 6:8], AF.Exp), reads=[bsm], writes=[bsm])
                S.op("dve", I_tt(sm[:, 8:10], sm[:, 4:6], sm[:, 6:8], ALU.add), reads=[bsm], writes=[bsm])
                S.op("dve", I_recip(sm[:, 10:12], sm[:, 8:10]), reads=[bsm], writes=[bsm])

            def stage2(b, hp, k, ob):
                kvh = hp // 4
                ei = hp % 2
                sm = att_small[hp % 2]
                bsm = B_small[hp % 2]
                fns = [I_tr(bankT[:, j * 2 + kb, :], e_bf[ei][:, j, kb * P:(kb + 1) * P], ident[:])
                       for j in range(2) for kb in range(2)]
                S.op("pe", fns, reads=[B_e[ei], B_ident], writes=[B_T])
                S.op("act", I_acopy(eT[ei][:], bankT[:, 0:4, :]), reads=[B_T], writes=[B_eT[ei]])
                if hp % 4 == 0:
                    d = state["dbank"] % 3
                    state["dbank"] += 1
                    ob[hp // 4] = d
                d = ob[hp // 4]
                c0 = (hp % 4) * 128
                fns = []
                for j in range(2):
                    oc = bankD[d][:, c0 + j * 64:c0 + (j + 1) * 64]
                    fns.append(I_mm(oc, eT[ei][:, j * 2, :], vtk[:, b, kvh * 64:(kvh + 1) * 64], True, False))
                    fns.append(I_mm(oc, eT[ei][:, j * 2 + 1, :], vtk[:, b + 1, kvh * 64:(kvh + 1) * 64], False, True))
                S.op("pe", fns, reads=[B_eT[ei], B_kv], writes=[B_D[d]])
                rd = sm[:, 10:12]
                rd_b = bass.AP(rd.tensor, rd.offset, [list(rd.ap[0]), list(rd.ap[1]), [0, 64]])
                S.op("dve", I_tt(hbf[k][:, hp * 128:(hp + 1) * 128].rearrange("p (j d) -> p j d", d=64),
                                 bankD[d][:, c0:c0 + 128].rearrange("p (j d) -> p j d", d=64), rd_b, ALU.mult),
                     reads=[B_D[d], bsm], writes=[B_hbf[k], B_D[d]])

            for b in range(nb):
                k = state["dctr"] % 2
                state["dctr"] += 1
                use_first = is_first_pass and b == cfg.halo_blocks
                mask = m2fb if use_first else m2b
                ob = [None, None]
                stage1(b, 0, mask)
                for hp in range(8):
                    if hp + 1 < 8:
                        stage1(b, hp + 1, mask)
                    stage2(b, hp, k, ob)
                emit_transpose_block(b, k, mixT, B_mixT)
            S.op("dve", I_cp(kT2[:, :, 0:P], kT2[:, :, nb * P:(nb + 1) * P]), reads=[B_kv], writes=[B_kv])
            S.op("dve", I_cp(vtk[:, 0, :], vtk[:, nb, :]), reads=[B_kv], writes=[B_kv])
            ln_slot = next_ln()
            emit_outproj(nb, KC, mixT, lambda b: [B_mixT[b]], 1.0 / ALPHA, ln_slot, bias_tile=bot[:])
            issue_ln()

        subs = []
        for li in range(L):
            subs.append(("ffn", li, 0))
            subs.append(("mix", li, li % 3))
            subs.append(("ffn", li, 1))
        if cfg.sub_limit is not None:
            subs = subs[:cfg.sub_limit]
        npass = len(cfg.pass_blocks)
        wd_seq = []
        for _ in range(npass):
            for (kind, li, x_) in subs:
                if kind == "ffn":
                    for j in range(FC):
                        slab_plan.append(wgu_d[li * 2 + x_, j])
                    ln_plan.append(li * 3 + (0 if x_ == 0 else 2))
                    wd_seq.append((wd_d[li * 2 + x_], FC))
                else:
                    slot_ = li // 3
                    if x_ == 0:
                        for s_ in range(8):
                            slab_plan.append(gwin_d[slot_, s_])
                        ln_plan.append(L * 3 + slot_)
                        wd_seq.append((gwout_d[slot_], KC))
                    elif x_ == 1:
                        for s_ in range(16):
                            slab_plan.append(hwin_d[slot_, s_])
                        wd_seq.append((hwout_d[slot_], KC))
                    else:
                        for s_ in range(6):
                            slab_plan.append(awqkv_d[slot_, s_])
                        wd_seq.append((awo_d[slot_], KC))
                    ln_plan.append(li * 3 + 1)
        wd_ctr = [0]

        def issue_next_wd():
            if wd_ctr[0] < len(wd_seq):
                issue_wd(*wd_seq[wd_ctr[0]])
                wd_ctr[0] += 1

        for _ in range(NSLAB):
            issue_slab()
        issue_ln()
        issue_ln()
        issue_next_wd()

        out_evs = []
        blk0 = 0
        for pi, nb in enumerate(cfg.pass_blocks):
            src = x_d[blk0 * P:(blk0 + nb) * P, :].rearrange("(b p) d -> p b d", p=P)
            S.dma("sp", [(h_tok[:, 0:nb, :], src)], "xload", writes=B_h[:nb])
            for b in range(nb):
                k = state["dctr"] % 2
                state["dctr"] += 1
                S.op("act", I_acopy(hbf[k][:], h_tok[:, b, :]), reads=[B_h[b]], writes=[B_hbf[k]])
                emit_transpose_block(b, k, hT, B_hT)
            lo = 0
            for (kind, li, x_) in subs:
                if kind == "ffn":
                    emit_ffn(nb, lo)
                elif x_ == 0:
                    emit_gmlp(li // 3, nb, lo)
                elif x_ == 1:
                    emit_hgrn(li // 3, nb, pi == 0)
                else:
                    emit_attn(li // 3, nb, pi == 0)
                    if pi == 0 and TRIM_HALO and li == L - 2:
                        lo = min(cfg.halo_blocks, nb - 1)
                issue_next_wd()
            pairs = []
            for b in range(nb):
                gb = blk0 + b
                if gb < cfg.halo_blocks:
                    continue
                ob_ = gb - cfg.halo_blocks
                pairs.append((out_d[ob_ * P:(ob_ + 1) * P, :], h_tok[:, b, :]))
            if pairs:
                ev = S.dma("sp", pairs, "store", reads=B_h[:nb])
                out_evs.append(ev)
            blk0 += nb
        S.wait_all("sp", out_evs)

        with nc.Block() as block:
            @block.tensor
            def _(h):
                S.replay("pe", h)

            @block.scalar
            def _(h):
                S.replay("act", h)

            @block.vector
            def _(h):
                S.replay("dve", h)

            @block.gpsimd
            def _(h):
                S.replay("pool", h)

            @block.sync
            def _(h):
                S.replay("sp", h)
    return nc


def make_in_maps(cfg, x_cores, inputs, L, flags=None):
    f32 = np.float32
    ncore = len(x_cores)
    if flags is None:
        flags = [1.0] * ncore
    NBM = max(cfg.pass_blocks)
    TM = NBM * P
    NG_, NH_, NA_ = n_gmlp(L), n_hgrn(L), n_attn(L)
    wgu = inputs["ffn_w_gate_up"]
    wd = inputs["ffn_w_down"]
    common = {
        "wgu": np.stack([slab_layout(gate_up_interleave(wgu[li, fi])) for li in range(L) for fi in range(2)]),
        "wd": np.ascontiguousarray(wd[:L].reshape(L * 2, DFF, D)),
        "lng": np.ascontiguousarray(np.concatenate(
            [inputs["ln_gain"][:L].reshape(L * 3, D), inputs["gmlp_ln_gain"][:2].reshape(-1, D)], 0)[:L * 3 + 2]),
        "lnb": np.ascontiguousarray(np.concatenate(
            [inputs["ln_bias"][:L].reshape(L * 3, D), inputs["gmlp_ln_bias"][:2].reshape(-1, D)], 0)[:L * 3 + 2]),
        "ident": np.eye(P, dtype=f32),
        "tril": np.tril(np.ones((P, P), f32)),
        "cm64": np.triu(np.ones((64, 64), f32)),
        "scanm": np.tile((np.arange(TM) % 64 != 0).astype(f32)[None, :], (P, 1)),
        "hlb": np.ascontiguousarray(inputs["hgrn_lb_logits"].reshape(DEPTH, 8, P).transpose(2, 0, 1)),
    }
    ng = max(NG_, 1)
    common["gwin"] = np.stack([slab_layout(inputs["gmlp_w_in"][s]) for s in range(ng)])
    common["gwout"] = np.ascontiguousarray(inputs["gmlp_w_out"][:ng])
    common["gwsp"] = np.ascontiguousarray(inputs["gmlp_w_spatial"][:ng])
    common["gbs"] = np.ascontiguousarray(inputs["gmlp_b_spatial"][:ng].reshape(ng, 8 * P))
    hw = inputs["hgrn_w_in"][0]
    q_, f_, i_, g_ = [hw[:, j * D:(j + 1) * D].reshape(D, 8, P) for j in range(4)]
    hperm = np.concatenate([np.concatenate([q_[:, h], f_[:, h], i_[:, h], g_[:, h]], axis=1) for h in range(8)], axis=1)
    common["hwin"] = slab_layout(hperm)[None]
    common["hwout"] = np.ascontiguousarray(inputs["hgrn_w_out"][:1])
    common["hng"] = np.ascontiguousarray(inputs["hgrn_norm_gain"][:1])
    common["hngc"] = np.ascontiguousarray(inputs["hgrn_norm_gain"][0].reshape(P, 1))
    aw = inputs["attn_w_qkv"][0]
    ab = inputs["attn_b_qkv"][0]
    k0, k1 = aw[:, 1024:1088], aw[:, 1088:1152]
    awp = np.concatenate([aw[:, :1024], k0, k0, k1, k1, aw[:, 1152:1280], np.zeros((D, 128), f32)], axis=1)
    common["awqkv"] = slab_layout(awp)[None]
    common["abq"] = np.ascontiguousarray(ab[:1024].reshape(8, P).T)[None]
    bk0, bk1 = ab[1024:1088], ab[1088:1152]
    common["abk"] = np.ascontiguousarray(np.stack([np.concatenate([bk0, bk0]), np.concatenate([bk1, bk1])], 1))[None]
    common["abv"] = np.ascontiguousarray(ab[1152:1280])[None]
    common["asink"] = np.ascontiguousarray(inputs["attn_sinks"][:1])
    common["awo"] = np.ascontiguousarray(inputs["attn_w_o"][:1])
    common["abo"] = np.ascontiguousarray(inputs["attn_b_o"][:1])
    NEG = -30000.0
    qi = np.arange(P)[:, None]
    kj = np.arange(P)[None, :]
    m_prev = np.where(kj > qi, 0.0, NEG).astype(f32)
    m_cur = np.where(kj <= qi, 0.0, NEG).astype(f32)
    common["m2"] = np.concatenate([m_prev, m_cur], 1)
    maps = []
    for c in range(ncore):
        m = dict(common)
        m["x"] = np.ascontiguousarray(x_cores[c], dtype=f32)
        m["hflag"] = np.full((P, 1), flags[c], f32)
        if flags[c] > 0:
            m["m2f"] = common["m2"]
        else:
            m["m2f"] = np.concatenate([np.full((P, P), NEG, f32), m_cur], 1)
        maps.append(m)
    return maps


PASS_BLOCKS = [6, 6, 6, 6, 6, 4]


def kernel(**inputs):
    inputs = {k: np.asarray(v) for k, v in inputs.items()}
    x = inputs["x"]
    cfg = Cfg(PASS_BLOCKS, HALO // P)
    x_cores, flags = [], []
    for c in range(NCORES):
        b, p = divmod(c, 4)
        xc = np.zeros((cfg.ntok, D), np.float32)
        s = p * CHUNK - HALO
        if s < 0:
            xc[HALO:] = x[b, 0:CHUNK]
            flags.append(0.0)
        else:
            xc[:] = x[b, s:s + cfg.ntok]
            flags.append(1.0)
        x_cores.append(xc)
    nc = build_program(cfg)
    in_maps = make_in_maps(cfg, x_cores, inputs, DEPTH, flags)
    res = run_bass_kernel_spmd(nc, in_maps, core_ids=list(range(NCORES)))
    out = np.empty((BATCH, SEQ, D), np.float32)
    for c in range(NCORES):
        b, p = divmod(c, 4)
        out[b, p * CHUNK:(p + 1) * CHUNK] = res.results[c]["out"]
    return out
```
